# Optimizing a Trainium2 kernel written in Bass

```python
import math
import jax, jax.numpy as jnp
from jax import lax
import numpy as np

D_MODEL = 1024
BATCH = 32
SEQ = 256
DEPTH = 4
DEC_BATCH = 8
DEC_SEQ = 2048
PAST_LEN = 256

GRID_W = 64
N_EVEN = (DEPTH + 1) // 2
N_ODD = DEPTH // 2
N_DIR = 2
N_MOD = 6
D_FF = 4 * D_MODEL
EPS = 1e-6
POS_BASE = 10000.0
D_A = D_MODEL // 2
S5_GROUP = 16
G_A = D_A // S5_GROUP
P_A = 64
D_B = D_MODEL // 2
H_B = 4
DK_B = D_B // H_B
DV_B = D_B // H_B
CHUNK = 64
CONV_K = 4
CONV_LEFT = (CONV_K - 1) // 2
D_RNN = D_MODEL
LRU_BLOCKS = 4
LRU_BS = D_RNN // LRU_BLOCKS
LRU_C = 8.0
IN_EVEN = 2 * D_A + 4 * D_B + 2 * N_DIR * H_B
IN_ODD = 2 * D_RNN

kernel_name = 'hybrid_s5_gdn_rglru_diffusion_step'


def rms_norm(x, g):
    xf = x.astype(jnp.float32)
    y = xf * lax.rsqrt(jnp.mean(xf * xf, axis=-1, keepdims=True) + EPS)
    return y * g.astype(jnp.float32)


def modulate(h, shift, scale):
    return h * (1.0 + scale) + shift


def split_cols(x, sizes):
    out, start = [], 0
    for s in sizes:
        out.append(x[..., start:start + s])
        start += s
    return out


def l2norm(x):
    return x * lax.rsqrt(jnp.sum(x * x, axis=-1, keepdims=True) + EPS)


def grid_sincos(n_tokens):
    f32 = jnp.float32
    rows = n_tokens // GRID_W
    row = jnp.repeat(jnp.arange(rows, dtype=f32), GRID_W)
    col = jnp.tile(jnp.arange(GRID_W, dtype=f32), rows)
    n_freq = D_MODEL // 4
    omega = POS_BASE ** (-jnp.arange(n_freq, dtype=f32) / n_freq)
    ar = row[:, None] * omega
    ac = col[:, None] * omega
    return jnp.concatenate([jnp.sin(ar), jnp.cos(ar), jnp.sin(ac), jnp.cos(ac)], axis=-1)


def dwconv_centred(x, w, b):
    L = x.shape[1]
    xp = jnp.pad(x, ((0, 0), (CONV_LEFT, CONV_K - 1 - CONV_LEFT), (0, 0)))
    out = b.astype(jnp.float32)
    for j in range(CONV_K):
        out = out + xp[:, j:j + L] * w[j].astype(jnp.float32)
    return out


def _real_combine(l, r):
    a_l, b_l = l
    a_r, b_r = r
    return a_l * a_r, a_r * b_l + b_r


def linear_scan(a, b, h0, reverse):
    a_cum, h = lax.associative_scan(_real_combine, (a, b), axis=1, reverse=reverse)
    h = h + a_cum * h0[:, None]
    final = h[:, 0] if reverse else h[:, -1]
    return h, final


def _complex_combine(l, r):
    ar_l, ai_l, br_l, bi_l = l
    ar_r, ai_r, br_r, bi_r = r
    return (ar_l * ar_r - ai_l * ai_r,
            ar_l * ai_r + ai_l * ar_r,
            ar_r * br_l - ai_r * bi_l + br_r,
            ar_r * bi_l + ai_r * br_l + bi_r)


def complex_scan(a_re, a_im, b_re, b_im, h0_re, h0_im, reverse):
    A_re, A_im, h_re, h_im = lax.associative_scan(
        _complex_combine, (a_re, a_im, b_re, b_im), axis=1, reverse=reverse)
    h0r = h0_re[:, None]
    h0i = h0_im[:, None]
    h_re = h_re + A_re * h0r - A_im * h0i
    h_im = h_im + A_re * h0i + A_im * h0r
    if reverse:
        return h_re, h_im, h_re[:, 0], h_im[:, 0]
    return h_re, h_im, h_re[:, -1], h_im[:, -1]


def s5_mixer(u, z, lam_re, lam_im, log_dt, b_re, b_im, c_re, c_im, d_skip, h0_re, h0_im):
    f32 = jnp.float32
    bsz, L, _ = u.shape
    uf = u.astype(f32)
    ug = uf.reshape(bsz, L, G_A, S5_GROUP)
    bu_re = jnp.einsum('blgc,gpc->blgp', ug, b_re.astype(f32))
    bu_im = jnp.einsum('blgc,gpc->blgp', ug, b_im.astype(f32))
    y = uf * d_skip.astype(f32)
    finals_re, finals_im = [], []
    for d in range(N_DIR):
        lr = lam_re[d].astype(f32)
        li = lam_im[d].astype(f32)
        dt = jnp.exp(log_dt[d].astype(f32))[:, None]
        mag = jnp.exp(lr * dt)
        ar = mag * jnp.cos(li * dt)
        ai = mag * jnp.sin(li * dt)
        den = lr * lr + li * li
        fr = ((ar - 1.0) * lr + ai * li) / den
        fi = (ai * lr - (ar - 1.0) * li) / den
        br = fr * bu_re - fi * bu_im
        bi = fr * bu_im + fi * bu_re
        shape = br.shape
        h_re, h_im, f_re, f_im = complex_scan(
            jnp.broadcast_to(ar, shape), jnp.broadcast_to(ai, shape), br, bi,
            h0_re[:, d].astype(f32), h0_im[:, d].astype(f32), reverse=(d == 1))
        y = y + (jnp.einsum('blgp,gcp->blgc', h_re, c_re.astype(f32))
                 - jnp.einsum('blgp,gcp->blgc', h_im, c_im.astype(f32))).reshape(bsz, L, D_A)
        finals_re.append(f_re)
        finals_im.append(f_im)
    out = jax.nn.gelu(y) * jax.nn.sigmoid(z.astype(f32))
    return out, jnp.stack(finals_re, axis=1), jnp.stack(finals_im, axis=1)


def gated_delta_chunked(q, k, v, beta, g, S0):
    bsz, L, H, _ = q.shape
    dv = v.shape[-1]
    n = L // CHUNK

    def to_chunks(t):
        return t.reshape(bsz, n, CHUNK, H, -1).transpose(1, 0, 3, 2, 4)

    qc, kc, vc = to_chunks(q), to_chunks(k), to_chunks(v)
    bc = beta.reshape(bsz, n, CHUNK, H).transpose(1, 0, 3, 2)
    gc = jnp.cumsum(g.reshape(bsz, n, CHUNK, H).transpose(1, 0, 3, 2), axis=-1)
    idx = jnp.arange(CHUNK)
    incl = idx[:, None] >= idx[None, :]
    strict = idx[:, None] > idx[None, :]
    decay = jnp.exp(jnp.where(incl, gc[..., :, None] - gc[..., None, :], -jnp.inf))
    kb = kc * bc[..., None]
    vb = vc * bc[..., None]
    lmat = jnp.where(strict, jnp.einsum('nbhcd,nbhed->nbhce', kb, kc) * decay, 0.0)
    a_mat = lmat + jnp.eye(CHUNK, dtype=lmat.dtype)
    rhs = jnp.concatenate([vb, kb * jnp.exp(gc)[..., None]], axis=-1)
    sol = lax.linalg.triangular_solve(a_mat, rhs, left_side=True, lower=True, unit_diagonal=True)
    u_c, w_c = sol[..., :dv], sol[..., dv:]
    qk = jnp.where(incl, jnp.einsum('nbhcd,nbhed->nbhce', qc, kc) * decay, 0.0)

    def step(S, xs):
        q_i, k_i, u_i, w_i, g_i, qk_i = xs
        v_new = u_i - jnp.einsum('bhck,bhkv->bhcv', w_i, S)
        o_i = (jnp.einsum('bhck,bhkv->bhcv', q_i * jnp.exp(g_i)[..., None], S)
               + jnp.einsum('bhce,bhev->bhcv', qk_i, v_new))
        g_last = g_i[..., -1:]
        S = (S * jnp.exp(g_last)[..., None]
             + jnp.einsum('bhck,bhcv->bhkv', k_i * jnp.exp(g_last - g_i)[..., None], v_new))
        return S, o_i

    S_fin, o = lax.scan(step, S0, (qc, kc, u_c, w_c, gc, qk))
    o = o.transpose(1, 0, 3, 2, 4).reshape(bsz, L, H, dv)
    return o, S_fin


def gdn_mixer(q, k, v, z, a_raw, b_raw, conv_w, conv_b, a_log, dt_bias, o_norm, S0):
    f32 = jnp.float32
    bsz, L, _ = q.shape
    qkv = jax.nn.silu(dwconv_centred(jnp.concatenate([q, k, v], axis=-1).astype(f32), conv_w, conv_b))
    qh, kh, vh = split_cols(qkv, (D_B, D_B, D_B))
    qh = l2norm(qh.reshape(bsz, L, H_B, DK_B)) * (DK_B ** -0.5)
    kh = l2norm(kh.reshape(bsz, L, H_B, DK_B))
    vh = vh.reshape(bsz, L, H_B, DV_B)
    a_raw = a_raw.astype(f32).reshape(bsz, L, N_DIR, H_B)
    b_raw = b_raw.astype(f32).reshape(bsz, L, N_DIR, H_B)
    o = 0.0
    finals = []
    for d in range(N_DIR):
        beta = jax.nn.sigmoid(b_raw[:, :, d])
        g = -jnp.exp(a_log[d].astype(f32)) * jax.nn.softplus(a_raw[:, :, d] + dt_bias[d].astype(f32))
        s0 = S0[:, d].astype(f32)
        if d == 0:
            o_d, s_d = gated_delta_chunked(qh, kh, vh, beta, g, s0)
        else:
            rev = lambda t: jnp.flip(t, axis=1)
            o_d, s_d = gated_delta_chunked(rev(qh), rev(kh), rev(vh), rev(beta), rev(g), s0)
            o_d = rev(o_d)
        o = o + o_d
        finals.append(s_d)
    o = rms_norm(o, o_norm) * jax.nn.silu(z.astype(f32).reshape(bsz, L, H_B, DV_B))
    return o.reshape(bsz, L, D_B), jnp.stack(finals, axis=1)


def rglru_mixer(xb, yb, conv_w, conv_b, w_r, b_r, w_i, b_i, lam, h0):
    f32 = jnp.float32
    bsz, L, _ = xb.shape
    x = dwconv_centred(xb.astype(f32), conv_w, conv_b)
    xblk = x.reshape(bsz, L, LRU_BLOCKS, LRU_BS)
    h_sum = 0.0
    finals = []
    for d in range(N_DIR):
        r = jax.nn.sigmoid(jnp.einsum('blni,nij->blnj', xblk, w_r[d].astype(f32)).reshape(bsz, L, D_RNN)
                           + b_r[d].astype(f32))
        i = jax.nn.sigmoid(jnp.einsum('blni,nij->blnj', xblk, w_i[d].astype(f32)).reshape(bsz, L, D_RNN)
                           + b_i[d].astype(f32))
        log_a = -LRU_C * r * jax.nn.softplus(-lam[d].astype(f32))
        a = jnp.exp(log_a)
        b = jnp.sqrt(-jnp.expm1(2.0 * log_a)) * (i * x)
        h, h_fin = linear_scan(a, b, h0[:, d].astype(f32), reverse=(d == 1))
        h_sum = h_sum + h
        finals.append(h_fin)
    return h_sum * jax.nn.gelu(yb.astype(f32)), jnp.stack(finals, axis=1)


def trunk(x, mod, s5_re0, s5_im0, delta0, lru0, p):
    dtype = x.dtype
    fin_re, fin_im, fin_delta, fin_lru = [], [], [], []
    for l in range(DEPTH):
        sh_m, sc_m, gt_m, sh_f, sc_f, gt_f = jnp.split(mod[l][:, None, :], N_MOD, axis=-1)
        h = modulate(rms_norm(x, p['norm_mix_pre'][l]), sh_m, sc_m)
        if l % 2 == 0:
            e = l // 2
            u_a, z_a, q, k, v, z_b, a_raw, b_raw = split_cols(
                h @ p['w_in_even'][e], (D_A, D_A, D_B, D_B, D_B, D_B, N_DIR * H_B, N_DIR * H_B))
            y_a, f_re, f_im = s5_mixer(u_a, z_a, p['s5_lam_re'][e], p['s5_lam_im'][e], p['s5_log_dt'][e],
                                       p['s5_b_re'][e], p['s5_b_im'][e], p['s5_c_re'][e], p['s5_c_im'][e],
                                       p['s5_d'][e], s5_re0[:, e], s5_im0[:, e])
            y_b, f_d = gdn_mixer(q, k, v, z_b, a_raw, b_raw, p['gdn_conv_w'][e], p['gdn_conv_b'][e],
                                 p['gdn_a_log'][e], p['gdn_dt_bias'][e], p['gdn_o_norm'][e], delta0[:, e])
            out = jnp.concatenate([y_a, y_b], axis=-1) @ p['w_out_even'][e]
            fin_re.append(f_re)
            fin_im.append(f_im)
            fin_delta.append(f_d)
        else:
            o = l // 2
            x_b, y_g = split_cols(h @ p['w_in_odd'][o], (D_RNN, D_RNN))
            y_c, f_l = rglru_mixer(x_b, y_g, p['lru_conv_w'][o], p['lru_conv_b'][o], p['lru_w_r'][o],
                                   p['lru_b_r'][o], p['lru_w_i'][o], p['lru_b_i'][o], p['lru_lam'][o],
                                   lru0[:, o])
            out = y_c @ p['w_out_odd'][o]
            fin_lru.append(f_l)
        x = (x + gt_m * rms_norm(out, p['norm_mix_post'][l])).astype(dtype)
        h = modulate(rms_norm(x, p['norm_mlp_pre'][l]), sh_f, sc_f)
        f = jnp.square(jax.nn.relu(h @ p['w_mlp_in'][l])) @ p['w_mlp_out'][l]
        x = (x + gt_f * rms_norm(f, p['norm_mlp_post'][l])).astype(dtype)
    return (x, jnp.stack(fin_re, axis=1), jnp.stack(fin_im, axis=1),
            jnp.stack(fin_delta, axis=1), jnp.stack(fin_lru, axis=1))


def setup_inputs(seed: int = 0) -> dict:
    key = jax.random.key(seed)
    ks = iter(jax.random.split(key, 64))
    f32 = jnp.float32

    def normal(shape, scale):
        return jax.random.normal(next(ks), shape, f32) * scale

    def gain(shape):
        return 1.0 + normal(shape, 0.02)

    def inv_softplus_dt(shape):
        dt = jnp.exp(jax.random.uniform(next(ks), shape, f32, math.log(1e-3), math.log(1e-1)))
        return dt + jnp.log(-jnp.expm1(-dt))

    x_prompt = normal((BATCH, SEQ, D_MODEL), 1.0)
    x_sample = normal((DEC_BATCH, DEC_SEQ, D_MODEL), 1.0)
    state_s5_re = normal((DEC_BATCH, N_EVEN, N_DIR, G_A, P_A), 0.1)
    state_s5_im = normal((DEC_BATCH, N_EVEN, N_DIR, G_A, P_A), 0.1)
    state_delta = normal((DEC_BATCH, N_EVEN, N_DIR, H_B, DK_B, DV_B), 0.05)
    state_lru = normal((DEC_BATCH, N_ODD, N_DIR, D_RNN), 0.5)
    c = normal((DEC_BATCH, D_MODEL), 1.0)
    c_ctx = normal((D_MODEL,), 1.0)
    w_ada = normal((DEPTH, D_MODEL, N_MOD * D_MODEL), 0.5 * D_MODEL ** -0.5)
    b_ada = normal((DEPTH, N_MOD * D_MODEL), 0.02)
    norm_mix_pre = gain((DEPTH, D_MODEL))
    norm_mix_post = gain((DEPTH, D_MODEL))
    norm_mlp_pre = gain((DEPTH, D_MODEL))
    norm_mlp_post = gain((DEPTH, D_MODEL))
    w_mlp_in = normal((DEPTH, D_MODEL, D_FF), D_MODEL ** -0.5)
    w_mlp_out = normal((DEPTH, D_FF, D_MODEL), D_FF ** -0.5)
    w_in_even = normal((N_EVEN, D_MODEL, IN_EVEN), D_MODEL ** -0.5)
    w_out_even = normal((N_EVEN, D_A + D_B, D_MODEL), (D_A + D_B) ** -0.5)
    s5_lam_re = -0.5 + normal((N_EVEN, N_DIR, G_A, P_A), 0.01)
    s5_lam_im = jnp.pi * jnp.arange(P_A, dtype=f32) + normal((N_EVEN, N_DIR, G_A, P_A), 0.01)
    s5_log_dt = jax.random.uniform(next(ks), (N_EVEN, N_DIR, G_A), f32, math.log(1e-3), math.log(1e-1))
    s5_b_re = normal((N_EVEN, G_A, P_A, S5_GROUP), (2.0 * S5_GROUP) ** -0.5)
    s5_b_im = normal((N_EVEN, G_A, P_A, S5_GROUP), (2.0 * S5_GROUP) ** -0.5)
    s5_c_re = normal((N_EVEN, G_A, S5_GROUP, P_A), (2.0 * P_A) ** -0.5)
    s5_c_im = normal((N_EVEN, G_A, S5_GROUP, P_A), (2.0 * P_A) ** -0.5)
    s5_d = normal((N_EVEN, D_A), 1.0)
    gdn_conv_w = normal((N_EVEN, CONV_K, 3 * D_B), CONV_K ** -0.5)
    gdn_conv_b = normal((N_EVEN, 3 * D_B), 0.02)
    gdn_a_log = jnp.log(jax.random.uniform(next(ks), (N_EVEN, N_DIR, H_B), f32, 1.0, 16.0))
    gdn_dt_bias = inv_softplus_dt((N_EVEN, N_DIR, H_B))
    gdn_o_norm = gain((N_EVEN, DV_B))
    w_in_odd = normal((N_ODD, D_MODEL, IN_ODD), D_MODEL ** -0.5)
    w_out_odd = normal((N_ODD, D_RNN, D_MODEL), D_RNN ** -0.5)
    lru_conv_w = normal((N_ODD, CONV_K, D_RNN), CONV_K ** -0.5)
    lru_conv_b = normal((N_ODD, D_RNN), 0.02)
    lru_w_r = normal((N_ODD, N_DIR, LRU_BLOCKS, LRU_BS, LRU_BS), LRU_BS ** -0.5)
    lru_b_r = normal((N_ODD, N_DIR, D_RNN), 0.02)
    lru_w_i = normal((N_ODD, N_DIR, LRU_BLOCKS, LRU_BS, LRU_BS), LRU_BS ** -0.5)
    lru_b_i = normal((N_ODD, N_DIR, D_RNN), 0.02)
    a0 = jax.random.uniform(next(ks), (N_ODD, N_DIR, D_RNN), f32, 0.9, 0.999)
    s = a0 ** (1.0 / LRU_C)
    lru_lam = jnp.log(s) - jnp.log1p(-s)
    return {
        'x_prompt': x_prompt, 'x_sample': x_sample,
        'state_s5_re': state_s5_re, 'state_s5_im': state_s5_im,
        'state_delta': state_delta, 'state_lru': state_lru,
        'c': c, 'c_ctx': c_ctx,
        'w_ada': w_ada, 'b_ada': b_ada,
        'norm_mix_pre': norm_mix_pre, 'norm_mix_post': norm_mix_post,
        'norm_mlp_pre': norm_mlp_pre, 'norm_mlp_post': norm_mlp_post,
        'w_mlp_in': w_mlp_in, 'w_mlp_out': w_mlp_out,
        'w_in_even': w_in_even, 'w_out_even': w_out_even,
        's5_lam_re': s5_lam_re, 's5_lam_im': s5_lam_im, 's5_log_dt': s5_log_dt,
        's5_b_re': s5_b_re, 's5_b_im': s5_b_im, 's5_c_re': s5_c_re, 's5_c_im': s5_c_im, 's5_d': s5_d,
        'gdn_conv_w': gdn_conv_w, 'gdn_conv_b': gdn_conv_b, 'gdn_a_log': gdn_a_log,
        'gdn_dt_bias': gdn_dt_bias, 'gdn_o_norm': gdn_o_norm,
        'w_in_odd': w_in_odd, 'w_out_odd': w_out_odd,
        'lru_conv_w': lru_conv_w, 'lru_conv_b': lru_conv_b,
        'lru_w_r': lru_w_r, 'lru_b_r': lru_b_r, 'lru_w_i': lru_w_i, 'lru_b_i': lru_b_i,
        'lru_lam': lru_lam,
    }


def reference(x_prompt, x_sample, state_s5_re, state_s5_im, state_delta, state_lru, c, c_ctx,
              w_ada, b_ada, norm_mix_pre, norm_mix_post, norm_mlp_pre, norm_mlp_post,
              w_mlp_in, w_mlp_out, w_in_even, w_out_even,
              s5_lam_re, s5_lam_im, s5_log_dt, s5_b_re, s5_b_im, s5_c_re, s5_c_im, s5_d,
              gdn_conv_w, gdn_conv_b, gdn_a_log, gdn_dt_bias, gdn_o_norm,
              w_in_odd, w_out_odd, lru_conv_w, lru_conv_b, lru_w_r, lru_b_r, lru_w_i, lru_b_i,
              lru_lam):
    f32 = jnp.float32
    p = dict(norm_mix_pre=norm_mix_pre, norm_mix_post=norm_mix_post,
             norm_mlp_pre=norm_mlp_pre, norm_mlp_post=norm_mlp_post,
             w_mlp_in=w_mlp_in, w_mlp_out=w_mlp_out, w_in_even=w_in_even, w_out_even=w_out_even,
             s5_lam_re=s5_lam_re, s5_lam_im=s5_lam_im, s5_log_dt=s5_log_dt,
             s5_b_re=s5_b_re, s5_b_im=s5_b_im, s5_c_re=s5_c_re, s5_c_im=s5_c_im, s5_d=s5_d,
             gdn_conv_w=gdn_conv_w, gdn_conv_b=gdn_conv_b, gdn_a_log=gdn_a_log,
             gdn_dt_bias=gdn_dt_bias, gdn_o_norm=gdn_o_norm,
             w_in_odd=w_in_odd, w_out_odd=w_out_odd, lru_conv_w=lru_conv_w, lru_conv_b=lru_conv_b,
             lru_w_r=lru_w_r, lru_b_r=lru_b_r, lru_w_i=lru_w_i, lru_b_i=lru_b_i, lru_lam=lru_lam)
    mod_ctx = (jnp.einsum('d,lde->le', jax.nn.silu(c_ctx.astype(f32)), w_ada) + b_ada.astype(f32))[:, None, :]
    mod_lat = jnp.einsum('bd,lde->lbe', jax.nn.silu(c.astype(f32)), w_ada) + b_ada.astype(f32)[:, None, :]

    bp = x_prompt.shape[0]
    zero_re = jnp.zeros((bp, N_EVEN, N_DIR, G_A, P_A), f32)
    zero_delta = jnp.zeros((bp, N_EVEN, N_DIR, H_B, DK_B, DV_B), f32)
    zero_lru = jnp.zeros((bp, N_ODD, N_DIR, D_RNN), f32)
    y_prompt, new_s5_re, new_s5_im, new_delta, new_lru = trunk(
        x_prompt, mod_ctx, zero_re, zero_re, zero_delta, zero_lru, p)

    x_lat = (x_sample.astype(f32) + grid_sincos(x_sample.shape[1])[None]).astype(x_sample.dtype)
    y_sample = trunk(x_lat, mod_lat, state_s5_re, state_s5_im, state_delta, state_lru, p)[0]
    return (y_prompt, y_sample, new_s5_re, new_s5_im, new_delta, new_lru)
```

```python
import numpy as np
from contextlib import ExitStack
import concourse.bass as bass
import concourse.mybir as mybir
from concourse.bass import AP
from concourse.bass_utils import run_bass_kernel_spmd

F32 = mybir.dt.float32
BF16 = mybir.dt.bfloat16
ALU = mybir.AluOpType
AF = mybir.ActivationFunctionType

ENGS = ("pe", "act", "dve", "pool", "sp")
NDMASEM = 12


class Prog:
    def __init__(self, nc, same_engine_sync=("act", "dve", "pool")):
        self.nc = nc
        self.es = ExitStack()
        self.q = {e: [] for e in ENGS}
        self.count = {e: 0 for e in ENGS}
        self.seen = {e: {} for e in ENGS}
        self.clock_at = {}
        self.last_w = {}
        self.readers = {}
        self.ses = set(same_engine_sync)
        self.sem = {}
        for e in ENGS:
            if e != "sp":
                self.sem[e] = self.es.enter_context(nc.semaphore("s_" + e))
        self.dsem = {}
        self.dcnt = {}
        self.drr = {}
        for qn in ("sp", "act"):
            self.dsem[qn] = [self.es.enter_context(nc.semaphore("d_%s%d" % (qn, i))) for i in range(NDMASEM)]
            self.dcnt[qn] = [0] * NDMASEM
            self.drr[qn] = 0
        self.nwaits = 0
        self.nops = 0

    def sb(self, name, shape, dt=F32):
        return self.es.enter_context(self.nc.sbuf_tensor("sb_" + name, list(shape), dt))

    def ps(self, name, shape, dt=F32):
        return self.es.enter_context(self.nc.psum_tensor("ps_" + name, list(shape), dt))

    def _deps(self, reads, writes):
        deps = set()
        for k in reads:
            t = self.last_w.get(k)
            if t is not None:
                deps.add(t)
        for k in writes:
            t = self.last_w.get(k)
            if t is not None:
                deps.add(t)
            for r in self.readers.get(k, ()):
                deps.add(r)
        return deps

    def _mkwaits(self, eng, deps):
        need = {}
        for (e, s) in deps:
            if need.get(e, 0) < s:
                need[e] = s
        waits = []
        seen = self.seen[eng]
        for e, s in need.items():
            if e == eng and eng not in self.ses:
                continue
            if seen.get(e, 0) >= s:
                continue
            waits.append((e, s))
        for e, s in waits:
            if seen.get(e, 0) < s:
                seen[e] = s
            ck = self.clock_at.get((e, s))
            if ck:
                for f, v in ck.items():
                    if seen.get(f, 0) < v:
                        seen[f] = v
        self.nwaits += len(waits)
        return waits

    def _record(self, tok, reads, writes):
        for k in reads:
            self.readers.setdefault(k, []).append(tok)
        for k in writes:
            self.last_w[k] = tok
            self.readers[k] = []

    def op(self, eng, fn, reads=(), writes=()):
        deps = self._deps(reads, writes)
        waits = self._mkwaits(eng, deps)
        self.count[eng] += 1
        tok = (eng, self.count[eng])
        self.q[eng].append((fn, waits, ("c", eng)))
        self.clock_at[tok] = dict(self.seen[eng])
        self._record(tok, reads, writes)
        self.nops += 1
        return tok

    def dma(self, qn, fn, reads=(), writes=()):
        deps = self._deps(reads, writes)
        i = self.drr[qn]
        self.drr[qn] = (i + 1) % NDMASEM
        semname = ("d", qn, i)
        prev = self.dcnt[qn][i]
        if prev:
            deps.add((semname, prev))
        waits = self._mkwaits(qn, deps)
        self.dcnt[qn][i] = prev + 16
        tok = (semname, prev + 16)
        self.q[qn].append((fn, waits, ("d", qn, i)))
        self.clock_at[tok] = dict(self.seen[qn])
        self._record(tok, reads, writes)
        self.nops += 1
        return tok

    def barrier(self):
        toks = set()
        for e in ENGS:
            if e != "sp" and self.count[e]:
                toks.add((e, self.count[e]))
        for qn in self.dsem:
            for i in range(NDMASEM):
                if self.dcnt[qn][i]:
                    toks.add((("d", qn, i), self.dcnt[qn][i]))
        for e in ENGS:
            w = self._mkwaits(e, toks)
            if w:
                self.q[e].append((None, w, None))
        self.last_w = {}
        self.readers = {}

    def _semof(self, e):
        if isinstance(e, tuple):
            return self.dsem[e[1]][e[2]]
        return self.sem[e]

    def emit(self, final_tokens):
        nc = self.nc
        fw = self._mkwaits("sp", set(final_tokens))
        with nc.Block() as block:
            def run(engname):
                def body(eng):
                    for fn, waits, kind in self.q[engname]:
                        if fn is None:
                            for (e, s) in waits:
                                eng.wait_ge(self._semof(e), s)
                            continue
                        for (e, s) in waits[1:]:
                            eng.wait_ge(self._semof(e), s)
                        ins = fn(eng)
                        if waits:
                            ins._wait_ge(self._semof(waits[0][0]), waits[0][1])
                        if kind[0] == "c":
                            ins.then_inc(self.sem[kind[1]], 1)
                        else:
                            ins.then_inc(self.dsem[kind[1]][kind[2]], 16)
                    if engname == "sp":
                        for (e, s) in fw:
                            eng.wait_ge(self._semof(e), s)
                return body
            block.sync(run("sp"))
            block.tensor(run("pe"))
            block.scalar(run("act"))
            block.vector(run("dve"))
            block.gpsimd(run("pool"))
        self.es.close()


D = 1024
KT = 8
DFF = 4096
GA, PA = 32, 64
HB = 4
CH = 64
EPS = 1e-6
LRU_C = 8.0
TC = 128
PASSES = {"S": dict(nseq=1, L=2048), "P": dict(nseq=4, L=256)}

VEC = {}
_nv = [0]


def _vreg(name, n):
    VEC[name] = (_nv[0], n)
    _nv[0] += n


for _l in range(4):
    for _n in ("n_mix_pre", "n_mix_post", "n_mlp_pre", "n_mlp_post"):
        _vreg("%s%d" % (_n, _l), 8)
    _vreg("b_ada%d" % _l, 48)
for _e in range(2):
    _vreg("s5_lam_re%d" % _e, 32)
    _vreg("s5_lam_im%d" % _e, 32)
    _vreg("s5_dt%d" % _e, 32)
    _vreg("s5_d%d" % _e, 4)
    _vreg("gdn_conv_w%d" % _e, 48)
    _vreg("gdn_conv_b%d" % _e, 12)
    _vreg("gdn_alog%d" % _e, 8)
    _vreg("gdn_dtb%d" % _e, 8)
    _vreg("gdn_onorm%d" % _e, 1)
for _o in range(2):
    _vreg("lru_conv_w%d" % _o, 32)
    _vreg("lru_conv_b%d" % _o, 8)
    _vreg("lru_b_r%d" % _o, 16)
    _vreg("lru_b_i%d" % _o, 16)
    _vreg("lru_lam%d" % _o, 16)
NV = _nv[0]


def vcol(name, i=0, n=1):
    o, _ = VEC[name]
    return slice(o + i, o + i + n)


class Builder:
    def __init__(self, layers=(0, 1, 2, 3), passes=("S", "P"), debug=False):
        self.layers = layers
        self.passes = passes
        self.debug = debug
        nc = self.nc = bass.Bass("TRN2", target_bir_lowering=False)
        self.P = Prog(nc)
        self.outs = []
        self._decl_dram()
        self._alloc()
        self._prologue()
        for pn in passes:
            self.run_pass(pn)
        self.P.emit(self.outs)

    def _decl_dram(self):
        nc = self.nc

        def din(name, shape, dt=F32):
            return nc.dram_tensor(name, list(shape), dt, kind="ExternalInput").ap()

        def dout(name, shape):
            return nc.dram_tensor(name, list(shape), F32, kind="ExternalOutput").ap()
        self.d = d = {}
        d["xs"] = din("xs", [2048, D])
        d["xp"] = din("xp", [1024, D])
        d["pos"] = din("pos", [2048, D])
        d["cst"] = din("cst", [128, 128 * 9])
        d["bmk"] = din("bmk", [128, 14 * 128])
        d["cvec"] = din("cvec", [128, 8, 2])
        d["vecs"] = din("vecs", [128, NV])
        d["w_ada"] = din("w_ada", [4, 48, 128, 1024])
        d["win_e"] = din("win_e", [2, 25, 128, 1024])
        d["wout_e"] = din("wout_e", [2, 8, 128, 1024])
        d["win_o"] = din("win_o", [2, 16, 128, 1024])
        d["wout_o"] = din("wout_o", [2, 8, 128, 1024])
        d["wm1"] = din("wm1", [4, 32, 128, 1024])
        d["wm2"] = din("wm2", [4, 8, 4, 128, 1024])
        d["wlru"] = din("wlru", [2, 2, 4, 128, 1024])
        d["s5b"] = din("s5b", [2, 128, 4, 4, 2, 128])
        d["s5c"] = din("s5c", [2, 128, 16, 2, 128])
        d["s5h0"] = din("s5h0", [128, 2, 2, 2, 16])
        d["dl0"] = din("dl0", [128, 2, 2, 4, 128])
        d["lru0"] = din("lru0", [128, 2, 2, 8])
        d["ys"] = dout("ys", [2048, D])
        d["yp"] = dout("yp", [1024, D])
        d["o_s5"] = dout("o_s5", [4, 2, 2, 2, 128, 16])
        d["o_dl"] = dout("o_dl", [4, 2, 2, 4, 128, 128])
        d["o_lru"] = dout("o_lru", [4, 2, 2, 128, 8])
        if self.debug:
            d["dbg"] = dout("dbg", [128, 8, 2048])

    def _alloc(self):
        P = self.P
        self.x = P.sb("x", [128, KT, 2048], F32)
        self.h = P.sb("h", [128, KT, 2048], BF16)
        self.yb = P.sb("yb", [128, KT, 2048], BF16)
        self.wst = P.sb("wst", [128, 2, 1024], F32)
        self.wbf = P.sb("wbf", [128, 2, 1024], BF16)
        self.cst = P.sb("cst", [128, 128 * 9], F32)
        self.cbf = P.sb("cbf", [128, 256], BF16)
        self.vecs = P.sb("vecs", [128, NV], F32)
        self.mod = P.sb("mod", [128, 4, 48, 2], F32)
        self.mvec = P.sb("mvec", [128, 4, 2, 4, 8], F32)
        self.cv = P.sb("cv", [128, 8, 2], F32)
        self.ARENA = 13312
        self.arena = P.sb("arena", [128, self.ARENA], F32)
        self.pb = [P.ps("pb%d" % i, [128, 512], F32) for i in range(8)]
        self.slab_i = 0
        self.pb_rr = 0
        self.finl = P.sb("finl", [128, 4, 2, 8], F32)
        self.epsc = P.sb("epsc", [128, 1], F32)
        self.onec = P.sb("onec", [128, 1], F32)
        self.lcst = P.sb("lcst", [128, 16], F32)
        self.lh0 = P.sb("lh0", [128, 2, 8], F32)
        self.fin5 = P.sb("fin5", [128, 4, 2, 2, 16], F32)
        self.s5p = P.sb("s5p", [128, 16, 32], F32)
        self.s5h = P.sb("s5h", [128, 2, 2, 16], F32)
        self.ident = self.cst[:, 0:128]
        self.ones = self.cst[:, 128:256]
        self.m_gt = self.cst[:, 256:384]
        self.m_ge = self.cst[:, 384:512]
        self.m_lt = self.cst[:, 512:640]
        self.m_le = self.cst[:, 640:768]
        self.m_ngt = self.cst[:, 768:896]
        self.jidx = self.cst[:, 896:1024]
        self.m_nlt = self.cst[:, 1024:1152]
        self.ones_bf = self.cbf[:, 0:128]
        self.ident_bf = self.cbf[:, 128:256]

    def carve(self, off, n, dt=F32, parts=128):
        if dt == F32:
            return self.arena[0:parts, off:off + n]
        v = self.arena[0:parts, off:off + (n + 1) // 2].bitcast(BF16)
        return v[:, 0:n]

    def _bk(self, w, *aps):
        extra = []
        for a in aps:
            nm = getattr(a, "name", None)
            if isinstance(nm, str) and nm.startswith("ps_pb"):
                k = ("BANK", nm)
                if k not in extra:
                    extra.append(k)
        return list(w) + extra if extra else w

    def act(self, out, in_, func, r, w, bias=None, scale=None):
        w = self._bk(w, out, in_)
        kw = {}
        if bias is not None:
            kw["bias"] = bias
        if scale is not None:
            kw["scale"] = scale
        return self.P.op("act", lambda e: e.activation(out=out, in_=in_, func=func, **kw), reads=r, writes=w)

    def tt(self, eng, out, a, b, op, r, w):
        w = self._bk(w, out, a, b)
        return self.P.op(eng, lambda e: e.tensor_tensor(out=out, in0=a, in1=b, op=op), reads=r, writes=w)

    def ts(self, eng, out, a, s1, op0, r, w, s2=None, op1=None):
        w = self._bk(w, out, a)
        if op1 is None:
            return self.P.op(eng, lambda e: e.tensor_scalar(out=out, in0=a, scalar1=s1, scalar2=None, op0=op0), reads=r, writes=w)
        return self.P.op(eng, lambda e: e.tensor_scalar(out=out, in0=a, scalar1=s1, scalar2=s2, op0=op0, op1=op1), reads=r, writes=w)

    def stt(self, out, a, s, b, op0, op1, r, w):
        w = self._bk(w, out, a, b)
        return self.P.op("dve", lambda e: e.scalar_tensor_tensor(out=out, in0=a, scalar=s, in1=b, op0=op0, op1=op1), reads=r, writes=w)

    def cp(self, eng, out, in_, r, w):
        w = self._bk(w, out, in_)
        if eng == "act":
            return self.P.op("act", lambda e: e.copy(out=out, in_=in_), reads=r, writes=w)
        return self.P.op(eng, lambda e: e.tensor_copy(out=out, in_=in_), reads=r, writes=w)

    def mm(self, out, lhsT, rhs, start, stop, r, w):
        w = self._bk(w, out)
        return self.P.op("pe", lambda e: e.matmul(out, lhsT=lhsT, rhs=rhs, start=start, stop=stop), reads=r, writes=w)

    def tr(self, out, in_, ident, r, w):
        w = self._bk(w, out)
        return self.P.op("pe", lambda e: e.transpose(out, in_, ident), reads=r, writes=w)

    def memset(self, eng, ap, val, w):
        return self.P.op(eng, lambda e: e.memset(ap, val), writes=w)

    def load(self, out, in_, w, r=()):
        return self.P.dma("sp", lambda e: e.dma_start(out=out, in_=in_), reads=r, writes=w)

    def store(self, out, in_, r):
        t = self.P.dma("sp", lambda e: e.dma_start(out=out, in_=in_), reads=r)
        self.outs.append(t)
        return t

    def slab(self, dram_ap, cast=True):
        deep = getattr(self, "deep_ring", False)
        st_bufs = [self.wst[:, 0, :], self.wst[:, 1, :]]
        bf_bufs = [self.wbf[:, 0, :], self.wbf[:, 1, :]]
        if deep:
            st_bufs += [self.carve(4608, 1024), self.carve(5632, 1024), self.carve(11264, 1024), self.carve(12288, 1024)]
            bf_bufs += [self.carve(6656, 1024, BF16)]
        self.slab_n = getattr(self, "slab_n", 0) + 1
        i = self.slab_n % len(st_bufs)
        j = self.slab_n % len(bf_bufs)
        self.load(st_bufs[i], dram_ap, w=[("wst", i)])
        if not cast:
            return st_bufs[i], ("wst", i)
        self.cp(getattr(self, "cast_eng", "act"), bf_bufs[j], st_bufs[i], r=[("wst", i)], w=[("wbf", j)])
        return bf_bufs[j], ("wbf", j)

    def _prologue(self):
        P = self.P
        d = self.d
        self.load(self.cst[:], d["cst"], w=["cst"])
        self.load(self.vecs[:], d["vecs"], w=["vecs"])
        self.load(self.cv[:], d["cvec"], w=["cv"])
        self.memset("pool", self.epsc[:], EPS, w=["epsc"])
        self.memset("pool", self.onec[:], 1.0, w=["onec"])
        self.cp("dve", self.cbf[:, 0:128], self.cst[:, 128:256], r=["cst"], w=["cbf"])
        self.cp("dve", self.cbf[:, 128:256], self.cst[:, 0:128], r=["cst", "cbf"], w=["cbf"])
        self.act(self.cv[:], self.cv[:], AF.Silu, r=["cv"], w=["cv"])
        for l in self.layers:
            pm = self.pb[l % 2]
            for ft in range(48):
                wv, wk = self.slab(d["w_ada"][l, ft], cast=False)
                w3 = wv.rearrange("p (k c) -> p k c", k=8)
                for kt in range(8):
                    self.mm(pm[:, 2 * ft:2 * ft + 2], w3[:, kt, :], self.cv[:, kt, :], kt == 0, kt == 7,
                            r=[wk, "cv"], w=[("pm", l % 2)])
            for j in range(2):
                self.tt("dve", self.mod[:, l, :, j], pm[:, 0:96].rearrange("p (f j) -> p f j", j=2)[:, :, j],
                        self.vecs[:, vcol("b_ada%d" % l, 0, 48)], ALU.add, r=[("pm", l % 2), "vecs"], w=[("mod", l, j)])
                for q, (npre, npost, o_sc, o_gt) in enumerate((("n_mix_pre", "n_mix_post", 8, 16), ("n_mlp_pre", "n_mlp_post", 32, 40))):
                    self.stt(self.mvec[:, l, j, 2 * q, :], self.mod[:, l, o_sc:o_sc + 8, j], 1.0,
                             self.vecs[:, vcol("%s%d" % (npre, l), 0, 8)], ALU.add, ALU.mult,
                             r=[("mod", l, j), "vecs"], w=[("mvec", l, j, 2 * q)])
                    self.tt("dve", self.mvec[:, l, j, 2 * q + 1, :], self.mod[:, l, o_gt:o_gt + 8, j],
                            self.vecs[:, vcol("%s%d" % (npost, l), 0, 8)], ALU.mult,
                            r=[("mod", l, j), "vecs"], w=[("mvec", l, j, 2 * q + 1)])
        P.barrier()

    def run_pass(self, pn):
        cfg = PASSES[pn]
        self.pn = pn
        self.nseq, self.L = cfg["nseq"], cfg["L"]
        self.T = self.nseq * self.L
        self.NB = self.T // 512
        self.j = 1 if pn == "S" else 0
        self.load_x()
        for l in self.layers:
            self.modnorm_all(l)
            if l % 2 == 0:
                self.even_mixer(l)
            else:
                self.odd_mixer(l)
            self.P.barrier()
            self.cast_eng = "dve"
            self.deep_ring = True
            self.out_proj(l)
            self.P.barrier()
            self.mlp(l)
            self.cast_eng = "act"
            self.deep_ring = False
            self.P.barrier()
        if self.debug and pn == self.passes[-1]:
            self.store(self.d["dbg"][:, :, 0:self.T], self.x[:, :, 0:self.T], r=self.xkeys())
        self.store_x()
        self.P.barrier()

    def xkeys(self):
        return [("x", b) for b in range(self.NB)]

    def load_x(self):
        T = self.T
        src = self.d["xs"] if self.pn == "S" else self.d["xp"]
        xt = [self.carve(i * 1024, 1024) for i in range(2)]
        pt = [self.carve(2048 + i * 1024, 1024) for i in range(2)]
        for tt_ in range(T // 128):
            i = tt_ % 2
            self.load(xt[i], src[tt_ * 128:(tt_ + 1) * 128, :], w=[("xt", i)])
            if self.pn == "S":
                self.load(pt[i], self.d["pos"][tt_ * 128:(tt_ + 1) * 128, :], w=[("pt", i)])
                self.tt("dve", xt[i], xt[i], pt[i], ALU.add, r=[("xt", i), ("pt", i)], w=[("xt", i)])
            for hf in range(2):
                pbk = self.pb[(2 * tt_ + hf) % 4]
                key = ("pb", (2 * tt_ + hf) % 4)
                for q in range(4):
                    kt = hf * 4 + q
                    self.tr(pbk[:, q * 128:(q + 1) * 128], xt[i][:, kt * 128:(kt + 1) * 128], self.ident,
                            r=[("xt", i), "cst"], w=[key])
                self.cp("act" if hf == 0 else "dve", self.x[:, hf * 4:hf * 4 + 4, tt_ * 128:(tt_ + 1) * 128],
                        pbk[:, :].rearrange("p (q t) -> p q t", q=4), r=[key], w=[("x", tt_ // 4)])
        self.P.barrier()

    def store_x(self):
        T = self.T
        dst = self.d["ys"] if self.pn == "S" else self.d["yp"]
        ot = [self.carve(i * 1024, 1024) for i in range(2)]
        for tt_ in range(T // 128):
            i = tt_ % 2
            for hf in range(2):
                pbk = self.pb[(2 * tt_ + hf) % 4]
                key = ("pb", (2 * tt_ + hf) % 4)
                for q in range(4):
                    kt = hf * 4 + q
                    self.tr(pbk[:, q * 128:(q + 1) * 128], self.x[:, kt, tt_ * 128:(tt_ + 1) * 128], self.ident,
                            r=[("x", tt_ // 4), "cst"], w=[key])
                self.cp("act" if hf == 0 else "dve", ot[i][:, hf * 512:(hf + 1) * 512], pbk[:, :], r=[key], w=[("ot", i)])
            self.store(dst[tt_ * 128:(tt_ + 1) * 128, :], ot[i], r=[("ot", i)])

    def rstd_block(self, src3, srckeys, dstkey, scale, off):
        sq = self.carve(off, 4096, BF16).rearrange("p (k t) -> p k t", k=8)
        rs = self.carve(off + 2048, 512)
        self.act(sq, src3, AF.Square, r=srckeys, w=[("sq", off)])
        pbk, key = self.pb[7], ("pb", 7)
        for kt in range(8):
            self.mm(pbk[:, :], self.ones_bf, sq[:, kt, :], kt == 0, kt == 7, r=[("sq", off), "cbf"], w=[key])
        self.act(rs, pbk[:, :], AF.Sqrt, r=[key], w=[dstkey], bias=self.epsc[:, 0:1], scale=scale)
        self.P.op("dve", lambda e: e.reciprocal(out=rs, in_=rs), reads=[dstkey], writes=[dstkey])
        return rs

    def modnorm_block(self, l, which, b, hdst, hkey, off):
        xs = self.x[:, :, b * 512:(b + 1) * 512]
        rs = self.rstd_block(xs, [("x", b)], ("rs", off), 1.0 / D, off)
        tmp = self.carve(off + 2560, 512 * 2).rearrange("p (i t) -> p i t", i=2)
        sh_off = 0 if which == 0 else 24
        for kt in range(8):
            tk = ("mtmp", off, kt % 2)
            self.stt(tmp[:, kt % 2, :], self.x[:, kt, b * 512:(b + 1) * 512], self.mvec[:, l, self.j, 2 * which, kt:kt + 1], rs,
                     ALU.mult, ALU.mult, r=[("x", b), ("rs", off), ("mvec", l, self.j, 2 * which)], w=[tk])
            self.act(hdst(kt), tmp[:, kt % 2, :], AF.Identity, r=[tk, ("mod", l, self.j)], w=[hkey],
                     bias=self.mod[:, l, sh_off + kt, self.j:self.j + 1])

    def modnorm_all(self, l):
        for b in range(self.NB):
            self.modnorm_block(l, 0, b, lambda kt, b=b: self.h[:, kt, b * 512:(b + 1) * 512], ("h", b), off=(b % 2) * 3584)
        self.P.barrier()

    def proj(self, slab_dram, b, pbi, hkeys=None, hsrc=None):
        raise NotImplementedError

    def proj_full(self, slab_dram, consume, ncols=128):
        wv, wk = self.slab(slab_dram)
        w3 = wv.rearrange("p (k c) -> p k c", k=8)
        for b in range(self.NB):
            pbi = self.pb_rr
            self.pb_rr = (self.pb_rr + 1) % 4
            pbk, key = self.pb[pbi], ("pb", pbi)
            for kt in range(8):
                self.mm(pbk[0:ncols, :], w3[:, kt, 0:ncols], self.h[:, kt, b * 512:(b + 1) * 512], kt == 0, kt == 7,
                        r=[wk, ("h", b)], w=[key])
            consume(b, pbk[0:ncols, :], key)

    def out_proj(self, l):
        e = l // 2
        wd = self.d["wout_e"][e] if l % 2 == 0 else self.d["wout_o"][e]
        self.resid_update(l, 0, lambda ft: wd[ft], self.yb, 8)

    def resid_update(self, l, which, slab_of, src, nk, per_block=None):
        ob = self.carve(7168, 4096).rearrange("p (k t) -> p k t", k=8)
        for b in range(self.NB):
            for ft in range(8):
                wv, wk = self.slab(slab_of(ft))
                w3 = wv.rearrange("p (k c) -> p k c", k=8)
                pbi = ft % 4
                pbk, key = self.pb[pbi], ("pb", pbi)
                for kt in range(8):
                    self.mm(pbk[:, :], w3[:, kt, :], src[:, kt, b * 512:(b + 1) * 512], kt == 0, kt == 7,
                            r=[wk, ("yb", b)], w=[key])
                self.cp("act", ob[:, ft, :], pbk[:, :], r=[key], w=[("ob", ft)])
            self.post_block(l, which, b, ob, [("ob", ft) for ft in range(8)])

    def post_block(self, l, which, b, ob, obkeys):
        rs = self.rstd_block(ob, obkeys, ("rs", 0), 1.0 / D, 0)
        for kt in range(8):
            tk = ("ptmp", kt % 2)
            tmp = self.carve(2560 + (kt % 2) * 512, 512)
            self.stt(tmp, ob[:, kt, :], self.mvec[:, l, self.j, 2 * which + 1, kt:kt + 1], rs, ALU.mult, ALU.mult,
                     r=[("ob", kt), ("rs", 0), ("mvec", l, self.j, 2 * which + 1)], w=[tk])
            self.tt("pool", self.x[:, kt, b * 512:(b + 1) * 512], self.x[:, kt, b * 512:(b + 1) * 512], tmp, ALU.add,
                    r=[tk, ("x", b)], w=[("x", b)])

    def mlp(self, l):
        f1 = self.yb
        f1v = self.yb[:, :, :].rearrange("p k (a t) -> p (k a) t", t=512)
        hb = self.h[:, :, 0:512]
        ob = self.carve(7168, 4096).rearrange("p (k t) -> p k t", k=8)
        rl = [self.carve(3584 + i * 512, 512) for i in range(2)]
        for b in range(self.NB):
            self.modnorm_block(l, 1, b, lambda kt: self.h[:, kt, 0:512], ("h", 0), off=0)
            for jt in range(32):
                wv, wk = self.slab(self.d["wm1"][l, jt])
                w3 = wv.rearrange("p (k c) -> p k c", k=8)
                pbi = jt % 4
                pbk, key = self.pb[pbi], ("pb", pbi)
                for kt in range(8):
                    self.mm(pbk[:, :], w3[:, kt, :], self.h[:, kt, 0:512], kt == 0, kt == 7, r=[wk, ("h", 0)], w=[key])
                rk = ("rl", jt % 2)
                self.act(rl[jt % 2], pbk[:, :], AF.Relu, r=[key], w=[rk])
                self.tt("pool" if jt % 2 else "dve", f1v[:, jt, :], rl[jt % 2], rl[jt % 2], ALU.mult, r=[rk], w=[("f1", jt)])
            for ft in range(8):
                pbi = 4 + ft % 2
                pbk, key = self.pb[pbi], ("pb", pbi)
                for jg in range(4):
                    wv, wk = self.slab(self.d["wm2"][l, ft, jg])
                    w3 = wv.rearrange("p (k c) -> p k c", k=8)
                    for jj in range(8):
                        jt = jg * 8 + jj
                        self.mm(pbk[:, :], w3[:, jj, :], f1v[:, jt, :], jt == 0, jt == 31, r=[wk, ("f1", jt)], w=[key])
                self.cp("act", ob[:, ft, :], pbk[:, :], r=[key], w=[("ob", ft)])
            self.post_block(l, 1, b, ob, [("ob", ft) for ft in range(8)])

    def even_mixer(self, l):
        self.s5_mixer(l)
        self.P.barrier()
        self.gdn_mixer(l)

    def sincos(self, arg, sin_out, cos_out, k, r, key):
        TWO_PI = 6.283185307179586
        C1 = 6.28125
        C2 = TWO_PI - C1
        MAGIC = 12582912.0
        PI = 3.141592653589793
        kk, rk = (key, "k"), (key, "r")
        self.ts("dve", k, arg, 1.0 / TWO_PI, ALU.mult, r=[(key, "arg")], w=[kk], s2=MAGIC, op1=ALU.add)
        self.ts("dve", k, k, MAGIC, ALU.subtract, r=[kk], w=[kk])
        self.stt(r, k, -C1, arg, ALU.mult, ALU.add, r=[kk, (key, "arg")], w=[rk])
        self.stt(r, k, -C2, r, ALU.mult, ALU.add, r=[kk, rk], w=[rk])

        def wrap(y):
            self.ts("dve", k, y, PI, ALU.is_gt, r=[rk], w=[kk], s2=-TWO_PI, op1=ALU.mult)
            self.tt("dve", y, y, k, ALU.add, r=[rk, kk], w=[rk])
            self.ts("dve", k, y, -PI, ALU.is_lt, r=[rk], w=[kk], s2=TWO_PI, op1=ALU.mult)
            self.tt("dve", y, y, k, ALU.add, r=[rk, kk], w=[rk])
        wrap(r)
        self.act(sin_out, r, AF.Sin, r=[rk], w=[(key, "sin")])
        self.ts("dve", r, r, PI / 2, ALU.add, r=[rk], w=[rk])
        wrap(r)
        self.act(cos_out, r, AF.Sin, r=[rk], w=[(key, "cos")])

    def s5_params(self, e):
        V, R = self.vecs, self.s5p
        lr = V[:, vcol("s5_lam_re%d" % e, 0, 32)]
        li = V[:, vcol("s5_lam_im%d" % e, 0, 32)]
        K = "s5p"
        rw = dict(r=[K, "vecs"], w=[K])
        self.act(R[:, 7, :], V[:, vcol("s5_dt%d" % e, 0, 32)], AF.Exp, **rw)
        self.tt("dve", R[:, 6, :], lr, R[:, 7, :], ALU.mult, **rw)
        self.tt("dve", R[:, 0, :], li, R[:, 7, :], ALU.mult, **rw)
        self.act(R[:, 1, :], R[:, 6, :], AF.Exp, **rw)
        self.P.op("dve", lambda en: en.tensor_copy(out=R[:, 15, :], in_=R[:, 0, :]), reads=[K], writes=[("sc1", "arg")])
        self.sincos(R[:, 15, :], R[:, 10, :], R[:, 9, :], R[:, 7, :], R[:, 8, :], "sc1")
        self.ts("dve", R[:, 15, :], R[:, 0, :], float(TC), ALU.mult, r=[K, ("sc1", "sin"), ("sc1", "cos")], w=[("sc2", "arg")])
        self.sincos(R[:, 15, :], R[:, 14, :], R[:, 13, :], R[:, 7, :], R[:, 8, :], "sc2")
        rw = dict(r=[K, "vecs", ("sc1", "sin"), ("sc1", "cos"), ("sc2", "sin"), ("sc2", "cos")], w=[K])
        self.tt("dve", R[:, 2, :], R[:, 1, :], R[:, 9, :], ALU.mult, **rw)
        self.tt("dve", R[:, 3, :], R[:, 1, :], R[:, 10, :], ALU.mult, **rw)
        self.tt("dve", R[:, 6, :], lr, lr, ALU.mult, **rw)
        self.tt("dve", R[:, 7, :], li, li, ALU.mult, **rw)
        self.tt("dve", R[:, 6, :], R[:, 6, :], R[:, 7, :], ALU.add, **rw)
        self.P.op("dve", lambda en: en.reciprocal(out=R[:, 6, :], in_=R[:, 6, :]), reads=[K], writes=[K])
        self.ts("dve", R[:, 7, :], R[:, 2, :], -1.0, ALU.add, **rw)
        self.tt("dve", R[:, 4, :], R[:, 7, :], lr, ALU.mult, **rw)
        self.tt("dve", R[:, 8, :], R[:, 3, :], li, ALU.mult, **rw)
        self.tt("dve", R[:, 4, :], R[:, 4, :], R[:, 8, :], ALU.add, **rw)
        self.tt("dve", R[:, 4, :], R[:, 4, :], R[:, 6, :], ALU.mult, **rw)
        self.tt("dve", R[:, 5, :], R[:, 3, :], lr, ALU.mult, **rw)
        self.tt("dve", R[:, 8, :], R[:, 7, :], li, ALU.mult, **rw)
        self.tt("dve", R[:, 5, :], R[:, 5, :], R[:, 8, :], ALU.subtract, **rw)
        self.tt("dve", R[:, 5, :], R[:, 5, :], R[:, 6, :], ALU.mult, **rw)
        self.tt("dve", R[:, 11, :], R[:, 4, :], R[:, 9, :], ALU.mult, **rw)
        self.tt("dve", R[:, 8, :], R[:, 5, :], R[:, 10, :], ALU.mult, **rw)
        self.tt("dve", R[:, 11, :], R[:, 11, :], R[:, 8, :], ALU.add, **rw)
        self.tt("dve", R[:, 12, :], R[:, 5, :], R[:, 9, :], ALU.mult, **rw)
        self.tt("dve", R[:, 8, :], R[:, 4, :], R[:, 10, :], ALU.mult, **rw)
        self.tt("dve", R[:, 12, :], R[:, 12, :], R[:, 8, :], ALU.subtract, **rw)
        if self.pn == "S":
            H = self.s5h
            self.load(H[:], self.d["s5h0"][:, e], w=["s5h"])
            self.tt("dve", R[:, 6, :], R[:, 4, :], R[:, 4, :], ALU.mult, **rw)
            self.tt("dve", R[:, 7, :], R[:, 5, :], R[:, 5, :], ALU.mult, **rw)
            self.tt("dve", R[:, 6, :], R[:, 6, :], R[:, 7, :], ALU.add, **rw)
            self.P.op("dve", lambda en: en.reciprocal(out=R[:, 6, :], in_=R[:, 6, :]), reads=[K], writes=[K])
            self.tt("dve", R[:, 7, :], R[:, 9, :], R[:, 4, :], ALU.mult, **rw)
            self.tt("dve", R[:, 15, :], R[:, 10, :], R[:, 5, :], ALU.mult, **rw)
            self.tt("dve", R[:, 7, :], R[:, 7, :], R[:, 15, :], ALU.add, **rw)
            self.tt("dve", R[:, 7, :], R[:, 7, :], R[:, 6, :], ALU.mult, **rw)
            self.tt("dve", R[:, 8, :], R[:, 10, :], R[:, 4, :], ALU.mult, **rw)
            self.tt("dve", R[:, 15, :], R[:, 9, :], R[:, 5, :], ALU.mult, **rw)
            self.tt("dve", R[:, 8, :], R[:, 8, :], R[:, 15, :], ALU.subtract, **rw)
            self.tt("dve", R[:, 8, :], R[:, 8, :], R[:, 6, :], ALU.mult, **rw)
            Gr = R[:, 7, :].rearrange("p (d s) -> p d s", d=2)
            Gi = R[:, 8, :].rearrange("p (d s) -> p d s", d=2)
            t6 = R[:, 6, :].rearrange("p (d s) -> p d s", d=2)
            t15 = R[:, 15, :].rearrange("p (d s) -> p d s", d=2)
            rw2 = dict(r=[K, "s5h"], w=[K])
            self.tt("dve", t6, Gr, H[:, :, 0, :], ALU.mult, **rw2)
            self.tt("dve", t15, Gi, H[:, :, 1, :], ALU.mult, **rw2)
            self.tt("dve", t6, t6, t15, ALU.subtract, **rw2)
            self.tt("dve", t15, Gr, H[:, :, 1, :], ALU.mult, **rw2)
            self.tt("dve", Gr, Gi, H[:, :, 0, :], ALU.mult, **rw2)
            self.tt("dve", H[:, :, 1, :], t15, Gr, ALU.add, r=[K, "s5h"], w=["s5h"])
            self.cp("dve", H[:, :, 0, :], t6, r=[K, "s5h"], w=["s5h"])

    def s5_mixer(self, l):
        e = l // 2
        T, L, nseq, NB = self.T, self.L, self.nseq, self.NB
        V, R = self.vecs, self.s5p
        nch = T // TC
        cps = L // TC
        self.s5_params(e)
        self.P.barrier()
        A = self.carve
        costab = A(0, 1024).rearrange("p (i j) -> p i j", i=8)
        sintab = A(1024, 1024).rearrange("p (i j) -> p i j", i=8)
        rhotab = A(2048, 1024).rearrange("p (i j) -> p i j", i=8)
        argt = A(3072, 1024)
        kt_ = A(4096, 1024)
        rt_ = A(5120, 1024)
        slot = []
        for i in range(4):
            base = 3072 + i * 1280
            slot.append(dict(t=[A(base + q * 128, 128) for q in range(6)],
                             g2=A(base + 768, 256).rearrange("p (r t) -> p r t", r=2),
                             pr=[A(base + 1024 + q * 64, 128, BF16) for q in range(4)]))
        u_bf = A(8192, 2048, BF16)
        y_acc = A(9216, 2048)
        ctp = A(11264, 3072, BF16).rearrange("p (a d r c) -> p a d r c", a=4, d=2, r=3)
        bT = A(12800, 1024, BF16).rearrange("p (a r c) -> p a r c", a=4, r=2)
        cstage = A(4352, 1024).rearrange("p (a r c) -> p a r c", a=4, r=2)
        bstage = A(3072, 1024).rearrange("p (a r c) -> p a r c", a=4, r=2)
        cin = self.lcst[:, 0:8].rearrange("p (a r) -> p a r", a=4)
        ctmp = self.lcst[:, 8:16].rearrange("p (a r) -> p a r", a=4)
        eis = self.lh0[:, :, :].rearrange("p d (a r) -> p d a r", a=4)
        ep = [A(3072 + i * 512, 512) for i in range(2)]
        for ut in range(4):
            self.load(cstage, self.d["s5c"][e][:, 4 * ut:4 * ut + 4], w=["cstage"])
            self.load(bstage, self.d["s5b"][e][:, ut], w=["bstage"])
            self.cp("pool", bT, bstage, r=["bstage"], w=["bT"])
            for dd in range(2):
                for sl in range(4):
                    col = dd * 16 + 4 * ut + sl
                    fr, fi = R[:, 4, col:col + 1], R[:, 5, col:col + 1]
                    cre, cim = cstage[:, sl, 0, :], cstage[:, sl, 1, :]
                    tA, tB = slot[3]["t"][0], slot[3]["t"][1]
                    self.ts("pool", tA, cim, fi, ALU.mult, r=["cstage", "s5p"], w=["tA"])
                    self.stt(ctp[:, sl, dd, 0, :], cre, fr, tA, ALU.mult, ALU.subtract, r=["cstage", "s5p", "tA"], w=["ctp"])
                    self.ts("pool", tB, cim, fr, ALU.mult, r=["cstage", "s5p"], w=["tB"])
                    self.stt(tB, cre, fi, tB, ALU.mult, ALU.add, r=["cstage", "s5p", "tB"], w=["tB"])
                    self.ts("dve", ctp[:, sl, dd, 2, :], tB, -1.0, ALU.mult, r=["tB"], w=["ctp"])
                    self.ts("dve", ctp[:, sl, dd, 1, :], ctp[:, sl, dd, 0, :], -1.0, ALU.mult, r=["ctp"], w=["ctp"])
                    Ei_c = R[:, 14, col:col + 1]
                    self.ts("dve", eis[:, dd, sl, 0:1], Ei_c, -1.0, ALU.mult, r=["s5p"], w=["eis"])
                    self.cp("dve", eis[:, dd, sl, 1:2], Ei_c, r=["s5p"], w=["eis"])
            self.P.barrier()
            for dd in range(2):
                for sl in range(4):
                    col = dd * 16 + 4 * ut + sl
                    i = dd * 4 + sl
                    self.ts("dve", argt[:, i * 128:(i + 1) * 128], self.jidx, R[:, 0, col:col + 1], ALU.mult,
                            r=["cst", "s5p"], w=[("tab", "arg")])
                    self.ts("pool", rhotab[:, i, :], self.ones, R[:, 1, col:col + 1], ALU.mult, r=["cst", "s5p"], w=["rhotab"])
            self.sincos(argt, sintab[:, :, :].rearrange("p i j -> p (i j)"), costab[:, :, :].rearrange("p i j -> p (i j)"), kt_, rt_, "tab")
            def cons(b, ps, key):
                self.cp("act", u_bf[:, b * 512:(b + 1) * 512], ps, r=[key], w=["u_bf"])
            self.proj_full(self.d["win_e"][e, ut], cons)
            self.P.barrier()
            for dd in range(2):
                order = list(range(nch)) if dd == 0 else list(range(nch - 1, -1, -1))
                pend = None
                for ci, c in enumerate(order):
                    tk = slice(c * TC, (c + 1) * TC)
                    seq = c // cps
                    first = (c % cps == 0) if dd == 0 else (c % cps == cps - 1)
                    last = (c % cps == cps - 1) if dd == 0 else (c % cps == 0)
                    ypk = ("yps", ci % 2)
                    yps = self.pb[4 + ci % 2][:, 0:TC]
                    for sl in range(4):
                        reg = sl
                        bu = self.pb[reg][:, 0:256].rearrange("p (r t) -> p r t", r=2)
                        for ri in range(2):
                            self.mm(bu[:, ri, :], bT[:, sl, ri, :], u_bf[:, tk], True, True,
                                    r=["bT", "u_bf"], w=[("bu", reg, ri)])
                    for sl in range(4):
                        reg = sl
                        i = dd * 4 + sl
                        col = dd * 16 + 4 * ut + sl
                        S = slot[sl]
                        sk = lambda n, sl=sl: ("slot", sl, n)
                        bu = self.pb[reg][:, 0:256].rearrange("p (r t) -> p r t", r=2)
                        bre_p, bim_p = bu[:, 0, :], bu[:, 1, :]
                        if dd == 1:
                            bre_p, bim_p = self.rev(bre_p), self.rev(bim_p)
                        t1, t2, t3, t4, bre, bim = S["t"]
                        g2 = S["g2"]
                        gre, gim = g2[:, 0, :], g2[:, 1, :]
                        cs, sn, rh = costab[:, i, :], sintab[:, i, :], rhotab[:, i, :]
                        tabk = [("tab", "sin"), ("tab", "cos")]
                        if first:
                            if self.pn == "S":
                                self.cp("pool", cin[:, sl, :], self.s5h[:, dd, :, 4 * ut + sl], r=["s5h"], w=[("cin", sl)])
                            else:
                                self.memset("pool", cin[:, sl, :], 0.0, w=[("cin", sl)])
                        self.tt("dve", t1, bre_p, cs, ALU.mult, r=[("bu", reg, 0)] + tabk, w=[sk("t1")])
                        self.tt("dve", t2, bim_p, sn, ALU.mult, r=[("bu", reg, 1)] + tabk, w=[sk("t2")])
                        self.tt("pool", bre, t1, t2, ALU.add, r=[sk("t1"), sk("t2")], w=[sk("bre")])
                        self.tt("dve", t3, bim_p, cs, ALU.mult, r=[("bu", reg, 1)] + tabk, w=[sk("t3")])
                        self.tt("dve", t4, bre_p, sn, ALU.mult, r=[("bu", reg, 0)] + tabk, w=[sk("t4")])
                        self.tt("pool", bim, t3, t4, ALU.subtract, r=[sk("t3"), sk("t4")], w=[sk("bim")])
                        for (g_, b_, ri) in ((gre, bre, 0), (gim, bim, 1)):
                            self.P.op("dve", lambda en, g_=g_, b_=b_, rh=rh, ini=cin[:, sl, ri:ri + 1]: en.tensor_tensor_scan(
                                out=g_, data0=rh, data1=b_, initial=ini, op0=ALU.mult, op1=ALU.add),
                                reads=[sk("bre" if ri == 0 else "bim"), "rhotab", ("cin", sl)], writes=[sk("g2")])
                        Er = R[:, 13, col:col + 1]
                        gl = g2[:, :, TC - 1]
                        (gps, gpn), (gfs, gfn) = gl.ap
                        gl_sw = AP(gl.tensor, gl.offset + gfs, [[gps, gpn], [-gfs, 2]])
                        gk = [sk("g2"), "s5p", "eis"]
                        self.tt("dve", ctmp[:, sl, :], gl_sw, eis[:, dd, sl, :], ALU.mult, r=gk, w=[("ctmp", sl)])
                        self.stt(cin[:, sl, :], gl, Er, ctmp[:, sl, :], ALU.mult, ALU.add, r=gk + [("ctmp", sl)], w=[("cin", sl)])
                        if last and self.pn == "P":
                            F2r, F2i = R[:, 11, col:col + 1], R[:, 12, col:col + 1]
                            s_ = 4 * ut + sl
                            o_re = self.fin5[:, seq, dd, 0, s_:s_ + 1]
                            o_im = self.fin5[:, seq, dd, 1, s_:s_ + 1]
                            ck = [("cin", sl), "s5p"]
                            self.ts("pool", ctmp[:, sl, 0:1], cin[:, sl, 1:2], F2i, ALU.mult, r=ck, w=[("ctmp", sl)])
                            self.ts("pool", ctmp[:, sl, 1:2], cin[:, sl, 0:1], F2i, ALU.mult, r=ck, w=[("ctmp", sl)])
                            self.stt(o_re, cin[:, sl, 0:1], F2r, ctmp[:, sl, 0:1], ALU.mult, ALU.subtract, r=ck + [("ctmp", sl)], w=["fin5"])
                            self.stt(o_im, cin[:, sl, 1:2], F2r, ctmp[:, sl, 1:2], ALU.mult, ALU.add, r=ck + [("ctmp", sl)], w=["fin5"])
                        prs = S["pr"]
                        prs_o = prs if dd == 0 else [self.rev(p_) for p_ in prs]
                        for q, (gg, tab, var) in enumerate(((gre, cs, 0), (gim, sn, 1), (gre, sn, 2), (gim, cs, 2))):
                            self.tt("pool", prs_o[q], gg, tab, ALU.mult, r=[sk("g2")] + tabk, w=[sk("pr%d" % q)])
                            self.mm(yps, ctp[:, sl, dd, var, :], prs[q], sl == 0 and q == 0, sl == 3 and q == 3,
                                    r=["ctp", sk("pr%d" % q)], w=[ypk])
                    if dd == 0:
                        self.cp("act", y_acc[:, tk], yps, r=[ypk], w=[("y_acc", c)])
                    else:
                        self.tt("dve", y_acc[:, tk], y_acc[:, tk], yps, ALU.add, r=[ypk, ("y_acc", c)], w=[("y_acc", c)])
            self.P.barrier()
            yk = [("y_acc", c) for c in range(nch)]

            def cons_z(b, ps, key, ut=ut):
                bs = slice(b * 512, (b + 1) * 512)
                self.act(ep[0], ps, AF.Sigmoid, r=[key], w=["ep0"])
                self.stt(ep[1], u_bf[:, bs], V[:, vcol("s5_d%d" % e, ut, 1)], y_acc[:, bs], ALU.mult, ALU.add,
                         r=["u_bf", "vecs"] + yk, w=["ep1"])
                self.act(ep[1], ep[1], AF.Gelu_apprx_tanh, r=["ep1"], w=["ep1"])
                self.tt("dve", self.yb[:, ut, bs], ep[1], ep[0], ALU.mult, r=["ep0", "ep1"], w=[("yb", b)])
            self.proj_full(self.d["win_e"][e, 4 + ut], cons_z)
            self.P.barrier()
        if self.pn == "P":
            for seq in range(4):
                for dd in range(2):
                    for ri in range(2):
                        self.store(self.d["o_s5"][seq, e, dd, ri], self.fin5[:, seq, dd, ri, :], r=["fin5"])

    def bmid(self, ap2, n):
        (ps, pn_), (fs, fn) = ap2.ap
        return AP(ap2.tensor, ap2.offset, [[ps, pn_], [0, n], [fs, fn]])

    def gdn_mixer(self, l):
        e = l // 2
        T, L, nseq, NB = self.T, self.L, self.nseq, self.NB
        V = self.vecs
        A = self.carve
        GC = 128
        NLV = 7
        nchk = T // GC
        cps = L // GC
        tb = 10240
        abtok = A(tb, 256).rearrange("p (n c) -> p n c", c=16)[:, 0:nchk, :]
        def tok8(off):
            return A(tb + off, 128).rearrange("p (n c) -> p n c", c=8)[:, 0:nchk, :]
        gtok, btok, gam, beg, kd, eglast = tok8(256), tok8(384), tok8(512), tok8(640), tok8(768), tok8(896)
        nexp = A(tb + 1024, 8)
        wv, wk = self.slab(self.d["win_e"][e, 24])
        w3 = wv.rearrange("p (k c) -> p k c", k=8)
        pq = self.pb[6]
        for n in range(nchk):
            for kt in range(8):
                self.mm(pq[:, n * 16:(n + 1) * 16], self.h[:, kt, n * GC:(n + 1) * GC], w3[:, kt, 0:16], kt == 0, kt == 7,
                        r=[wk, ("h", n // 4)], w=["pq"])
        self.cp("act", abtok, pq[:, 0:nchk * 16].rearrange("p (n c) -> p n c", c=16), r=["pq"], w=["abtok"])
        K = "gtokk"
        rw = dict(r=["abtok", "vecs", K], w=[K])
        self.tt("dve", gtok, abtok[:, :, 0:8], self.bmid(V[:, vcol("gdn_dtb%d" % e, 0, 8)], nchk), ALU.add, **rw)
        self.act(gtok, gtok, AF.Exp, **rw)
        self.act(gtok, gtok, AF.Ln, bias=self.onec[:, 0:1], **rw)
        self.act(nexp, V[:, vcol("gdn_alog%d" % e, 0, 8)], AF.Exp, **rw)
        self.ts("dve", nexp, nexp, -1.0, ALU.mult, **rw)
        self.tt("dve", gtok, gtok, self.bmid(nexp, nchk), ALU.mult, **rw)
        self.act(btok, abtok[:, :, 8:16], AF.Sigmoid, **rw)
        self.P.barrier()
        gdir = A(tb + 1032, 128)
        for dd, tri in ((0, self.m_le), (1, self.m_ge)):
            gd = gdir[:, dd * 64:dd * 64 + nchk * 4]
            self.cp("dve", gd.rearrange("p (n c) -> p n c", c=4), gtok[:, :, dd * 4:(dd + 1) * 4], r=[K], w=[("gdir", dd)])
            self.mm(pq[:, dd * 64:dd * 64 + nchk * 4], tri, gd, True, True, r=[("gdir", dd), "cst"], w=["pq"])
        g2d = A(tb + 256, nchk * 8)
        pl2 = pq[:, 256:256 + nchk * 8]
        self.mm(pl2, self.ones, g2d, True, True, r=[K, "cst"], w=["pq2"])
        pl = pl2.rearrange("p (n c) -> p n c", c=8)
        for dd in range(2):
            self.cp("act", gam[:, :, dd * 4:(dd + 1) * 4], pq[:, dd * 64:dd * 64 + nchk * 4].rearrange("p (n c) -> p n c", c=4),
                    r=["pq"], w=[K])
        self.act(eglast, pl, AF.Exp, r=["pq2"], w=[K])
        self.tt("dve", kd, pl, gam, ALU.subtract, r=["pq2", K], w=[K])
        self.act(kd, kd, AF.Exp, **rw)
        self.act(beg, gam, AF.Exp, **rw)
        self.tt("dve", beg, beg, btok, ALU.mult, **rw)
        self.P.barrier()
        bmk = A(11400, 2 * NLV * 128)
        self.load(bmk, self.d["bmk"], w=["bmk"])
        bml = bmk[:, 0:NLV * 128].rearrange("p (m c) -> p m c", m=NLV)
        bmu = bmk[:, NLV * 128:2 * NLV * 128].rearrange("p (m c) -> p m c", m=NLV)
        q_bf, k_bf, v_bf = A(0, 2048, BF16), A(1024, 2048, BF16), A(2048, 2048, BF16)
        craw, cout = A(3072, 2048), A(5120, 2048)
        o_acc = A(3072, 2048)
        I128 = self.ident
        for hd in range(4):
            for which, (dst, tile0) in enumerate(((q_bf, 8), (k_bf, 12), (v_bf, 16))):
                ci = which * 4 + hd

                def cons(b, ps, key):
                    self.cp("act", craw[:, b * 512:(b + 1) * 512], ps, r=[key], w=["craw"])
                self.proj_full(self.d["win_e"][e, tile0 + hd], cons)
                cw = lambda j, ci=ci: V[:, vcol("gdn_conv_w%d" % e, j * 12 + ci, 1)]
                co = cout[:, 0:T]
                self.act(co, craw[:, 0:T], AF.Identity, r=["craw", "vecs"], w=["cout"],
                         bias=V[:, vcol("gdn_conv_b%d" % e, ci, 1)], scale=cw(1))
                r3 = craw[:, 0:T].rearrange("p (s l) -> p s l", s=nseq)
                x3 = co.rearrange("p (s l) -> p s l", s=nseq)
                kk = dict(r=["craw", "cout", "vecs"], w=["cout"])
                self.stt(x3[:, :, 1:L], r3[:, :, 0:L - 1], cw(0), x3[:, :, 1:L], ALU.mult, ALU.add, **kk)
                self.stt(x3[:, :, 0:L - 1], r3[:, :, 1:L], cw(2), x3[:, :, 0:L - 1], ALU.mult, ALU.add, **kk)
                self.stt(x3[:, :, 0:L - 2], r3[:, :, 2:L], cw(3), x3[:, :, 0:L - 2], ALU.mult, ALU.add, **kk)
                self.act(co, co, AF.Silu, r=["cout"], w=["cout"])
                if which == 2:
                    self.cp("pool", dst[:, 0:T], co, r=["cout"], w=["qkv"])
                else:
                    for b in range(NB):
                        bs = slice(b * 512, (b + 1) * 512)
                        sq = A(7168, 512, BF16)
                        rs = A(7424, 512)
                        self.act(sq, cout[:, bs], AF.Square, r=["cout"], w=["gsq"])
                        self.mm(self.pb[7][:, :], self.ones_bf, sq, True, True, r=["gsq", "cbf"], w=[("pb", 7)])
                        self.act(rs, self.pb[7][:, :], AF.Sqrt, r=[("pb", 7)], w=["grs"], bias=self.epsc[:, 0:1])
                        self.P.op("dve", lambda en, rs=rs: en.reciprocal(out=rs, in_=rs), reads=["grs"], writes=["grs"])
                        self.stt(dst[:, bs], cout[:, bs], (128.0 ** -0.5) if which == 0 else 1.0, rs, ALU.mult, ALU.mult,
                                 r=["cout", "grs"], w=["qkv"])
            self.P.barrier()
            self.memset("pool", o_acc[:, 0:T], 0.0, w=[("o_acc", n) for n in range(nchk)])
            ST = []
            for st in range(2):
                base = 5120 + st * 2560
                o = [base]

                def nx(n, dt=F32, o=o):
                    ap = A(o[0], n, dt)
                    o[0] += n if dt == F32 else (n + 1) // 2
                    return ap
                ST.append(dict(gtri=nx(128), dce=nx(128), dec=nx(128), A0=nx(128), Ncm=nx(128), B0=nx(128), Nem=nx(128),
                               Nem2=nx(128),
                               T=nx(128), TT=nx(128), M1=nx(128), qk=nx(128, BF16), X0=nx(256), X1=nx(256),
                               wmt=nx(128, BF16), vn=nx(128, BF16), kdb=nx(128, BF16), eg=nx(128), qg=nx(128, BF16),
                               S=nx(128), Sb=nx(128, BF16)))
                assert o[0] - base <= 2560

            def unit(st, n, hd=hd):
                dd = st
                Sx = ST[st]
                col = dd * 4 + hd
                tk = slice(n * GC, (n + 1) * GC)
                bA, bB, bC, bD = self.pb[st * 4], self.pb[st * 4 + 1], self.pb[st * 4 + 2], self.pb[st * 4 + 3]
                k_ = lambda name: ("g", st, name)
                tri = self.m_le if dd == 0 else self.m_ge
                negs = self.m_ngt if dd == 0 else self.m_nlt
                incl = self.m_le if dd == 0 else self.m_ge
                g_c, b_c, ga_c = gtok[:, n, col:col + 1], btok[:, n, col:col + 1], gam[:, n, col:col + 1]
                beg_c, kd_c, egl_c = beg[:, n, col:col + 1], kd[:, n, col:col + 1], eglast[:, n, col:col + 1]
                raw_ce, rawq_ec, Gam, ktok = bA[:, 0:128], bA[:, 128:256], bA[:, 256:384], bA[:, 384:512]
                vtok, xp, wmt_p = bB[:, 0:128], bB[:, 128:384], bB[:, 384:512]
                vn_p, ot_p, s_p, b0_p = bC[:, 0:128], bC[:, 128:256], bC[:, 256:384], bC[:, 384:512]
                m1_p, m2_p, m2t_p = bD[:, 0:128], bD[:, 128:256], bD[:, 256:384]
                A0, B0, Ncm, Nem, Tm, TTm, M1s = Sx["A0"], Sx["B0"], Sx["Ncm"], Sx["Nem"], Sx["T"], Sx["TT"], Sx["M1"]
                X0, X1 = Sx["X0"], Sx["X1"]
                mce = lambda m: (bml if dd == 0 else bmu)[:, m, :]
                mec = lambda m: (bmu if dd == 0 else bml)[:, m, :]
                stages = []

                def s1():
                    self.ts("pool", Sx["gtri"], tri, g_c, ALU.mult, r=["cst", K], w=[k_("gtri")])
                    self.mm(Gam, self.ones, Sx["gtri"], True, True, r=["cst", k_("gtri")], w=[k_("Gam")])
                    self.mm(raw_ce, k_bf[:, tk], k_bf[:, tk], True, True, r=["qkv"], w=[k_("raw_ce")])
                    self.mm(rawq_ec, k_bf[:, tk], q_bf[:, tk], True, True, r=["qkv"], w=[k_("rawq")])
                    self.mm(ktok, k_bf[:, tk], self.ident_bf, True, True, r=["qkv", "cbf"], w=[k_("ktok")])
                    self.mm(vtok, v_bf[:, tk], self.ident_bf, True, True, r=["qkv", "cbf"], w=[k_("vtok")])
                stages.append(s1)

                def s2():
                    self.ts("dve", Sx["dce"], Gam, ga_c, ALU.subtract, r=[k_("Gam"), K], w=[k_("dce")], s2=0.0, op1=ALU.max)
                    self.act(Sx["dce"], Sx["dce"], AF.Exp, r=[k_("dce")], w=[k_("dce")], scale=-1.0)
                    self.ts("dve", Sx["dec"], Gam, ga_c, ALU.subtract, r=[k_("Gam"), K], w=[k_("dec")], s2=0.0, op1=ALU.min)
                    self.act(Sx["dec"], Sx["dec"], AF.Exp, r=[k_("dec")], w=[k_("dec")])
                    self.tt("pool", Sx["dce"], Sx["dce"], negs, ALU.mult, r=[k_("dce"), "cst"], w=[k_("dce")])
                    self.tt("pool", Sx["dec"], Sx["dec"], incl, ALU.mult, r=[k_("dec"), "cst"], w=[k_("dec")])
                    self.act(Sx["eg"], Gam, AF.Exp, r=[k_("Gam")], w=[k_("eg")])
                    self.tt("dve", Sx["qg"], q_bf[:, tk], Sx["eg"], ALU.mult, r=["qkv", k_("eg")], w=[k_("qg")])
                stages.append(s2)

                def s3():
                    self.stt(A0, raw_ce, b_c, Sx["dce"], ALU.mult, ALU.mult, r=[k_("raw_ce"), K, k_("dce")], w=[k_("A0")])
                    self.tt("dve", Sx["qk"], rawq_ec, Sx["dec"], ALU.mult, r=[k_("rawq"), k_("dec")], w=[k_("qk")])
                    self.tr(b0_p, A0, I128, r=[k_("A0"), "cst"], w=[k_("b0p")])
                    self.cp("act", B0, b0_p, r=[k_("b0p")], w=[k_("B0")])
                    self.act(X0[:, 0:128], vtok, AF.Identity, r=[k_("vtok"), K], w=[k_("X0")], scale=b_c)
                    self.act(X0[:, 128:256], ktok, AF.Identity, r=[k_("ktok"), K], w=[k_("X0")], scale=beg_c)
                    self.act(Sx["kdb"], ktok, AF.Identity, r=[k_("ktok"), K], w=[k_("kdb")], scale=kd_c)
                stages.append(s3)

                NemB = [Nem, Sx["Nem2"]]

                def sl0():
                    self.tt("pool", Ncm, A0, mce(0), ALU.mult, r=[k_("A0"), "bmk"], w=[k_("Ncm")])
                    self.tt("pool", NemB[0], B0, mec(0), ALU.mult, r=[k_("B0"), "bmk"], w=[k_("Nem0")])
                    self.tt("dve", Tm, Ncm, I128, ALU.add, r=[k_("Ncm"), "cst"], w=[k_("T")])
                    self.tt("dve", TTm, NemB[0], I128, ALU.add, r=[k_("Nem0"), "cst"], w=[k_("TT")])
                    self.tt("pool", NemB[1], B0, mec(1), ALU.mult, r=[k_("B0"), "bmk"], w=[k_("Nem1")])
                stages.append(sl0)
                for m in range(1, NLV):
                    def sla(m=m):
                        cur = NemB[m % 2]
                        self.mm(m1_p, cur, Tm, True, True, r=[k_("Nem%d" % (m % 2)), k_("T")], w=[k_("m1p")])
                        self.cp("act", M1s, m1_p, r=[k_("m1p")], w=[k_("M1")])
                        if m + 1 < NLV:
                            self.tt("pool", NemB[(m + 1) % 2], B0, mec(m + 1), ALU.mult, r=[k_("B0"), "bmk"],
                                    w=[k_("Nem%d" % ((m + 1) % 2))])

                    def slb(m=m):
                        self.mm(m2_p, TTm, M1s, True, True, r=[k_("TT"), k_("M1")], w=[k_("m2p")])
                        self.tt("dve", Tm, Tm, m2_p, ALU.add, r=[k_("T"), k_("m2p")], w=[k_("T")])
                        self.tr(m2t_p, Tm, I128, r=[k_("T"), "cst"], w=[k_("m2tp")])
                        self.cp("act", TTm, m2t_p, r=[k_("m2tp")], w=[k_("TT")])
                    stages.append(sla)
                    stages.append(slb)

                def sx():
                    self.mm(xp, TTm, X0, True, True, r=[k_("TT"), k_("X0")], w=[k_("xp")])
                    self.cp("act", X1, xp, r=[k_("xp")], w=[k_("X1")])
                stages.append(sx)

                def s4():
                    self.tr(wmt_p, X1[:, 128:256], I128, r=[k_("X1"), "cst"], w=[k_("wmtp")])
                    self.cp("act", Sx["wmt"], wmt_p, r=[k_("wmtp")], w=[k_("wmt")])
                    self.mm(vn_p, Sx["wmt"], Sx["Sb"], True, True, r=[k_("wmt"), k_("Sb")], w=[k_("vnp")])
                    self.tt("dve", Sx["vn"], X1[:, 0:128], vn_p, ALU.subtract, r=[k_("X1"), k_("vnp")], w=[k_("vn")])
                stages.append(s4)

                def s5():
                    self.mm(ot_p, Sx["Sb"], Sx["qg"], True, False, r=[k_("Sb"), k_("qg")], w=[k_("otp")])
                    self.mm(ot_p, Sx["vn"], Sx["qk"], False, True, r=[k_("vn"), k_("qk")], w=[k_("otp")])
                    self.mm(s_p, Sx["kdb"], Sx["vn"], True, True, r=[k_("kdb"), k_("vn")], w=[k_("sp")])
                    self.tt("dve", o_acc[:, tk], o_acc[:, tk], ot_p, ALU.add, r=[k_("otp"), ("o_acc", n)], w=[("o_acc", n)])
                    self.stt(Sx["S"], Sx["S"], egl_c, s_p, ALU.mult, ALU.add, r=[k_("S"), K, k_("sp")], w=[k_("S")])
                    self.cp("pool", Sx["Sb"], Sx["S"], r=[k_("S")], w=[k_("Sb")])
                stages.append(s5)
                return stages

            for seq in range(nseq):
                for st in range(2):
                    Sx = ST[st]
                    if self.pn == "S":
                        self.load(Sx["S"], self.d["dl0"][:, e, st, hd, :], w=[("g", st, "S")])
                    else:
                        self.memset("pool", Sx["S"], 0.0, w=[("g", st, "S")])
                    self.cp("pool", Sx["Sb"], Sx["S"], r=[("g", st, "S")], w=[("g", st, "Sb")])
                for i in range(cps):
                    units = [unit(0, seq * cps + i), unit(1, seq * cps + cps - 1 - i)]
                    for sg in range(len(units[0])):
                        for st in range(2):
                            units[st][sg]()
                if self.pn == "P":
                    for st in range(2):
                        self.store(self.d["o_dl"][seq, e, st, hd], ST[st]["S"], r=[("g", st, "S")])
            self.P.barrier()
            okeys = [("o_acc", n) for n in range(nchk)]
            et = [A(5120 + i * 512, 512) for i in range(3)]

            def cons_z(b, ps, key, hd=hd):
                bs = slice(b * 512, (b + 1) * 512)
                sq = A(7168, 512, BF16)
                self.act(et[0], ps, AF.Silu, r=[key], w=["et0"])
                self.act(sq, o_acc[:, bs], AF.Square, r=okeys, w=["gsq"])
                self.mm(self.pb[7][:, :], self.ones_bf, sq, True, True, r=["gsq", "cbf"], w=[("pb", 7)])
                self.act(et[1], self.pb[7][:, :], AF.Sqrt, r=[("pb", 7)], w=["et1"], bias=self.epsc[:, 0:1], scale=1.0 / 128)
                self.P.op("dve", lambda en: en.reciprocal(out=et[1], in_=et[1]), reads=["et1"], writes=["et1"])
                self.stt(et[2], o_acc[:, bs], V[:, vcol("gdn_onorm%d" % e, 0, 1)], et[1], ALU.mult, ALU.mult,
                         r=okeys + ["et1", "vecs"], w=["et2"])
                self.tt("dve", self.yb[:, 4 + hd, bs], et[2], et[0], ALU.mult, r=["et0", "et2"], w=[("yb", b)])
            self.proj_full(self.d["win_e"][e, 20 + hd], cons_z)
            self.P.barrier()

    def odd_mixer(self, l):
        o = l // 2
        T, L, nseq, NB = self.T, self.L, self.nseq, self.NB
        V = self.vecs
        xc = self.carve(0, 4096).rearrange("p (i t) -> p i t", i=2)
        xcb = self.carve(4096, 4096, BF16).rearrange("p (i t) -> p i t", i=2)
        hs = self.carve(6144, 2048)
        tmp = [self.carve(8192 + i * 512, 512) for i in range(6)]
        t_ra, t_i, t_a2, t_b = tmp[0], tmp[1], tmp[2], tmp[3]
        t_h = [tmp[4], tmp[5]]
        cst = self.lcst
        self.act(cst[:, :], V[:, vcol("lru_lam%d" % o, 0, 16)], AF.Exp, r=["vecs"], w=["lcst"], scale=-1.0)
        self.act(cst[:, :], cst[:, :], AF.Ln, r=["lcst"], w=["lcst"], bias=self.onec[:, 0:1])
        self.ts("dve", cst[:, :], cst[:, :], -LRU_C, ALU.mult, r=["lcst"], w=["lcst"])
        if self.pn == "S":
            self.load(self.lh0[:], self.d["lru0"][:, o], w=["lh0"])
        for n in range(4):
            for ti in range(2):
                ft = 2 * n + ti
                raw = hs

                def cons(b, ps, key, raw=raw):
                    self.cp("act", raw[:, b * 512:(b + 1) * 512], ps, r=[key], w=["hs"])
                self.proj_full(self.d["win_o"][o, ft], cons)
                cw = lambda j, ft=ft: V[:, vcol("lru_conv_w%d" % o, j * 8 + ft, 1)]
                xci = xc[:, ti, 0:T]
                self.act(xci, raw[:, 0:T], AF.Identity, r=["hs", "vecs"], w=[("xc", ti)],
                         bias=V[:, vcol("lru_conv_b%d" % o, ft, 1)], scale=cw(1))
                r3 = raw[:, 0:T].rearrange("p (s l) -> p s l", s=nseq)
                x3 = xci.rearrange("p (s l) -> p s l", s=nseq)
                self.stt(x3[:, :, 1:L], r3[:, :, 0:L - 1], cw(0), x3[:, :, 1:L], ALU.mult, ALU.add, r=["hs", ("xc", ti), "vecs"], w=[("xc", ti)])
                self.stt(x3[:, :, 0:L - 1], r3[:, :, 1:L], cw(2), x3[:, :, 0:L - 1], ALU.mult, ALU.add, r=["hs", ("xc", ti), "vecs"], w=[("xc", ti)])
                self.stt(x3[:, :, 0:L - 2], r3[:, :, 2:L], cw(3), x3[:, :, 0:L - 2], ALU.mult, ALU.add, r=["hs", ("xc", ti), "vecs"], w=[("xc", ti)])
                self.cp("pool", xcb[:, ti, 0:T], xci, r=[("xc", ti)], w=[("xcb", ti)])
            for ti in range(2):
                jt = 2 * n + ti
                for dd in range(2):
                    wv, wk = self.slab(self.d["wlru"][o, dd, n])
                    w4 = wv.rearrange("p (g k c) -> p g k c", g=2, k=2)
                    blocks = list(range(NB)) if dd == 0 else list(range(NB - 1, -1, -1))
                    prev = None
                    for bi, b in enumerate(blocks):
                        bs = slice(b * 512, (b + 1) * 512)
                        pr, pi_ = self.pb[4], self.pb[5]
                        for g, pp in ((0, pr), (1, pi_)):
                            for kt in range(2):
                                self.mm(pp[:, :], w4[:, g, kt, ti * 128:(ti + 1) * 128], xcb[:, kt, bs], kt == 0, kt == 1,
                                        r=[wk, ("xcb", kt)], w=[("pb", 4 + g)])
                        self.act(t_ra, pr[:, :], AF.Sigmoid, r=[("pb", 4), "vecs"], w=["t_ra"], bias=V[:, vcol("lru_b_r%d" % o, dd * 8 + jt, 1)])
                        self.act(t_ra, t_ra, AF.Exp, r=["t_ra", "lcst"], w=["t_ra"], scale=cst[:, dd * 8 + jt:dd * 8 + jt + 1])
                        self.act(t_i, pi_[:, :], AF.Sigmoid, r=[("pb", 5), "vecs"], w=["t_i"], bias=V[:, vcol("lru_b_i%d" % o, dd * 8 + jt, 1)])
                        self.tt("pool", t_a2, t_ra, t_ra, ALU.mult, r=["t_ra"], w=["t_a2"])
                        self.act(t_a2, t_a2, AF.Sqrt, r=["t_a2"], w=["t_a2"], bias=self.onec[:, 0:1], scale=-1.0)
                        self.tt("dve", t_b, t_i, xc[:, ti, bs], ALU.mult, r=["t_i", ("xc", ti)], w=["t_b"])
                        self.tt("dve", t_b, t_b, t_a2, ALU.mult, r=["t_b", "t_a2"], w=["t_b"])
                        th = t_h[bi % 2]
                        thk = ("t_h", bi % 2)
                        nsub = max(1, 512 // L)
                        seglen = min(512, L)
                        for sg in (range(nsub) if dd == 0 else range(nsub - 1, -1, -1)):
                            lo = sg * seglen
                            tok0 = b * 512 + lo
                            seq = tok0 // L
                            first = (tok0 % L == 0) if dd == 0 else ((tok0 + seglen) % L == 0)
                            if first:
                                init = self.lh0[:, dd, jt:jt + 1] if self.pn == "S" else 0.0
                                ir = ["lh0"] if self.pn == "S" else []
                            else:
                                pth = t_h[(bi - 1) % 2]
                                init = pth[:, 511:512] if dd == 0 else pth[:, 0:1]
                                ir = [("t_h", (bi - 1) % 2)]
                            o_ap, a_ap, b_ap = th[:, lo:lo + seglen], t_ra[:, lo:lo + seglen], t_b[:, lo:lo + seglen]
                            if dd == 1:
                                o_ap, a_ap, b_ap = self.rev(o_ap), self.rev(a_ap), self.rev(b_ap)
                            self.P.op("dve", lambda e, o_ap=o_ap, a_ap=a_ap, b_ap=b_ap, init=init: e.tensor_tensor_scan(
                                out=o_ap, data0=a_ap, data1=b_ap, initial=init, op0=ALU.mult, op1=ALU.add),
                                reads=["t_ra", "t_b"] + ir, writes=[thk])
                            last = ((tok0 + seglen) % L == 0) if dd == 0 else (tok0 % L == 0)
                            if last and self.pn == "P":
                                col = lo + seglen - 1 if dd == 0 else lo
                                self.cp("pool", self.finl[:, seq, dd, jt:jt + 1], th[:, col:col + 1], r=[thk], w=["finl"])
                        if dd == 0:
                            self.cp("pool", hs[:, bs], th, r=[thk], w=["hs"])
                        else:
                            self.tt("pool", hs[:, bs], hs[:, bs], th, ALU.add, r=[thk, "hs"], w=["hs"])

                def cons2(b, ps, key, jt=jt):
                    self.act(t_i, ps, AF.Gelu_apprx_tanh, r=[key], w=["t_i"])
                    self.tt("dve", self.yb[:, jt, b * 512:(b + 1) * 512], t_i, hs[:, b * 512:(b + 1) * 512], ALU.mult,
                            r=["t_i", "hs"], w=[("yb", b)])
                self.proj_full(self.d["win_o"][o, 8 + jt], cons2)
        if self.pn == "P":
            for seq in range(4):
                for dd in range(2):
                    self.store(self.d["o_lru"][seq, o, dd], self.finl[:, seq, dd, :], r=["finl"])

    def rev(self, ap):
        (ps, pn_), (fs, fn) = ap.ap
        return AP(ap.tensor, ap.offset + fs * (fn - 1), [[ps, pn_], [-fs, fn]])


def _fm(v, nt):
    return np.ascontiguousarray(np.asarray(v, np.float32).reshape(nt, 128).T)


def _slabify(W):
    W = np.asarray(W, np.float32)
    K, N = W.shape
    Np = (N + 127) // 128 * 128
    if Np != N:
        W = np.concatenate([W, np.zeros((K, Np - N), np.float32)], axis=1)
    return np.ascontiguousarray(W.reshape(8, 128, Np // 128, 128).transpose(2, 1, 0, 3).reshape(Np // 128, 128, 1024))


def _s5_state_layout(a):
    a = np.asarray(a, np.float32)
    lead = a.shape[:-2]
    a = a.reshape(lead + (16, 2, 64))
    nl = len(lead)
    a = np.moveaxis(a, (nl + 1, nl + 2), (0, 1))
    return np.ascontiguousarray(a.reshape((128,) + lead + (16,)))


def _grid_sincos(n_tokens):
    rows = n_tokens // 64
    row = np.repeat(np.arange(rows, dtype=np.float32), 64)
    col = np.tile(np.arange(64, dtype=np.float32), rows)
    n_freq = D // 4
    omega = (np.float32(10000.0) ** (-np.arange(n_freq, dtype=np.float32) / np.float32(n_freq))).astype(np.float32)
    ar = row[:, None] * omega
    ac = col[:, None] * omega
    return np.concatenate([np.sin(ar), np.cos(ar), np.sin(ac), np.cos(ac)], axis=-1).astype(np.float32)


def _consts():
    r = np.arange(128)[:, None]
    q = np.arange(128)[None, :]
    mats = [np.eye(128), np.ones((128, 128)), r > q, r >= q, r < q, r <= q, -(r > q).astype(np.float32), q + 0 * r, -(r < q).astype(np.float32)]
    return np.ascontiguousarray(np.concatenate([np.asarray(m, np.float32) for m in mats], axis=1))


def prepare_shared(inp):
    g = lambda k: np.asarray(inp[k], np.float32)
    sh = {}
    vecs = np.zeros((128, NV), np.float32)

    def put(name, arr):
        o, n = VEC[name]
        assert arr.shape == (128, n), (name, arr.shape, n)
        vecs[:, o:o + n] = arr
    for l in range(4):
        put("n_mix_pre%d" % l, _fm(g("norm_mix_pre")[l], 8))
        put("n_mix_post%d" % l, _fm(g("norm_mix_post")[l], 8))
        put("n_mlp_pre%d" % l, _fm(g("norm_mlp_pre")[l], 8))
        put("n_mlp_post%d" % l, _fm(g("norm_mlp_post")[l], 8))
        put("b_ada%d" % l, _fm(g("b_ada")[l], 48))
    for e in range(2):
        put("s5_lam_re%d" % e, _s5_state_layout(g("s5_lam_re")[e]).reshape(128, 32))
        put("s5_lam_im%d" % e, _s5_state_layout(g("s5_lam_im")[e]).reshape(128, 32))
        ldt = np.broadcast_to(g("s5_log_dt")[e][:, :, None], (2, 32, 64))
        put("s5_dt%d" % e, _s5_state_layout(ldt).reshape(128, 32))
        put("s5_d%d" % e, _fm(g("s5_d")[e], 4))
        cw = g("gdn_conv_w")[e]
        put("gdn_conv_w%d" % e, np.concatenate([_fm(cw[j], 12) for j in range(4)], axis=1))
        put("gdn_conv_b%d" % e, _fm(g("gdn_conv_b")[e], 12))
        put("gdn_alog%d" % e, np.broadcast_to(g("gdn_a_log")[e].reshape(1, 8), (128, 8)))
        put("gdn_dtb%d" % e, np.broadcast_to(g("gdn_dt_bias")[e].reshape(1, 8), (128, 8)))
        put("gdn_onorm%d" % e, g("gdn_o_norm")[e].reshape(128, 1))
    for o in range(2):
        cw = g("lru_conv_w")[o]
        put("lru_conv_w%d" % o, np.concatenate([_fm(cw[j], 8) for j in range(4)], axis=1))
        put("lru_conv_b%d" % o, _fm(g("lru_conv_b")[o], 8))
        put("lru_b_r%d" % o, np.concatenate([_fm(g("lru_b_r")[o, dd], 8) for dd in range(2)], axis=1))
        put("lru_b_i%d" % o, np.concatenate([_fm(g("lru_b_i")[o, dd], 8) for dd in range(2)], axis=1))
        put("lru_lam%d" % o, np.concatenate([_fm(g("lru_lam")[o, dd], 8) for dd in range(2)], axis=1))
    sh["vecs"] = vecs
    sh["cst"] = _consts()
    sh["pos"] = _grid_sincos(2048)
    c_ = np.arange(128)[:, None]
    e_ = np.arange(128)[None, :]
    lows = []
    for m in range(7):
        sz = 1 << m
        lows.append(((c_ // (2 * sz) == e_ // (2 * sz)) & (c_ % (2 * sz) >= sz) & (e_ % (2 * sz) < sz)).astype(np.float32))
    ups = [mk.T for mk in lows]
    sh["bmk"] = np.ascontiguousarray(np.concatenate(lows + ups, axis=1))
    sh["w_ada"] = np.stack([_slabify(g("w_ada")[l]) for l in range(4)])
    sh["win_e"] = np.stack([_slabify(g("w_in_even")[e]) for e in range(2)])
    sh["wout_e"] = np.stack([_slabify(g("w_out_even")[e]) for e in range(2)])
    sh["win_o"] = np.stack([_slabify(g("w_in_odd")[o]) for o in range(2)])
    sh["wout_o"] = np.stack([_slabify(g("w_out_odd")[o]) for o in range(2)])
    sh["wm1"] = np.stack([_slabify(g("w_mlp_in")[l]) for l in range(4)])
    w2 = g("w_mlp_out").reshape(4, 4, 8, 128, 8, 128).transpose(0, 4, 1, 3, 2, 5)
    sh["wm2"] = np.ascontiguousarray(w2.reshape(4, 8, 4, 128, 1024))
    wg = np.stack([g("lru_w_r"), g("lru_w_i")])
    wg = wg.reshape(2, 2, 2, 4, 2, 128, 256).transpose(1, 2, 3, 5, 0, 4, 6)
    sh["wlru"] = np.ascontiguousarray(wg.reshape(2, 2, 4, 128, 1024))
    s5b = np.zeros((2, 128, 4, 4, 2, 128), np.float32)
    s5c = np.zeros((2, 128, 16, 2, 128), np.float32)
    for ri, (kb, kc) in enumerate((("s5_b_re", "s5_c_re"), ("s5_b_im", "s5_c_im"))):
        B = g(kb)
        C = g(kc)
        for gg in range(32):
            s_, gh = gg // 2, gg % 2
            ut, sl = s_ // 4, s_ % 4
            rows = slice(sl * 32 + gh * 16, sl * 32 + gh * 16 + 16)
            cols = slice(gh * 64, gh * 64 + 64)
            s5b[:, rows, ut, sl, ri, cols] = B[:, gg].transpose(0, 2, 1)
            s5c[:, cols, s_, ri, rows] = C[:, gg].transpose(0, 2, 1)
    sh["s5b"] = s5b
    sh["s5c"] = s5c
    return sh


def prepare_core(inp, c, sh):
    g = lambda k: np.asarray(inp[k], np.float32)
    m = dict(sh)
    m["xs"] = np.ascontiguousarray(g("x_sample")[c])
    m["xp"] = np.ascontiguousarray(g("x_prompt")[4 * c:4 * c + 4].reshape(1024, D))
    cv = np.stack([_fm(g("c_ctx"), 8), _fm(g("c")[c], 8)], axis=-1)
    m["cvec"] = np.ascontiguousarray(cv)
    h0 = np.stack([g("state_s5_re")[c], g("state_s5_im")[c]], axis=2)
    m["s5h0"] = _s5_state_layout(h0)
    m["dl0"] = np.ascontiguousarray(g("state_delta")[c].transpose(3, 0, 1, 2, 4))
    m["lru0"] = np.ascontiguousarray(g("state_lru")[c].reshape(2, 2, 8, 128).transpose(3, 0, 1, 2))
    return m


_NC_CACHE = {}


def kernel(**inputs):
    if "nc" not in _NC_CACHE:
        _NC_CACHE["nc"] = Builder().nc
    nc = _NC_CACHE["nc"]
    sh = prepare_shared(inputs)
    in_maps = [prepare_core(inputs, c, sh) for c in range(8)]
    res = run_bass_kernel_spmd(nc, in_maps, core_ids=list(range(8))).results
    y_prompt = np.concatenate([r["yp"].reshape(4, 256, D) for r in res], axis=0)
    y_sample = np.stack([r["ys"] for r in res], axis=0)
    s5 = np.concatenate([r["o_s5"] for r in res], axis=0)
    s5 = s5.reshape(32, 2, 2, 2, 2, 64, 16).transpose(0, 1, 2, 3, 6, 4, 5).reshape(32, 2, 2, 2, 32, 64)
    new_re = np.ascontiguousarray(s5[:, :, :, 0])
    new_im = np.ascontiguousarray(s5[:, :, :, 1])
    new_delta = np.concatenate([r["o_dl"] for r in res], axis=0)
    lru = np.concatenate([r["o_lru"] for r in res], axis=0)
    new_lru = np.ascontiguousarray(lru.transpose(0, 1, 2, 4, 3).reshape(32, 2, 2, 1024))
    return (y_prompt.astype(np.float32), y_sample.astype(np.float32), new_re.astype(np.float32),
            new_im.astype(np.float32), new_delta.astype(np.float32), new_lru.astype(np.float32))
```

```python
import numpy as np
from contextlib import ExitStack
import concourse.bass as bass
import concourse.mybir as mybir
from concourse.bass import AP
from concourse.bass_utils import run_bass_kernel_spmd

F32 = mybir.dt.float32
BF16 = mybir.dt.bfloat16
ALU = mybir.AluOpType
AF = mybir.ActivationFunctionType

ENGS = ("pe", "act", "dve", "pool", "sp")
NDMASEM = 12


class Prog:
    def __init__(self, nc, same_engine_sync=("act", "dve", "pool")):
        self.nc = nc
        self.es = ExitStack()
        self.q = {e: [] for e in ENGS}
        self.count = {e: 0 for e in ENGS}
        self.seen = {e: {} for e in ENGS}
        self.clock_at = {}
        self.last_w = {}
        self.readers = {}
        self.ses = set(same_engine_sync)
        self.sem = {}
        for e in ENGS:
            if e != "sp":
                self.sem[e] = self.es.enter_context(nc.semaphore("s_" + e))
        self.dsem = {}
        self.dcnt = {}
        self.drr = {}
        for qn in ("sp", "act"):
            self.dsem[qn] = [self.es.enter_context(nc.semaphore("d_%s%d" % (qn, i))) for i in range(NDMASEM)]
            self.dcnt[qn] = [0] * NDMASEM
            self.drr[qn] = 0
        self.nwaits = 0
        self.nops = 0

    def sb(self, name, shape, dt=F32):
        return self.es.enter_context(self.nc.sbuf_tensor("sb_" + name, list(shape), dt))

    def ps(self, name, shape, dt=F32):
        return self.es.enter_context(self.nc.psum_tensor("ps_" + name, list(shape), dt))

    def _deps(self, reads, writes):
        deps = set()
        for k in reads:
            t = self.last_w.get(k)
            if t is not None:
                deps.add(t)
        for k in writes:
            t = self.last_w.get(k)
            if t is not None:
                deps.add(t)
            for r in self.readers.get(k, ()):
                deps.add(r)
        return deps

    def _mkwaits(self, eng, deps):
        need = {}
        for (e, s) in deps:
            if need.get(e, 0) < s:
                need[e] = s
        waits = []
        seen = self.seen[eng]
        for e, s in need.items():
            if e == eng and eng not in self.ses:
                continue
            if seen.get(e, 0) >= s:
                continue
            waits.append((e, s))
        for e, s in waits:
            if seen.get(e, 0) < s:
                seen[e] = s
            ck = self.clock_at.get((e, s))
            if ck:
                for f, v in ck.items():
                    if seen.get(f, 0) < v:
                        seen[f] = v
        self.nwaits += len(waits)
        return waits

    def _record(self, tok, reads, writes):
        for k in reads:
            self.readers.setdefault(k, []).append(tok)
        for k in writes:
            self.last_w[k] = tok
            self.readers[k] = []

    def op(self, eng, fn, reads=(), writes=()):
        deps = self._deps(reads, writes)
        waits = self._mkwaits(eng, deps)
        self.count[eng] += 1
        tok = (eng, self.count[eng])
        self.q[eng].append((fn, waits, ("c", eng)))
        self.clock_at[tok] = dict(self.seen[eng])
        self._record(tok, reads, writes)
        self.nops += 1
        return tok

    def dma(self, qn, fn, reads=(), writes=()):
        deps = self._deps(reads, writes)
        i = self.drr[qn]
        self.drr[qn] = (i + 1) % NDMASEM
        semname = ("d", qn, i)
        prev = self.dcnt[qn][i]
        if prev:
            deps.add((semname, prev))
        waits = self._mkwaits(qn, deps)
        self.dcnt[qn][i] = prev + 16
        tok = (semname, prev + 16)
        self.q[qn].append((fn, waits, ("d", qn, i)))
        self.clock_at[tok] = dict(self.seen[qn])
        self._record(tok, reads, writes)
        self.nops += 1
        return tok

    def barrier(self):
        toks = set()
        for e in ENGS:
            if e != "sp" and self.count[e]:
                toks.add((e, self.count[e]))
        for qn in self.dsem:
            for i in range(NDMASEM):
                if self.dcnt[qn][i]:
                    toks.add((("d", qn, i), self.dcnt[qn][i]))
        for e in ENGS:
            w = self._mkwaits(e, toks)
            if w:
                self.q[e].append((None, w, None))
        self.last_w = {}
        self.readers = {}

    def _semof(self, e):
        if isinstance(e, tuple):
            return self.dsem[e[1]][e[2]]
        return self.sem[e]

    def emit(self, final_tokens):
        nc = self.nc
        fw = self._mkwaits("sp", set(final_tokens))
        with nc.Block() as block:
            def run(engname):
                def body(eng):
                    for fn, waits, kind in self.q[engname]:
                        if fn is None:
                            for (e, s) in waits:
                                eng.wait_ge(self._semof(e), s)
                            continue
                        for (e, s) in waits[1:]:
                            eng.wait_ge(self._semof(e), s)
                        ins = fn(eng)
                        if waits:
                            ins._wait_ge(self._semof(waits[0][0]), waits[0][1])
                        if kind[0] == "c":
                            ins.then_inc(self.sem[kind[1]], 1)
                        else:
                            ins.then_inc(self.dsem[kind[1]][kind[2]], 16)
                    if engname == "sp":
                        for (e, s) in fw:
                            eng.wait_ge(self._semof(e), s)
                return body
            block.sync(run("sp"))
            block.tensor(run("pe"))
            block.scalar(run("act"))
            block.vector(run("dve"))
            block.gpsimd(run("pool"))
        self.es.close()


D = 1024
KT = 8
DFF = 4096
GA, PA = 32, 64
HB = 4
CH = 64
EPS = 1e-6
LRU_C = 8.0
TC = 128
PASSES = {"S": dict(nseq=1, L=2048), "P": dict(nseq=4, L=256)}

VEC = {}
_nv = [0]


def _vreg(name, n):
    VEC[name] = (_nv[0], n)
    _nv[0] += n


for _l in range(4):
    for _n in ("n_mix_pre", "n_mix_post", "n_mlp_pre", "n_mlp_post"):
        _vreg("%s%d" % (_n, _l), 8)
    _vreg("b_ada%d" % _l, 48)
for _e in range(2):
    _vreg("s5_lam_re%d" % _e, 32)
    _vreg("s5_lam_im%d" % _e, 32)
    _vreg("s5_dt%d" % _e, 32)
    _vreg("s5_d%d" % _e, 4)
    _vreg("gdn_conv_w%d" % _e, 48)
    _vreg("gdn_conv_b%d" % _e, 12)
    _vreg("gdn_alog%d" % _e, 8)
    _vreg("gdn_dtb%d" % _e, 8)
    _vreg("gdn_onorm%d" % _e, 1)
for _o in range(2):
    _vreg("lru_conv_w%d" % _o, 32)
    _vreg("lru_conv_b%d" % _o, 8)
    _vreg("lru_b_r%d" % _o, 16)
    _vreg("lru_b_i%d" % _o, 16)
    _vreg("lru_lam%d" % _o, 16)
NV = _nv[0]


def vcol(name, i=0, n=1):
    o, _ = VEC[name]
    return slice(o + i, o + i + n)


class Builder:
    def __init__(self, layers=(0, 1, 2, 3), passes=("S", "P"), debug=False):
        self.layers = layers
        self.passes = passes
        self.debug = debug
        nc = self.nc = bass.Bass("TRN2", target_bir_lowering=False)
        self.P = Prog(nc)
        self.outs = []
        self._decl_dram()
        self._alloc()
        self._prologue()
        for pn in passes:
            self.run_pass(pn)
        self.P.emit(self.outs)

    def _decl_dram(self):
        nc = self.nc

        def din(name, shape, dt=F32):
            return nc.dram_tensor(name, list(shape), dt, kind="ExternalInput").ap()

        def dout(name, shape):
            return nc.dram_tensor(name, list(shape), F32, kind="ExternalOutput").ap()
        self.d = d = {}
        d["xs"] = din("xs", [2048, D])
        d["xp"] = din("xp", [1024, D])
        d["pos"] = din("pos", [2048, D])
        d["cst"] = din("cst", [128, 128 * 9])
        d["bmk"] = din("bmk", [128, 14 * 128])
        d["cvec"] = din("cvec", [128, 8, 2])
        d["vecs"] = din("vecs", [128, NV])
        d["w_ada"] = din("w_ada", [4, 48, 128, 1024])
        d["win_e"] = din("win_e", [2, 25, 128, 1024])
        d["wout_e"] = din("wout_e", [2, 8, 128, 1024])
        d["win_o"] = din("win_o", [2, 16, 128, 1024])
        d["wout_o"] = din("wout_o", [2, 8, 128, 1024])
        d["wm1"] = din("wm1", [4, 32, 128, 1024])
        d["wm2"] = din("wm2", [4, 8, 4, 128, 1024])
        d["wlru"] = din("wlru", [2, 2, 4, 128, 1024])
        d["s5b"] = din("s5b", [2, 128, 4, 4, 2, 128])
        d["s5c"] = din("s5c", [2, 128, 16, 2, 128])
        d["s5h0"] = din("s5h0", [128, 2, 2, 2, 16])
        d["dl0"] = din("dl0", [128, 2, 2, 4, 128])
        d["lru0"] = din("lru0", [128, 2, 2, 8])
        d["ys"] = dout("ys", [2048, D])
        d["yp"] = dout("yp", [1024, D])
        d["o_s5"] = dout("o_s5", [4, 2, 2, 2, 128, 16])
        d["o_dl"] = dout("o_dl", [4, 2, 2, 4, 128, 128])
        d["o_lru"] = dout("o_lru", [4, 2, 2, 128, 8])
        if self.debug:
            d["dbg"] = dout("dbg", [128, 8, 2048])

    def _alloc(self):
        P = self.P
        self.x = P.sb("x", [128, KT, 2048], F32)
        self.h = P.sb("h", [128, KT, 2048], BF16)
        self.yb = P.sb("yb", [128, KT, 2048], BF16)
        self.wst = P.sb("wst", [128, 2, 1024], F32)
        self.wbf = P.sb("wbf", [128, 2, 1024], BF16)
        self.cst = P.sb("cst", [128, 128 * 9], F32)
        self.cbf = P.sb("cbf", [128, 256], BF16)
        self.vecs = P.sb("vecs", [128, NV], F32)
        self.mod = P.sb("mod", [128, 4, 48, 2], F32)
        self.mvec = P.sb("mvec", [128, 4, 2, 4, 8], F32)
        self.cv = P.sb("cv", [128, 8, 2], F32)
        self.ARENA = 13312
        self.arena = P.sb("arena", [128, self.ARENA], F32)
        self.pb = [P.ps("pb%d" % i, [128, 512], F32) for i in range(8)]
        self.slab_i = 0
        self.pb_rr = 0
        self.finl = P.sb("finl", [128, 4, 2, 8], F32)
        self.epsc = P.sb("epsc", [128, 1], F32)
        self.onec = P.sb("onec", [128, 1], F32)
        self.lcst = P.sb("lcst", [128, 16], F32)
        self.lh0 = P.sb("lh0", [128, 2, 8], F32)
        self.fin5 = P.sb("fin5", [128, 4, 2, 2, 16], F32)
        self.s5p = P.sb("s5p", [128, 16, 32], F32)
        self.s5h = P.sb("s5h", [128, 2, 2, 16], F32)
        self.ident = self.cst[:, 0:128]
        self.ones = self.cst[:, 128:256]
        self.m_gt = self.cst[:, 256:384]
        self.m_ge = self.cst[:, 384:512]
        self.m_lt = self.cst[:, 512:640]
        self.m_le = self.cst[:, 640:768]
        self.m_ngt = self.cst[:, 768:896]
        self.jidx = self.cst[:, 896:1024]
        self.m_nlt = self.cst[:, 1024:1152]
        self.ones_bf = self.cbf[:, 0:128]
        self.ident_bf = self.cbf[:, 128:256]

    def carve(self, off, n, dt=F32, parts=128):
        if dt == F32:
            return self.arena[0:parts, off:off + n]
        v = self.arena[0:parts, off:off + (n + 1) // 2].bitcast(BF16)
        return v[:, 0:n]

    def _bk(self, w, *aps):
        extra = []
        for a in aps:
            nm = getattr(a, "name", None)
            if isinstance(nm, str) and nm.startswith("ps_pb"):
                k = ("BANK", nm)
                if k not in extra:
                    extra.append(k)
        return list(w) + extra if extra else w

    def act(self, out, in_, func, r, w, bias=None, scale=None):
        w = self._bk(w, out, in_)
        kw = {}
        if bias is not None:
            kw["bias"] = bias
        if scale is not None:
            kw["scale"] = scale
        return self.P.op("act", lambda e: e.activation(out=out, in_=in_, func=func, **kw), reads=r, writes=w)

    def tt(self, eng, out, a, b, op, r, w):
        w = self._bk(w, out, a, b)
        return self.P.op(eng, lambda e: e.tensor_tensor(out=out, in0=a, in1=b, op=op), reads=r, writes=w)

    def ts(self, eng, out, a, s1, op0, r, w, s2=None, op1=None):
        w = self._bk(w, out, a)
        if op1 is None:
            return self.P.op(eng, lambda e: e.tensor_scalar(out=out, in0=a, scalar1=s1, scalar2=None, op0=op0), reads=r, writes=w)
        return self.P.op(eng, lambda e: e.tensor_scalar(out=out, in0=a, scalar1=s1, scalar2=s2, op0=op0, op1=op1), reads=r, writes=w)

    def stt(self, out, a, s, b, op0, op1, r, w):
        w = self._bk(w, out, a, b)
        return self.P.op("dve", lambda e: e.scalar_tensor_tensor(out=out, in0=a, scalar=s, in1=b, op0=op0, op1=op1), reads=r, writes=w)

    def cp(self, eng, out, in_, r, w):
        w = self._bk(w, out, in_)
        if eng == "act":
            return self.P.op("act", lambda e: e.copy(out=out, in_=in_), reads=r, writes=w)
        return self.P.op(eng, lambda e: e.tensor_copy(out=out, in_=in_), reads=r, writes=w)

    def mm(self, out, lhsT, rhs, start, stop, r, w):
        w = self._bk(w, out)
        return self.P.op("pe", lambda e: e.matmul(out, lhsT=lhsT, rhs=rhs, start=start, stop=stop), reads=r, writes=w)

    def tr(self, out, in_, ident, r, w):
        w = self._bk(w, out)
        return self.P.op("pe", lambda e: e.transpose(out, in_, ident), reads=r, writes=w)

    def memset(self, eng, ap, val, w):
        return self.P.op(eng, lambda e: e.memset(ap, val), writes=w)

    def load(self, out, in_, w, r=()):
        return self.P.dma("sp", lambda e: e.dma_start(out=out, in_=in_), reads=r, writes=w)

    def store(self, out, in_, r):
        t = self.P.dma("sp", lambda e: e.dma_start(out=out, in_=in_), reads=r)
        self.outs.append(t)
        return t

    def slab(self, dram_ap, cast=True):
        deep = getattr(self, "deep_ring", False)
        st_bufs = [self.wst[:, 0, :], self.wst[:, 1, :]]
        bf_bufs = [self.wbf[:, 0, :], self.wbf[:, 1, :]]
        if deep:
            st_bufs += [self.carve(4608, 1024), self.carve(5632, 1024), self.carve(11264, 1024), self.carve(12288, 1024)]
            bf_bufs += [self.carve(6656, 1024, BF16)]
        self.slab_n = getattr(self, "slab_n", 0) + 1
        i = self.slab_n % len(st_bufs)
        j = self.slab_n % len(bf_bufs)
        self.load(st_bufs[i], dram_ap, w=[("wst", i)])
        if not cast:
            return st_bufs[i], ("wst", i)
        self.cp(getattr(self, "cast_eng", "act"), bf_bufs[j], st_bufs[i], r=[("wst", i)], w=[("wbf", j)])
        return bf_bufs[j], ("wbf", j)

    def _prologue(self):
        P = self.P
        d = self.d
        self.load(self.cst[:], d["cst"], w=["cst"])
        self.load(self.vecs[:], d["vecs"], w=["vecs"])
        self.load(self.cv[:], d["cvec"], w=["cv"])
        self.memset("pool", self.epsc[:], EPS, w=["epsc"])
        self.memset("pool", self.onec[:], 1.0, w=["onec"])
        self.cp("dve", self.cbf[:, 0:128], self.cst[:, 128:256], r=["cst"], w=["cbf"])
        self.cp("dve", self.cbf[:, 128:256], self.cst[:, 0:128], r=["cst", "cbf"], w=["cbf"])
        self.act(self.cv[:], self.cv[:], AF.Silu, r=["cv"], w=["cv"])
        for l in self.layers:
            pm = self.pb[l % 2]
            for ft in range(48):
                wv, wk = self.slab(d["w_ada"][l, ft], cast=False)
                w3 = wv.rearrange("p (k c) -> p k c", k=8)
                for kt in range(8):
                    self.mm(pm[:, 2 * ft:2 * ft + 2], w3[:, kt, :], self.cv[:, kt, :], kt == 0, kt == 7,
                            r=[wk, "cv"], w=[("pm", l % 2)])
            for j in range(2):
                self.tt("dve", self.mod[:, l, :, j], pm[:, 0:96].rearrange("p (f j) -> p f j", j=2)[:, :, j],
                        self.vecs[:, vcol("b_ada%d" % l, 0, 48)], ALU.add, r=[("pm", l % 2), "vecs"], w=[("mod", l, j)])
                for q, (npre, npost, o_sc, o_gt) in enumerate((("n_mix_pre", "n_mix_post", 8, 16), ("n_mlp_pre", "n_mlp_post", 32, 40))):
                    self.stt(self.mvec[:, l, j, 2 * q, :], self.mod[:, l, o_sc:o_sc + 8, j], 1.0,
                             self.vecs[:, vcol("%s%d" % (npre, l), 0, 8)], ALU.add, ALU.mult,
                             r=[("mod", l, j), "vecs"], w=[("mvec", l, j, 2 * q)])
                    self.tt("dve", self.mvec[:, l, j, 2 * q + 1, :], self.mod[:, l, o_gt:o_gt + 8, j],
                            self.vecs[:, vcol("%s%d" % (npost, l), 0, 8)], ALU.mult,
                            r=[("mod", l, j), "vecs"], w=[("mvec", l, j, 2 * q + 1)])
        P.barrier()

    def run_pass(self, pn):
        cfg = PASSES[pn]
        self.pn = pn
        self.nseq, self.L = cfg["nseq"], cfg["L"]
        self.T = self.nseq * self.L
        self.NB = self.T // 512
        self.j = 1 if pn == "S" else 0
        self.load_x()
        for l in self.layers:
            self.modnorm_all(l)
            if l % 2 == 0:
                self.even_mixer(l)
            else:
                self.odd_mixer(l)
            self.P.barrier()
            self.cast_eng = "dve"
            self.deep_ring = True
            self.out_proj(l)
            self.P.barrier()
            self.mlp(l)
            self.cast_eng = "act"
            self.deep_ring = False
            self.P.barrier()
        if self.debug and pn == self.passes[-1]:
            self.store(self.d["dbg"][:, :, 0:self.T], self.x[:, :, 0:self.T], r=self.xkeys())
        self.store_x()
        self.P.barrier()

    def xkeys(self):
        return [("x", b) for b in range(self.NB)]

    def load_x(self):
        T = self.T
        src = self.d["xs"] if self.pn == "S" else self.d["xp"]
        xt = [self.carve(i * 1024, 1024) for i in range(2)]
        pt = [self.carve(2048 + i * 1024, 1024) for i in range(2)]
        for tt_ in range(T // 128):
            i = tt_ % 2
            self.load(xt[i], src[tt_ * 128:(tt_ + 1) * 128, :], w=[("xt", i)])
            if self.pn == "S":
                self.load(pt[i], self.d["pos"][tt_ * 128:(tt_ + 1) * 128, :], w=[("pt", i)])
                self.tt("dve", xt[i], xt[i], pt[i], ALU.add, r=[("xt", i), ("pt", i)], w=[("xt", i)])
            for hf in range(2):
                pbk = self.pb[(2 * tt_ + hf) % 4]
                key = ("pb", (2 * tt_ + hf) % 4)
                for q in range(4):
                    kt = hf * 4 + q
                    self.tr(pbk[:, q * 128:(q + 1) * 128], xt[i][:, kt * 128:(kt + 1) * 128], self.ident,
                            r=[("xt", i), "cst"], w=[key])
                self.cp("act" if hf == 0 else "dve", self.x[:, hf * 4:hf * 4 + 4, tt_ * 128:(tt_ + 1) * 128],
                        pbk[:, :].rearrange("p (q t) -> p q t", q=4), r=[key], w=[("x", tt_ // 4)])
        self.P.barrier()

    def store_x(self):
        T = self.T
        dst = self.d["ys"] if self.pn == "S" else self.d["yp"]
        ot = [self.carve(i * 1024, 1024) for i in range(2)]
        for tt_ in range(T // 128):
            i = tt_ % 2
            for hf in range(2):
                pbk = self.pb[(2 * tt_ + hf) % 4]
                key = ("pb", (2 * tt_ + hf) % 4)
                for q in range(4):
                    kt = hf * 4 + q
                    self.tr(pbk[:, q * 128:(q + 1) * 128], self.x[:, kt, tt_ * 128:(tt_ + 1) * 128], self.ident,
                            r=[("x", tt_ // 4), "cst"], w=[key])
                self.cp("act" if hf == 0 else "dve", ot[i][:, hf * 512:(hf + 1) * 512], pbk[:, :], r=[key], w=[("ot", i)])
            self.store(dst[tt_ * 128:(tt_ + 1) * 128, :], ot[i], r=[("ot", i)])

    def rstd_block(self, src3, srckeys, dstkey, scale, off):
        sq = self.carve(off, 4096, BF16).rearrange("p (k t) -> p k t", k=8)
        rs = self.carve(off + 2048, 512)
        self.act(sq, src3, AF.Square, r=srckeys, w=[("sq", off)])
        pbk, key = self.pb[7], ("pb", 7)
        for kt in range(8):
            self.mm(pbk[:, :], self.ones_bf, sq[:, kt, :], kt == 0, kt == 7, r=[("sq", off), "cbf"], w=[key])
        self.act(rs, pbk[:, :], AF.Sqrt, r=[key], w=[dstkey], bias=self.epsc[:, 0:1], scale=scale)
        self.P.op("dve", lambda e: e.reciprocal(out=rs, in_=rs), reads=[dstkey], writes=[dstkey])
        return rs

    def modnorm_block(self, l, which, b, hdst, hkey, off):
        xs = self.x[:, :, b * 512:(b + 1) * 512]
        rs = self.rstd_block(xs, [("x", b)], ("rs", off), 1.0 / D, off)
        tmp = self.carve(off + 2560, 512 * 2).rearrange("p (i t) -> p i t", i=2)
        sh_off = 0 if which == 0 else 24
        for kt in range(8):
            tk = ("mtmp", off, kt % 2)
            self.stt(tmp[:, kt % 2, :], self.x[:, kt, b * 512:(b + 1) * 512], self.mvec[:, l, self.j, 2 * which, kt:kt + 1], rs,
                     ALU.mult, ALU.mult, r=[("x", b), ("rs", off), ("mvec", l, self.j, 2 * which)], w=[tk])
            self.act(hdst(kt), tmp[:, kt % 2, :], AF.Identity, r=[tk, ("mod", l, self.j)], w=[hkey],
                     bias=self.mod[:, l, sh_off + kt, self.j:self.j + 1])

    def modnorm_all(self, l):
        for b in range(self.NB):
            self.modnorm_block(l, 0, b, lambda kt, b=b: self.h[:, kt, b * 512:(b + 1) * 512], ("h", b), off=(b % 2) * 3584)
        self.P.barrier()

    def proj(self, slab_dram, b, pbi, hkeys=None, hsrc=None):
        raise NotImplementedError

    def proj_full(self, slab_dram, consume, ncols=128):
        wv, wk = self.slab(slab_dram)
        w3 = wv.rearrange("p (k c) -> p k c", k=8)
        for b in range(self.NB):
            pbi = self.pb_rr
            self.pb_rr = (self.pb_rr + 1) % 4
            pbk, key = self.pb[pbi], ("pb", pbi)
            for kt in range(8):
                self.mm(pbk[0:ncols, :], w3[:, kt, 0:ncols], self.h[:, kt, b * 512:(b + 1) * 512], kt == 0, kt == 7,
                        r=[wk, ("h", b)], w=[key])
            consume(b, pbk[0:ncols, :], key)

    def out_proj(self, l):
        e = l // 2
        wd = self.d["wout_e"][e] if l % 2 == 0 else self.d["wout_o"][e]
        self.resid_update(l, 0, lambda ft: wd[ft], self.yb, 8)

    def resid_update(self, l, which, slab_of, src, nk, per_block=None):
        ob = self.carve(7168, 4096).rearrange("p (k t) -> p k t", k=8)
        for b in range(self.NB):
            for ft in range(8):
                wv, wk = self.slab(slab_of(ft))
                w3 = wv.rearrange("p (k c) -> p k c", k=8)
                pbi = ft % 4
                pbk, key = self.pb[pbi], ("pb", pbi)
                for kt in range(8):
                    self.mm(pbk[:, :], w3[:, kt, :], src[:, kt, b * 512:(b + 1) * 512], kt == 0, kt == 7,
                            r=[wk, ("yb", b)], w=[key])
                self.cp("act", ob[:, ft, :], pbk[:, :], r=[key], w=[("ob", ft)])
            self.post_block(l, which, b, ob, [("ob", ft) for ft in range(8)])

    def post_block(self, l, which, b, ob, obkeys):
        rs = self.rstd_block(ob, obkeys, ("rs", 0), 1.0 / D, 0)
        for kt in range(8):
            tk = ("ptmp", kt % 2)
            tmp = self.carve(2560 + (kt % 2) * 512, 512)
            self.stt(tmp, ob[:, kt, :], self.mvec[:, l, self.j, 2 * which + 1, kt:kt + 1], rs, ALU.mult, ALU.mult,
                     r=[("ob", kt), ("rs", 0), ("mvec", l, self.j, 2 * which + 1)], w=[tk])
            self.tt("pool", self.x[:, kt, b * 512:(b + 1) * 512], self.x[:, kt, b * 512:(b + 1) * 512], tmp, ALU.add,
                    r=[tk, ("x", b)], w=[("x", b)])

    def mlp(self, l):
        f1 = self.yb
        f1v = self.yb[:, :, :].rearrange("p k (a t) -> p (k a) t", t=512)
        hb = self.h[:, :, 0:512]
        ob = self.carve(7168, 4096).rearrange("p (k t) -> p k t", k=8)
        rl = [self.carve(3584 + i * 512, 512) for i in range(2)]
        for b in range(self.NB):
            self.modnorm_block(l, 1, b, lambda kt: self.h[:, kt, 0:512], ("h", 0), off=0)
            for jt in range(32):
                wv, wk = self.slab(self.d["wm1"][l, jt])
                w3 = wv.rearrange("p (k c) -> p k c", k=8)
                pbi = jt % 4
                pbk, key = self.pb[pbi], ("pb", pbi)
                for kt in range(8):
                    self.mm(pbk[:, :], w3[:, kt, :], self.h[:, kt, 0:512], kt == 0, kt == 7, r=[wk, ("h", 0)], w=[key])
                rk = ("rl", jt % 2)
                self.act(rl[jt % 2], pbk[:, :], AF.Relu, r=[key], w=[rk])
                self.tt("pool" if jt % 2 else "dve", f1v[:, jt, :], rl[jt % 2], rl[jt % 2], ALU.mult, r=[rk], w=[("f1", jt)])
            for ft in range(8):
                pbi = 4 + ft % 2
                pbk, key = self.pb[pbi], ("pb", pbi)
                for jg in range(4):
                    wv, wk = self.slab(self.d["wm2"][l, ft, jg])
                    w3 = wv.rearrange("p (k c) -> p k c", k=8)
                    for jj in range(8):
                        jt = jg * 8 + jj
                        self.mm(pbk[:, :], w3[:, jj, :], f1v[:, jt, :], jt == 0, jt == 31, r=[wk, ("f1", jt)], w=[key])
                self.cp("act", ob[:, ft, :], pbk[:, :], r=[key], w=[("ob", ft)])
            self.post_block(l, 1, b, ob, [("ob", ft) for ft in range(8)])

    def even_mixer(self, l):
        self.s5_mixer(l)
        self.P.barrier()
        self.gdn_mixer(l)

    def sincos(self, arg, sin_out, cos_out, k, r, key):
        TWO_PI = 6.283185307179586
        C1 = 6.28125
        C2 = TWO_PI - C1
        MAGIC = 12582912.0
        PI = 3.141592653589793
        kk, rk = (key, "k"), (key, "r")
        self.ts("dve", k, arg, 1.0 / TWO_PI, ALU.mult, r=[(key, "arg")], w=[kk], s2=MAGIC, op1=ALU.add)
        self.ts("dve", k, k, MAGIC, ALU.subtract, r=[kk], w=[kk])
        self.stt(r, k, -C1, arg, ALU.mult, ALU.add, r=[kk, (key, "arg")], w=[rk])
        self.stt(r, k, -C2, r, ALU.mult, ALU.add, r=[kk, rk], w=[rk])

        def wrap(y):
            self.ts("dve", k, y, PI, ALU.is_gt, r=[rk], w=[kk], s2=-TWO_PI, op1=ALU.mult)
            self.tt("dve", y, y, k, ALU.add, r=[rk, kk], w=[rk])
            self.ts("dve", k, y, -PI, ALU.is_lt, r=[rk], w=[kk], s2=TWO_PI, op1=ALU.mult)
            self.tt("dve", y, y, k, ALU.add, r=[rk, kk], w=[rk])
        wrap(r)
        self.act(sin_out, r, AF.Sin, r=[rk], w=[(key, "sin")])
        self.ts("dve", r, r, PI / 2, ALU.add, r=[rk], w=[rk])
        wrap(r)
        self.act(cos_out, r, AF.Sin, r=[rk], w=[(key, "cos")])

    def s5_params(self, e):
        V, R = self.vecs, self.s5p
        lr = V[:, vcol("s5_lam_re%d" % e, 0, 32)]
        li = V[:, vcol("s5_lam_im%d" % e, 0, 32)]
        K = "s5p"
        rw = dict(r=[K, "vecs"], w=[K])
        self.act(R[:, 7, :], V[:, vcol("s5_dt%d" % e, 0, 32)], AF.Exp, **rw)
        self.tt("dve", R[:, 6, :], lr, R[:, 7, :], ALU.mult, **rw)
        self.tt("dve", R[:, 0, :], li, R[:, 7, :], ALU.mult, **rw)
        self.act(R[:, 1, :], R[:, 6, :], AF.Exp, **rw)
        self.P.op("dve", lambda en: en.tensor_copy(out=R[:, 15, :], in_=R[:, 0, :]), reads=[K], writes=[("sc1", "arg")])
        self.sincos(R[:, 15, :], R[:, 10, :], R[:, 9, :], R[:, 7, :], R[:, 8, :], "sc1")
        self.ts("dve", R[:, 15, :], R[:, 0, :], float(TC), ALU.mult, r=[K, ("sc1", "sin"), ("sc1", "cos")], w=[("sc2", "arg")])
        self.sincos(R[:, 15, :], R[:, 14, :], R[:, 13, :], R[:, 7, :], R[:, 8, :], "sc2")
        rw = dict(r=[K, "vecs", ("sc1", "sin"), ("sc1", "cos"), ("sc2", "sin"), ("sc2", "cos")], w=[K])
        self.tt("dve", R[:, 2, :], R[:, 1, :], R[:, 9, :], ALU.mult, **rw)
        self.tt("dve", R[:, 3, :], R[:, 1, :], R[:, 10, :], ALU.mult, **rw)
        self.tt("dve", R[:, 6, :], lr, lr, ALU.mult, **rw)
        self.tt("dve", R[:, 7, :], li, li, ALU.mult, **rw)
        self.tt("dve", R[:, 6, :], R[:, 6, :], R[:, 7, :], ALU.add, **rw)
        self.P.op("dve", lambda en: en.reciprocal(out=R[:, 6, :], in_=R[:, 6, :]), reads=[K], writes=[K])
        self.ts("dve", R[:, 7, :], R[:, 2, :], -1.0, ALU.add, **rw)
        self.tt("dve", R[:, 4, :], R[:, 7, :], lr, ALU.mult, **rw)
        self.tt("dve", R[:, 8, :], R[:, 3, :], li, ALU.mult, **rw)
        self.tt("dve", R[:, 4, :], R[:, 4, :], R[:, 8, :], ALU.add, **rw)
        self.tt("dve", R[:, 4, :], R[:, 4, :], R[:, 6, :], ALU.mult, **rw)
        self.tt("dve", R[:, 5, :], R[:, 3, :], lr, ALU.mult, **rw)
        self.tt("dve", R[:, 8, :], R[:, 7, :], li, ALU.mult, **rw)
        self.tt("dve", R[:, 5, :], R[:, 5, :], R[:, 8, :], ALU.subtract, **rw)
        self.tt("dve", R[:, 5, :], R[:, 5, :], R[:, 6, :], ALU.mult, **rw)
        self.tt("dve", R[:, 11, :], R[:, 4, :], R[:, 9, :], ALU.mult, **rw)
        self.tt("dve", R[:, 8, :], R[:, 5, :], R[:, 10, :], ALU.mult, **rw)
        self.tt("dve", R[:, 11, :], R[:, 11, :], R[:, 8, :], ALU.add, **rw)
        self.tt("dve", R[:, 12, :], R[:, 5, :], R[:, 9, :], ALU.mult, **rw)
        self.tt("dve", R[:, 8, :], R[:, 4, :], R[:, 10, :], ALU.mult, **rw)
        self.tt("dve", R[:, 12, :], R[:, 12, :], R[:, 8, :], ALU.subtract, **rw)
        if self.pn == "S":
            H = self.s5h
            self.load(H[:], self.d["s5h0"][:, e], w=["s5h"])
            self.tt("dve", R[:, 6, :], R[:, 4, :], R[:, 4, :], ALU.mult, **rw)
            self.tt("dve", R[:, 7, :], R[:, 5, :], R[:, 5, :], ALU.mult, **rw)
            self.tt("dve", R[:, 6, :], R[:, 6, :], R[:, 7, :], ALU.add, **rw)
            self.P.op("dve", lambda en: en.reciprocal(out=R[:, 6, :], in_=R[:, 6, :]), reads=[K], writes=[K])
            self.tt("dve", R[:, 7, :], R[:, 9, :], R[:, 4, :], ALU.mult, **rw)
            self.tt("dve", R[:, 15, :], R[:, 10, :], R[:, 5, :], ALU.mult, **rw)
            self.tt("dve", R[:, 7, :], R[:, 7, :], R[:, 15, :], ALU.add, **rw)
            self.tt("dve", R[:, 7, :], R[:, 7, :], R[:, 6, :], ALU.mult, **rw)
            self.tt("dve", R[:, 8, :], R[:, 10, :], R[:, 4, :], ALU.mult, **rw)
            self.tt("dve", R[:, 15, :], R[:, 9, :], R[:, 5, :], ALU.mult, **rw)
            self.tt("dve", R[:, 8, :], R[:, 8, :], R[:, 15, :], ALU.subtract, **rw)
            self.tt("dve", R[:, 8, :], R[:, 8, :], R[:, 6, :], ALU.mult, **rw)
            Gr = R[:, 7, :].rearrange("p (d s) -> p d s", d=2)
            Gi = R[:, 8, :].rearrange("p (d s) -> p d s", d=2)
            t6 = R[:, 6, :].rearrange("p (d s) -> p d s", d=2)
            t15 = R[:, 15, :].rearrange("p (d s) -> p d s", d=2)
            rw2 = dict(r=[K, "s5h"], w=[K])
            self.tt("dve", t6, Gr, H[:, :, 0, :], ALU.mult, **rw2)
            self.tt("dve", t15, Gi, H[:, :, 1, :], ALU.mult, **rw2)
            self.tt("dve", t6, t6, t15, ALU.subtract, **rw2)
            self.tt("dve", t15, Gr, H[:, :, 1, :], ALU.mult, **rw2)
            self.tt("dve", Gr, Gi, H[:, :, 0, :], ALU.mult, **rw2)
            self.tt("dve", H[:, :, 1, :], t15, Gr, ALU.add, r=[K, "s5h"], w=["s5h"])
            self.cp("dve", H[:, :, 0, :], t6, r=[K, "s5h"], w=["s5h"])

    def s5_mixer(self, l):
        e = l // 2
        T, L, nseq, NB = self.T, self.L, self.nseq, self.NB
        V, R = self.vecs, self.s5p
        nch = T // TC
        cps = L // TC
        self.s5_params(e)
        self.P.barrier()
        A = self.carve
        costab = A(0, 1024).rearrange("p (i j) -> p i j", i=8)
        sintab = A(1024, 1024).rearrange("p (i j) -> p i j", i=8)
        rhotab = A(2048, 1024).rearrange("p (i j) -> p i j", i=8)
        argt = A(3072, 1024)
        kt_ = A(4096, 1024)
        rt_ = A(5120, 1024)
        slot = []
        for i in range(4):
            base = 3072 + i * 1280
            slot.append(dict(t=[A(base + q * 128, 128) for q in range(6)],
                             g2=A(base + 768, 256).rearrange("p (r t) -> p r t", r=2),
                             pr=[A(base + 1024 + q * 64, 128, BF16) for q in range(4)]))
        u_bf = A(8192, 2048, BF16)
        y_acc = A(9216, 2048)
        ctp = A(11264, 3072, BF16).rearrange("p (a d r c) -> p a d r c", a=4, d=2, r=3)
        bT = A(12800, 1024, BF16).rearrange("p (a r c) -> p a r c", a=4, r=2)
        cstage = A(4352, 1024).rearrange("p (a r c) -> p a r c", a=4, r=2)
        bstage = A(3072, 1024).rearrange("p (a r c) -> p a r c", a=4, r=2)
        cin = self.lcst[:, 0:8].rearrange("p (a r) -> p a r", a=4)
        ctmp = self.lcst[:, 8:16].rearrange("p (a r) -> p a r", a=4)
        eis = self.lh0[:, :, :].rearrange("p d (a r) -> p d a r", a=4)
        ep = [A(3072 + i * 512, 512) for i in range(2)]
        for ut in range(4):
            self.load(cstage, self.d["s5c"][e][:, 4 * ut:4 * ut + 4], w=["cstage"])
            self.load(bstage, self.d["s5b"][e][:, ut], w=["bstage"])
            self.cp("pool", bT, bstage, r=["bstage"], w=["bT"])
            for dd in range(2):
                for sl in range(4):
                    col = dd * 16 + 4 * ut + sl
                    fr, fi = R[:, 4, col:col + 1], R[:, 5, col:col + 1]
                    cre, cim = cstage[:, sl, 0, :], cstage[:, sl, 1, :]
                    tA, tB = slot[3]["t"][0], slot[3]["t"][1]
                    self.ts("pool", tA, cim, fi, ALU.mult, r=["cstage", "s5p"], w=["tA"])
                    self.stt(ctp[:, sl, dd, 0, :], cre, fr, tA, ALU.mult, ALU.subtract, r=["cstage", "s5p", "tA"], w=["ctp"])
                    self.ts("pool", tB, cim, fr, ALU.mult, r=["cstage", "s5p"], w=["tB"])
                    self.stt(tB, cre, fi, tB, ALU.mult, ALU.add, r=["cstage", "s5p", "tB"], w=["tB"])
                    self.ts("dve", ctp[:, sl, dd, 2, :], tB, -1.0, ALU.mult, r=["tB"], w=["ctp"])
                    self.ts("dve", ctp[:, sl, dd, 1, :], ctp[:, sl, dd, 0, :], -1.0, ALU.mult, r=["ctp"], w=["ctp"])
                    Ei_c = R[:, 14, col:col + 1]
                    self.ts("dve", eis[:, dd, sl, 0:1], Ei_c, -1.0, ALU.mult, r=["s5p"], w=["eis"])
                    self.cp("dve", eis[:, dd, sl, 1:2], Ei_c, r=["s5p"], w=["eis"])
            self.P.barrier()
            for dd in range(2):
                for sl in range(4):
                    col = dd * 16 + 4 * ut + sl
                    i = dd * 4 + sl
                    self.ts("dve", argt[:, i * 128:(i + 1) * 128], self.jidx, R[:, 0, col:col + 1], ALU.mult,
                            r=["cst", "s5p"], w=[("tab", "arg")])
                    self.ts("pool", rhotab[:, i, :], self.ones, R[:, 1, col:col + 1], ALU.mult, r=["cst", "s5p"], w=["rhotab"])
            self.sincos(argt, sintab[:, :, :].rearrange("p i j -> p (i j)"), costab[:, :, :].rearrange("p i j -> p (i j)"), kt_, rt_, "tab")
            def cons(b, ps, key):
                self.cp("act", u_bf[:, b * 512:(b + 1) * 512], ps, r=[key], w=["u_bf"])
            self.proj_full(self.d["win_e"][e, ut], cons)
            self.P.barrier()
            LA = 2
            for dd in range(2):
                order = list(range(nch)) if dd == 0 else list(range(nch - 1, -1, -1))
                tabk = [("tab", "sin"), ("tab", "cos")]

                def stageA(ci, c, sl, dd=dd):
                    tk = slice(c * TC, (c + 1) * TC)
                    if sl == 0:
                        for s2_ in range(4):
                            bu = self.pb[s2_][:, 0:256].rearrange("p (r t) -> p r t", r=2)
                            for ri in range(2):
                                self.mm(bu[:, ri, :], bT[:, s2_, ri, :], u_bf[:, tk], True, True,
                                        r=["bT", "u_bf"], w=[("bu", s2_, ri)])
                    reg = sl
                    i = dd * 4 + sl
                    S = slot[sl]
                    sk = lambda n, sl=sl: ("slot", sl, n)
                    bu = self.pb[reg][:, 0:256].rearrange("p (r t) -> p r t", r=2)
                    bre_p, bim_p = bu[:, 0, :], bu[:, 1, :]
                    if dd == 1:
                        bre_p, bim_p = self.rev(bre_p), self.rev(bim_p)
                    t1, t2, t3, t4, bre, bim = S["t"]
                    cs, sn = costab[:, i, :], sintab[:, i, :]
                    self.tt("dve", t1, bre_p, cs, ALU.mult, r=[("bu", reg, 0)] + tabk, w=[sk("t1")])
                    self.tt("dve", t2, bim_p, sn, ALU.mult, r=[("bu", reg, 1)] + tabk, w=[sk("t2")])
                    self.tt("pool", bre, t1, t2, ALU.add, r=[sk("t1"), sk("t2")], w=[sk("bre")])
                    self.tt("dve", t3, bim_p, cs, ALU.mult, r=[("bu", reg, 1)] + tabk, w=[sk("t3")])
                    self.tt("dve", t4, bre_p, sn, ALU.mult, r=[("bu", reg, 0)] + tabk, w=[sk("t4")])
                    self.tt("pool", bim, t3, t4, ALU.subtract, r=[sk("t3"), sk("t4")], w=[sk("bim")])

                def stageB(ci, c, sl, dd=dd):
                    tk = slice(c * TC, (c + 1) * TC)
                    seq = c // cps
                    first = (c % cps == 0) if dd == 0 else (c % cps == cps - 1)
                    last = (c % cps == cps - 1) if dd == 0 else (c % cps == 0)
                    ypk = ("yps", ci % 2)
                    yps = self.pb[4 + ci % 2][:, 0:TC]
                    i = dd * 4 + sl
                    col = dd * 16 + 4 * ut + sl
                    S = slot[sl]
                    sk = lambda n, sl=sl: ("slot", sl, n)
                    t1, t2, t3, t4, bre, bim = S["t"]
                    g2 = S["g2"]
                    gre, gim = g2[:, 0, :], g2[:, 1, :]
                    cs, sn, rh = costab[:, i, :], sintab[:, i, :], rhotab[:, i, :]
                    if first:
                        if self.pn == "S":
                            self.cp("pool", cin[:, sl, :], self.s5h[:, dd, :, 4 * ut + sl], r=["s5h"], w=[("cin", sl)])
                        else:
                            self.memset("pool", cin[:, sl, :], 0.0, w=[("cin", sl)])
                    for (g_, b_, ri) in ((gre, bre, 0), (gim, bim, 1)):
                        self.P.op("dve", lambda en, g_=g_, b_=b_, rh=rh, ini=cin[:, sl, ri:ri + 1]: en.tensor_tensor_scan(
                            out=g_, data0=rh, data1=b_, initial=ini, op0=ALU.mult, op1=ALU.add),
                            reads=[sk("bre" if ri == 0 else "bim"), "rhotab", ("cin", sl)], writes=[sk("g2")])
                    Er = R[:, 13, col:col + 1]
                    gl = g2[:, :, TC - 1]
                    (gps, gpn), (gfs, gfn) = gl.ap
                    gl_sw = AP(gl.tensor, gl.offset + gfs, [[gps, gpn], [-gfs, 2]])
                    gk = [sk("g2"), "s5p", "eis"]
                    self.tt("dve", ctmp[:, sl, :], gl_sw, eis[:, dd, sl, :], ALU.mult, r=gk, w=[("ctmp", sl)])
                    self.stt(cin[:, sl, :], gl, Er, ctmp[:, sl, :], ALU.mult, ALU.add, r=gk + [("ctmp", sl)], w=[("cin", sl)])
                    if last and self.pn == "P":
                        F2r, F2i = R[:, 11, col:col + 1], R[:, 12, col:col + 1]
                        s_ = 4 * ut + sl
                        o_re = self.fin5[:, seq, dd, 0, s_:s_ + 1]
                        o_im = self.fin5[:, seq, dd, 1, s_:s_ + 1]
                        ck = [("cin", sl), "s5p"]
                        self.ts("pool", ctmp[:, sl, 0:1], cin[:, sl, 1:2], F2i, ALU.mult, r=ck, w=[("ctmp", sl)])
                        self.ts("pool", ctmp[:, sl, 1:2], cin[:, sl, 0:1], F2i, ALU.mult, r=ck, w=[("ctmp", sl)])
                        self.stt(o_re, cin[:, sl, 0:1], F2r, ctmp[:, sl, 0:1], ALU.mult, ALU.subtract, r=ck + [("ctmp", sl)], w=["fin5"])
                        self.stt(o_im, cin[:, sl, 1:2], F2r, ctmp[:, sl, 1:2], ALU.mult, ALU.add, r=ck + [("ctmp", sl)], w=["fin5"])
                    prs = S["pr"]
                    prs_o = prs if dd == 0 else [self.rev(p_) for p_ in prs]
                    for q, (gg, tab, var) in enumerate(((gre, cs, 0), (gim, sn, 1), (gre, sn, 2), (gim, cs, 2))):
                        self.tt("pool", prs_o[q], gg, tab, ALU.mult, r=[sk("g2")] + tabk, w=[sk("pr%d" % q)])
                        self.mm(yps, ctp[:, sl, dd, var, :], prs[q], sl == 0 and q == 0, sl == 3 and q == 3,
                                r=["ctp", sk("pr%d" % q)], w=[ypk])
                    if sl == 3:
                        if dd == 0:
                            self.cp("act", y_acc[:, tk], yps, r=[ypk], w=[("y_acc", c)])
                        else:
                            self.tt("dve", y_acc[:, tk], y_acc[:, tk], yps, ALU.add, r=[ypk, ("y_acc", c)], w=[("y_acc", c)])

                units = [(ci, c, sl) for ci, c in enumerate(order) for sl in range(4)]
                for step in range(len(units) + LA):
                    if step < len(units):
                        stageA(*units[step])
                    if step - LA >= 0:
                        stageB(*units[step - LA])
            self.P.barrier()
            yk = [("y_acc", c) for c in range(nch)]

            def cons_z(b, ps, key, ut=ut):
                bs = slice(b * 512, (b + 1) * 512)
                self.act(ep[0], ps, AF.Sigmoid, r=[key], w=["ep0"])
                self.stt(ep[1], u_bf[:, bs], V[:, vcol("s5_d%d" % e, ut, 1)], y_acc[:, bs], ALU.mult, ALU.add,
                         r=["u_bf", "vecs"] + yk, w=["ep1"])
                self.act(ep[1], ep[1], AF.Gelu_apprx_tanh, r=["ep1"], w=["ep1"])
                self.tt("dve", self.yb[:, ut, bs], ep[1], ep[0], ALU.mult, r=["ep0", "ep1"], w=[("yb", b)])
            self.proj_full(self.d["win_e"][e, 4 + ut], cons_z)
            self.P.barrier()
        if self.pn == "P":
            for seq in range(4):
                for dd in range(2):
                    for ri in range(2):
                        self.store(self.d["o_s5"][seq, e, dd, ri], self.fin5[:, seq, dd, ri, :], r=["fin5"])

    def bmid(self, ap2, n):
        (ps, pn_), (fs, fn) = ap2.ap
        return AP(ap2.tensor, ap2.offset, [[ps, pn_], [0, n], [fs, fn]])

    def gdn_mixer(self, l):
        e = l // 2
        T, L, nseq, NB = self.T, self.L, self.nseq, self.NB
        V = self.vecs
        A = self.carve
        GC = 128
        NLV = 7
        nchk = T // GC
        cps = L // GC
        tb = 10240
        abtok = A(tb, 256).rearrange("p (n c) -> p n c", c=16)[:, 0:nchk, :]
        def tok8(off):
            return A(tb + off, 128).rearrange("p (n c) -> p n c", c=8)[:, 0:nchk, :]
        gtok, btok, gam, beg, kd, eglast = tok8(256), tok8(384), tok8(512), tok8(640), tok8(768), tok8(896)
        nexp = A(tb + 1024, 8)
        wv, wk = self.slab(self.d["win_e"][e, 24])
        w3 = wv.rearrange("p (k c) -> p k c", k=8)
        pq = self.pb[6]
        for n in range(nchk):
            for kt in range(8):
                self.mm(pq[:, n * 16:(n + 1) * 16], self.h[:, kt, n * GC:(n + 1) * GC], w3[:, kt, 0:16], kt == 0, kt == 7,
                        r=[wk, ("h", n // 4)], w=["pq"])
        self.cp("act", abtok, pq[:, 0:nchk * 16].rearrange("p (n c) -> p n c", c=16), r=["pq"], w=["abtok"])
        K = "gtokk"
        rw = dict(r=["abtok", "vecs", K], w=[K])
        self.tt("dve", gtok, abtok[:, :, 0:8], self.bmid(V[:, vcol("gdn_dtb%d" % e, 0, 8)], nchk), ALU.add, **rw)
        self.act(gtok, gtok, AF.Exp, **rw)
        self.act(gtok, gtok, AF.Ln, bias=self.onec[:, 0:1], **rw)
        self.act(nexp, V[:, vcol("gdn_alog%d" % e, 0, 8)], AF.Exp, **rw)
        self.ts("dve", nexp, nexp, -1.0, ALU.mult, **rw)
        self.tt("dve", gtok, gtok, self.bmid(nexp, nchk), ALU.mult, **rw)
        self.act(btok, abtok[:, :, 8:16], AF.Sigmoid, **rw)
        self.P.barrier()
        gdir = A(tb + 1032, 128)
        for dd, tri in ((0, self.m_le), (1, self.m_ge)):
            gd = gdir[:, dd * 64:dd * 64 + nchk * 4]
            self.cp("dve", gd.rearrange("p (n c) -> p n c", c=4), gtok[:, :, dd * 4:(dd + 1) * 4], r=[K], w=[("gdir", dd)])
            self.mm(pq[:, dd * 64:dd * 64 + nchk * 4], tri, gd, True, True, r=[("gdir", dd), "cst"], w=["pq"])
        g2d = A(tb + 256, nchk * 8)
        pl2 = pq[:, 256:256 + nchk * 8]
        self.mm(pl2, self.ones, g2d, True, True, r=[K, "cst"], w=["pq2"])
        pl = pl2.rearrange("p (n c) -> p n c", c=8)
        for dd in range(2):
            self.cp("act", gam[:, :, dd * 4:(dd + 1) * 4], pq[:, dd * 64:dd * 64 + nchk * 4].rearrange("p (n c) -> p n c", c=4),
                    r=["pq"], w=[K])
        self.act(eglast, pl, AF.Exp, r=["pq2"], w=[K])
        self.tt("dve", kd, pl, gam, ALU.subtract, r=["pq2", K], w=[K])
        self.act(kd, kd, AF.Exp, **rw)
        self.act(beg, gam, AF.Exp, **rw)
        self.tt("dve", beg, beg, btok, ALU.mult, **rw)
        self.P.barrier()
        bmk = A(11400, 2 * NLV * 128)
        self.load(bmk, self.d["bmk"], w=["bmk"])
        bml = bmk[:, 0:NLV * 128].rearrange("p (m c) -> p m c", m=NLV)
        bmu = bmk[:, NLV * 128:2 * NLV * 128].rearrange("p (m c) -> p m c", m=NLV)
        q_bf, k_bf, v_bf = A(0, 2048, BF16), A(1024, 2048, BF16), A(2048, 2048, BF16)
        craw, cout = A(3072, 2048), A(5120, 2048)
        o_acc = A(3072, 2048)
        I128 = self.ident
        for hd in range(4):
            for which, (dst, tile0) in enumerate(((q_bf, 8), (k_bf, 12), (v_bf, 16))):
                ci = which * 4 + hd

                def cons(b, ps, key):
                    self.cp("act", craw[:, b * 512:(b + 1) * 512], ps, r=[key], w=["craw"])
                self.proj_full(self.d["win_e"][e, tile0 + hd], cons)
                cw = lambda j, ci=ci: V[:, vcol("gdn_conv_w%d" % e, j * 12 + ci, 1)]
                co = cout[:, 0:T]
                self.act(co, craw[:, 0:T], AF.Identity, r=["craw", "vecs"], w=["cout"],
                         bias=V[:, vcol("gdn_conv_b%d" % e, ci, 1)], scale=cw(1))
                r3 = craw[:, 0:T].rearrange("p (s l) -> p s l", s=nseq)
                x3 = co.rearrange("p (s l) -> p s l", s=nseq)
                kk = dict(r=["craw", "cout", "vecs"], w=["cout"])
                self.stt(x3[:, :, 1:L], r3[:, :, 0:L - 1], cw(0), x3[:, :, 1:L], ALU.mult, ALU.add, **kk)
                self.stt(x3[:, :, 0:L - 1], r3[:, :, 1:L], cw(2), x3[:, :, 0:L - 1], ALU.mult, ALU.add, **kk)
                self.stt(x3[:, :, 0:L - 2], r3[:, :, 2:L], cw(3), x3[:, :, 0:L - 2], ALU.mult, ALU.add, **kk)
                self.act(co, co, AF.Silu, r=["cout"], w=["cout"])
                if which == 2:
                    self.cp("pool", dst[:, 0:T], co, r=["cout"], w=["qkv"])
                else:
                    for b in range(NB):
                        bs = slice(b * 512, (b + 1) * 512)
                        sq = A(7168, 512, BF16)
                        rs = A(7424, 512)
                        self.act(sq, cout[:, bs], AF.Square, r=["cout"], w=["gsq"])
                        self.mm(self.pb[7][:, :], self.ones_bf, sq, True, True, r=["gsq", "cbf"], w=[("pb", 7)])
                        self.act(rs, self.pb[7][:, :], AF.Sqrt, r=[("pb", 7)], w=["grs"], bias=self.epsc[:, 0:1])
                        self.P.op("dve", lambda en, rs=rs: en.reciprocal(out=rs, in_=rs), reads=["grs"], writes=["grs"])
                        self.stt(dst[:, bs], cout[:, bs], (128.0 ** -0.5) if which == 0 else 1.0, rs, ALU.mult, ALU.mult,
                                 r=["cout", "grs"], w=["qkv"])
            self.P.barrier()
            self.memset("pool", o_acc[:, 0:T], 0.0, w=[("o_acc", n) for n in range(nchk)])
            ST = []
            for st in range(2):
                base = 5120 + st * 2560
                o = [base]

                def nx(n, dt=F32, o=o):
                    ap = A(o[0], n, dt)
                    o[0] += n if dt == F32 else (n + 1) // 2
                    return ap
                ST.append(dict(gtri=nx(128), dce=nx(128), dec=nx(128), A0=nx(128), Ncm=nx(128), B0=nx(128), Nem=nx(128),
                               Nem2=nx(128),
                               T=nx(128), TT=nx(128), M1=nx(128), qk=nx(128, BF16), X0=nx(256), X1=nx(256),
                               wmt=nx(128, BF16), vn=nx(128, BF16), kdb=nx(128, BF16), eg=nx(128), qg=nx(128, BF16),
                               S=nx(128), Sb=nx(128, BF16)))
                assert o[0] - base <= 2560

            def unit(st, n, hd=hd):
                dd = st
                Sx = ST[st]
                col = dd * 4 + hd
                tk = slice(n * GC, (n + 1) * GC)
                bA, bB, bC, bD = self.pb[st * 4], self.pb[st * 4 + 1], self.pb[st * 4 + 2], self.pb[st * 4 + 3]
                k_ = lambda name: ("g", st, name)
                tri = self.m_le if dd == 0 else self.m_ge
                negs = self.m_ngt if dd == 0 else self.m_nlt
                incl = self.m_le if dd == 0 else self.m_ge
                g_c, b_c, ga_c = gtok[:, n, col:col + 1], btok[:, n, col:col + 1], gam[:, n, col:col + 1]
                beg_c, kd_c, egl_c = beg[:, n, col:col + 1], kd[:, n, col:col + 1], eglast[:, n, col:col + 1]
                raw_ce, rawq_ec, Gam, ktok = bA[:, 0:128], bA[:, 128:256], bA[:, 256:384], bA[:, 384:512]
                vtok, xp, wmt_p = bB[:, 0:128], bB[:, 128:384], bB[:, 384:512]
                vn_p, ot_p, s_p, b0_p = bC[:, 0:128], bC[:, 128:256], bC[:, 256:384], bC[:, 384:512]
                m1_p, m2_p, m2t_p = bD[:, 0:128], bD[:, 128:256], bD[:, 256:384]
                A0, B0, Ncm, Nem, Tm, TTm, M1s = Sx["A0"], Sx["B0"], Sx["Ncm"], Sx["Nem"], Sx["T"], Sx["TT"], Sx["M1"]
                X0, X1 = Sx["X0"], Sx["X1"]
                mce = lambda m: (bml if dd == 0 else bmu)[:, m, :]
                mec = lambda m: (bmu if dd == 0 else bml)[:, m, :]
                stages = []

                def s1():
                    self.ts("pool", Sx["gtri"], tri, g_c, ALU.mult, r=["cst", K], w=[k_("gtri")])
                    self.mm(Gam, self.ones, Sx["gtri"], True, True, r=["cst", k_("gtri")], w=[k_("Gam")])
                    self.mm(raw_ce, k_bf[:, tk], k_bf[:, tk], True, True, r=["qkv"], w=[k_("raw_ce")])
                    self.mm(rawq_ec, k_bf[:, tk], q_bf[:, tk], True, True, r=["qkv"], w=[k_("rawq")])
                    self.mm(ktok, k_bf[:, tk], self.ident_bf, True, True, r=["qkv", "cbf"], w=[k_("ktok")])
                    self.mm(vtok, v_bf[:, tk], self.ident_bf, True, True, r=["qkv", "cbf"], w=[k_("vtok")])
                stages.append(s1)

                def s2():
                    self.ts("dve", Sx["dce"], Gam, ga_c, ALU.subtract, r=[k_("Gam"), K], w=[k_("dce")], s2=0.0, op1=ALU.max)
                    self.act(Sx["dce"], Sx["dce"], AF.Exp, r=[k_("dce")], w=[k_("dce")], scale=-1.0)
                    self.ts("dve", Sx["dec"], Gam, ga_c, ALU.subtract, r=[k_("Gam"), K], w=[k_("dec")], s2=0.0, op1=ALU.min)
                    self.act(Sx["dec"], Sx["dec"], AF.Exp, r=[k_("dec")], w=[k_("dec")])
                    self.tt("pool", Sx["dce"], Sx["dce"], negs, ALU.mult, r=[k_("dce"), "cst"], w=[k_("dce")])
                    self.tt("pool", Sx["dec"], Sx["dec"], incl, ALU.mult, r=[k_("dec"), "cst"], w=[k_("dec")])
                    self.act(Sx["eg"], Gam, AF.Exp, r=[k_("Gam")], w=[k_("eg")])
                    self.tt("dve", Sx["qg"], q_bf[:, tk], Sx["eg"], ALU.mult, r=["qkv", k_("eg")], w=[k_("qg")])
                stages.append(s2)

                def s3():
                    self.stt(A0, raw_ce, b_c, Sx["dce"], ALU.mult, ALU.mult, r=[k_("raw_ce"), K, k_("dce")], w=[k_("A0")])
                    self.tt("dve", Sx["qk"], rawq_ec, Sx["dec"], ALU.mult, r=[k_("rawq"), k_("dec")], w=[k_("qk")])
                    self.tr(b0_p, A0, I128, r=[k_("A0"), "cst"], w=[k_("b0p")])
                    self.cp("act", B0, b0_p, r=[k_("b0p")], w=[k_("B0")])
                    self.act(X0[:, 0:128], vtok, AF.Identity, r=[k_("vtok"), K], w=[k_("X0")], scale=b_c)
                    self.act(X0[:, 128:256], ktok, AF.Identity, r=[k_("ktok"), K], w=[k_("X0")], scale=beg_c)
                    self.act(Sx["kdb"], ktok, AF.Identity, r=[k_("ktok"), K], w=[k_("kdb")], scale=kd_c)
                stages.append(s3)

                NemB = [Nem, Sx["Nem2"]]

                def sl0():
                    self.tt("pool", Ncm, A0, mce(0), ALU.mult, r=[k_("A0"), "bmk"], w=[k_("Ncm")])
                    self.tt("pool", NemB[0], B0, mec(0), ALU.mult, r=[k_("B0"), "bmk"], w=[k_("Nem0")])
                    self.tt("dve", Tm, Ncm, I128, ALU.add, r=[k_("Ncm"), "cst"], w=[k_("T")])
                    self.tt("dve", TTm, NemB[0], I128, ALU.add, r=[k_("Nem0"), "cst"], w=[k_("TT")])
                    self.tt("pool", NemB[1], B0, mec(1), ALU.mult, r=[k_("B0"), "bmk"], w=[k_("Nem1")])
                stages.append(sl0)
                for m in range(1, NLV):
                    def sla(m=m):
                        cur = NemB[m % 2]
                        self.mm(m1_p, cur, Tm, True, True, r=[k_("Nem%d" % (m % 2)), k_("T")], w=[k_("m1p")])
                        self.cp("act", M1s, m1_p, r=[k_("m1p")], w=[k_("M1")])
                        if m + 1 < NLV:
                            self.tt("pool", NemB[(m + 1) % 2], B0, mec(m + 1), ALU.mult, r=[k_("B0"), "bmk"],
                                    w=[k_("Nem%d" % ((m + 1) % 2))])

                    def slb(m=m):
                        self.mm(m2_p, TTm, M1s, True, True, r=[k_("TT"), k_("M1")], w=[k_("m2p")])
                        self.mm(m2t_p, M1s, TTm, True, True, r=[k_("TT"), k_("M1")], w=[k_("m2tp")])
                        self.tt("dve", Tm, Tm, m2_p, ALU.add, r=[k_("T"), k_("m2p")], w=[k_("T")])
                        self.tt("dve", TTm, TTm, m2t_p, ALU.add, r=[k_("TT"), k_("m2tp")], w=[k_("TT")])
                    stages.append(sla)
                    stages.append(slb)

                def sx():
                    self.mm(xp, TTm, X0, True, True, r=[k_("TT"), k_("X0")], w=[k_("xp")])
                    self.cp("act", X1, xp, r=[k_("xp")], w=[k_("X1")])
                stages.append(sx)

                def s4():
                    self.tr(wmt_p, X1[:, 128:256], I128, r=[k_("X1"), "cst"], w=[k_("wmtp")])
                    self.cp("act", Sx["wmt"], wmt_p, r=[k_("wmtp")], w=[k_("wmt")])
                    self.mm(vn_p, Sx["wmt"], Sx["Sb"], True, True, r=[k_("wmt"), k_("Sb")], w=[k_("vnp")])
                    self.tt("dve", Sx["vn"], X1[:, 0:128], vn_p, ALU.subtract, r=[k_("X1"), k_("vnp")], w=[k_("vn")])
                stages.append(s4)

                def s5():
                    self.mm(ot_p, Sx["Sb"], Sx["qg"], True, False, r=[k_("Sb"), k_("qg")], w=[k_("otp")])
                    self.mm(ot_p, Sx["vn"], Sx["qk"], False, True, r=[k_("vn"), k_("qk")], w=[k_("otp")])
                    self.mm(s_p, Sx["kdb"], Sx["vn"], True, True, r=[k_("kdb"), k_("vn")], w=[k_("sp")])
                    self.tt("dve", o_acc[:, tk], o_acc[:, tk], ot_p, ALU.add, r=[k_("otp"), ("o_acc", n)], w=[("o_acc", n)])
                    self.stt(Sx["S"], Sx["S"], egl_c, s_p, ALU.mult, ALU.add, r=[k_("S"), K, k_("sp")], w=[k_("S")])
                    self.cp("pool", Sx["Sb"], Sx["S"], r=[k_("S")], w=[k_("Sb")])
                stages.append(s5)
                return stages

            for seq in range(nseq):
                for st in range(2):
                    Sx = ST[st]
                    if self.pn == "S":
                        self.load(Sx["S"], self.d["dl0"][:, e, st, hd, :], w=[("g", st, "S")])
                    else:
                        self.memset("pool", Sx["S"], 0.0, w=[("g", st, "S")])
                    self.cp("pool", Sx["Sb"], Sx["S"], r=[("g", st, "S")], w=[("g", st, "Sb")])
                for i in range(cps):
                    units = [unit(0, seq * cps + i), unit(1, seq * cps + cps - 1 - i)]
                    for sg in range(len(units[0])):
                        for st in range(2):
                            units[st][sg]()
                if self.pn == "P":
                    for st in range(2):
                        self.store(self.d["o_dl"][seq, e, st, hd], ST[st]["S"], r=[("g", st, "S")])
            self.P.barrier()
            okeys = [("o_acc", n) for n in range(nchk)]
            et = [A(5120 + i * 512, 512) for i in range(3)]

            def cons_z(b, ps, key, hd=hd):
                bs = slice(b * 512, (b + 1) * 512)
                sq = A(7168, 512, BF16)
                self.act(et[0], ps, AF.Silu, r=[key], w=["et0"])
                self.act(sq, o_acc[:, bs], AF.Square, r=okeys, w=["gsq"])
                self.mm(self.pb[7][:, :], self.ones_bf, sq, True, True, r=["gsq", "cbf"], w=[("pb", 7)])
                self.act(et[1], self.pb[7][:, :], AF.Sqrt, r=[("pb", 7)], w=["et1"], bias=self.epsc[:, 0:1], scale=1.0 / 128)
                self.P.op("dve", lambda en: en.reciprocal(out=et[1], in_=et[1]), reads=["et1"], writes=["et1"])
                self.stt(et[2], o_acc[:, bs], V[:, vcol("gdn_onorm%d" % e, 0, 1)], et[1], ALU.mult, ALU.mult,
                         r=okeys + ["et1", "vecs"], w=["et2"])
                self.tt("dve", self.yb[:, 4 + hd, bs], et[2], et[0], ALU.mult, r=["et0", "et2"], w=[("yb", b)])
            self.proj_full(self.d["win_e"][e, 20 + hd], cons_z)
            self.P.barrier()

    def odd_mixer(self, l):
        o = l // 2
        T, L, nseq, NB = self.T, self.L, self.nseq, self.NB
        V = self.vecs
        xc = self.carve(0, 4096).rearrange("p (i t) -> p i t", i=2)
        xcb = self.carve(4096, 4096, BF16).rearrange("p (i t) -> p i t", i=2)
        hs = self.carve(6144, 2048)
        tmp = [self.carve(8192 + i * 512, 512) for i in range(6)]
        t_ra, t_i, t_a2, t_b = tmp[0], tmp[1], tmp[2], tmp[3]
        t_h = [tmp[4], tmp[5]]
        cst = self.lcst
        self.act(cst[:, :], V[:, vcol("lru_lam%d" % o, 0, 16)], AF.Exp, r=["vecs"], w=["lcst"], scale=-1.0)
        self.act(cst[:, :], cst[:, :], AF.Ln, r=["lcst"], w=["lcst"], bias=self.onec[:, 0:1])
        self.ts("dve", cst[:, :], cst[:, :], -LRU_C, ALU.mult, r=["lcst"], w=["lcst"])
        if self.pn == "S":
            self.load(self.lh0[:], self.d["lru0"][:, o], w=["lh0"])
        for n in range(4):
            for ti in range(2):
                ft = 2 * n + ti
                raw = hs

                def cons(b, ps, key, raw=raw):
                    self.cp("act", raw[:, b * 512:(b + 1) * 512], ps, r=[key], w=["hs"])
                self.proj_full(self.d["win_o"][o, ft], cons)
                cw = lambda j, ft=ft: V[:, vcol("lru_conv_w%d" % o, j * 8 + ft, 1)]
                xci = xc[:, ti, 0:T]
                self.act(xci, raw[:, 0:T], AF.Identity, r=["hs", "vecs"], w=[("xc", ti)],
                         bias=V[:, vcol("lru_conv_b%d" % o, ft, 1)], scale=cw(1))
                r3 = raw[:, 0:T].rearrange("p (s l) -> p s l", s=nseq)
                x3 = xci.rearrange("p (s l) -> p s l", s=nseq)
                self.stt(x3[:, :, 1:L], r3[:, :, 0:L - 1], cw(0), x3[:, :, 1:L], ALU.mult, ALU.add, r=["hs", ("xc", ti), "vecs"], w=[("xc", ti)])
                self.stt(x3[:, :, 0:L - 1], r3[:, :, 1:L], cw(2), x3[:, :, 0:L - 1], ALU.mult, ALU.add, r=["hs", ("xc", ti), "vecs"], w=[("xc", ti)])
                self.stt(x3[:, :, 0:L - 2], r3[:, :, 2:L], cw(3), x3[:, :, 0:L - 2], ALU.mult, ALU.add, r=["hs", ("xc", ti), "vecs"], w=[("xc", ti)])
                self.cp("pool", xcb[:, ti, 0:T], xci, r=[("xc", ti)], w=[("xcb", ti)])
            for ti in range(2):
                jt = 2 * n + ti
                for dd in range(2):
                    wv, wk = self.slab(self.d["wlru"][o, dd, n])
                    w4 = wv.rearrange("p (g k c) -> p g k c", g=2, k=2)
                    blocks = list(range(NB)) if dd == 0 else list(range(NB - 1, -1, -1))
                    prev = None
                    for bi, b in enumerate(blocks):
                        bs = slice(b * 512, (b + 1) * 512)
                        pr, pi_ = self.pb[4], self.pb[5]
                        for g, pp in ((0, pr), (1, pi_)):
                            for kt in range(2):
                                self.mm(pp[:, :], w4[:, g, kt, ti * 128:(ti + 1) * 128], xcb[:, kt, bs], kt == 0, kt == 1,
                                        r=[wk, ("xcb", kt)], w=[("pb", 4 + g)])
                        self.act(t_ra, pr[:, :], AF.Sigmoid, r=[("pb", 4), "vecs"], w=["t_ra"], bias=V[:, vcol("lru_b_r%d" % o, dd * 8 + jt, 1)])
                        self.act(t_ra, t_ra, AF.Exp, r=["t_ra", "lcst"], w=["t_ra"], scale=cst[:, dd * 8 + jt:dd * 8 + jt + 1])
                        self.act(t_i, pi_[:, :], AF.Sigmoid, r=[("pb", 5), "vecs"], w=["t_i"], bias=V[:, vcol("lru_b_i%d" % o, dd * 8 + jt, 1)])
                        self.tt("pool", t_a2, t_ra, t_ra, ALU.mult, r=["t_ra"], w=["t_a2"])
                        self.act(t_a2, t_a2, AF.Sqrt, r=["t_a2"], w=["t_a2"], bias=self.onec[:, 0:1], scale=-1.0)
                        self.tt("dve", t_b, t_i, xc[:, ti, bs], ALU.mult, r=["t_i", ("xc", ti)], w=["t_b"])
                        self.tt("dve", t_b, t_b, t_a2, ALU.mult, r=["t_b", "t_a2"], w=["t_b"])
                        th = t_h[bi % 2]
                        thk = ("t_h", bi % 2)
                        nsub = max(1, 512 // L)
                        seglen = min(512, L)
                        for sg in (range(nsub) if dd == 0 else range(nsub - 1, -1, -1)):
                            lo = sg * seglen
                            tok0 = b * 512 + lo
                            seq = tok0 // L
                            first = (tok0 % L == 0) if dd == 0 else ((tok0 + seglen) % L == 0)
                            if first:
                                init = self.lh0[:, dd, jt:jt + 1] if self.pn == "S" else 0.0
                                ir = ["lh0"] if self.pn == "S" else []
                            else:
                                pth = t_h[(bi - 1) % 2]
                                init = pth[:, 511:512] if dd == 0 else pth[:, 0:1]
                                ir = [("t_h", (bi - 1) % 2)]
                            o_ap, a_ap, b_ap = th[:, lo:lo + seglen], t_ra[:, lo:lo + seglen], t_b[:, lo:lo + seglen]
                            if dd == 1:
                                o_ap, a_ap, b_ap = self.rev(o_ap), self.rev(a_ap), self.rev(b_ap)
                            self.P.op("dve", lambda e, o_ap=o_ap, a_ap=a_ap, b_ap=b_ap, init=init: e.tensor_tensor_scan(
                                out=o_ap, data0=a_ap, data1=b_ap, initial=init, op0=ALU.mult, op1=ALU.add),
                                reads=["t_ra", "t_b"] + ir, writes=[thk])
                            last = ((tok0 + seglen) % L == 0) if dd == 0 else (tok0 % L == 0)
                            if last and self.pn == "P":
                                col = lo + seglen - 1 if dd == 0 else lo
                                self.cp("pool", self.finl[:, seq, dd, jt:jt + 1], th[:, col:col + 1], r=[thk], w=["finl"])
                        if dd == 0:
                            self.cp("pool", hs[:, bs], th, r=[thk], w=["hs"])
                        else:
                            self.tt("pool", hs[:, bs], hs[:, bs], th, ALU.add, r=[thk, "hs"], w=["hs"])

                def cons2(b, ps, key, jt=jt):
                    self.act(t_i, ps, AF.Gelu_apprx_tanh, r=[key], w=["t_i"])
                    self.tt("dve", self.yb[:, jt, b * 512:(b + 1) * 512], t_i, hs[:, b * 512:(b + 1) * 512], ALU.mult,
                            r=["t_i", "hs"], w=[("yb", b)])
                self.proj_full(self.d["win_o"][o, 8 + jt], cons2)
        if self.pn == "P":
            for seq in range(4):
                for dd in range(2):
                    self.store(self.d["o_lru"][seq, o, dd], self.finl[:, seq, dd, :], r=["finl"])

    def rev(self, ap):
        (ps, pn_), (fs, fn) = ap.ap
        return AP(ap.tensor, ap.offset + fs * (fn - 1), [[ps, pn_], [-fs, fn]])


def _fm(v, nt):
    return np.ascontiguousarray(np.asarray(v, np.float32).reshape(nt, 128).T)


def _slabify(W):
    W = np.asarray(W, np.float32)
    K, N = W.shape
    Np = (N + 127) // 128 * 128
    if Np != N:
        W = np.concatenate([W, np.zeros((K, Np - N), np.float32)], axis=1)
    return np.ascontiguousarray(W.reshape(8, 128, Np // 128, 128).transpose(2, 1, 0, 3).reshape(Np // 128, 128, 1024))


def _s5_state_layout(a):
    a = np.asarray(a, np.float32)
    lead = a.shape[:-2]
    a = a.reshape(lead + (16, 2, 64))
    nl = len(lead)
    a = np.moveaxis(a, (nl + 1, nl + 2), (0, 1))
    return np.ascontiguousarray(a.reshape((128,) + lead + (16,)))


def _grid_sincos(n_tokens):
    rows = n_tokens // 64
    row = np.repeat(np.arange(rows, dtype=np.float32), 64)
    col = np.tile(np.arange(64, dtype=np.float32), rows)
    n_freq = D // 4
    omega = (np.float32(10000.0) ** (-np.arange(n_freq, dtype=np.float32) / np.float32(n_freq))).astype(np.float32)
    ar = row[:, None] * omega
    ac = col[:, None] * omega
    return np.concatenate([np.sin(ar), np.cos(ar), np.sin(ac), np.cos(ac)], axis=-1).astype(np.float32)


def _consts():
    r = np.arange(128)[:, None]
    q = np.arange(128)[None, :]
    mats = [np.eye(128), np.ones((128, 128)), r > q, r >= q, r < q, r <= q, -(r > q).astype(np.float32), q + 0 * r, -(r < q).astype(np.float32)]
    return np.ascontiguousarray(np.concatenate([np.asarray(m, np.float32) for m in mats], axis=1))


def prepare_shared(inp):
    g = lambda k: np.asarray(inp[k], np.float32)
    sh = {}
    vecs = np.zeros((128, NV), np.float32)

    def put(name, arr):
        o, n = VEC[name]
        assert arr.shape == (128, n), (name, arr.shape, n)
        vecs[:, o:o + n] = arr
    for l in range(4):
        put("n_mix_pre%d" % l, _fm(g("norm_mix_pre")[l], 8))
        put("n_mix_post%d" % l, _fm(g("norm_mix_post")[l], 8))
        put("n_mlp_pre%d" % l, _fm(g("norm_mlp_pre")[l], 8))
        put("n_mlp_post%d" % l, _fm(g("norm_mlp_post")[l], 8))
        put("b_ada%d" % l, _fm(g("b_ada")[l], 48))
    for e in range(2):
        put("s5_lam_re%d" % e, _s5_state_layout(g("s5_lam_re")[e]).reshape(128, 32))
        put("s5_lam_im%d" % e, _s5_state_layout(g("s5_lam_im")[e]).reshape(128, 32))
        ldt = np.broadcast_to(g("s5_log_dt")[e][:, :, None], (2, 32, 64))
        put("s5_dt%d" % e, _s5_state_layout(ldt).reshape(128, 32))
        put("s5_d%d" % e, _fm(g("s5_d")[e], 4))
        cw = g("gdn_conv_w")[e]
        put("gdn_conv_w%d" % e, np.concatenate([_fm(cw[j], 12) for j in range(4)], axis=1))
        put("gdn_conv_b%d" % e, _fm(g("gdn_conv_b")[e], 12))
        put("gdn_alog%d" % e, np.broadcast_to(g("gdn_a_log")[e].reshape(1, 8), (128, 8)))
        put("gdn_dtb%d" % e, np.broadcast_to(g("gdn_dt_bias")[e].reshape(1, 8), (128, 8)))
        put("gdn_onorm%d" % e, g("gdn_o_norm")[e].reshape(128, 1))
    for o in range(2):
        cw = g("lru_conv_w")[o]
        put("lru_conv_w%d" % o, np.concatenate([_fm(cw[j], 8) for j in range(4)], axis=1))
        put("lru_conv_b%d" % o, _fm(g("lru_conv_b")[o], 8))
        put("lru_b_r%d" % o, np.concatenate([_fm(g("lru_b_r")[o, dd], 8) for dd in range(2)], axis=1))
        put("lru_b_i%d" % o, np.concatenate([_fm(g("lru_b_i")[o, dd], 8) for dd in range(2)], axis=1))
        put("lru_lam%d" % o, np.concatenate([_fm(g("lru_lam")[o, dd], 8) for dd in range(2)], axis=1))
    sh["vecs"] = vecs
    sh["cst"] = _consts()
    sh["pos"] = _grid_sincos(2048)
    c_ = np.arange(128)[:, None]
    e_ = np.arange(128)[None, :]
    lows = []
    for m in range(7):
        sz = 1 << m
        lows.append(((c_ // (2 * sz) == e_ // (2 * sz)) & (c_ % (2 * sz) >= sz) & (e_ % (2 * sz) < sz)).astype(np.float32))
    ups = [mk.T for mk in lows]
    sh["bmk"] = np.ascontiguousarray(np.concatenate(lows + ups, axis=1))
    sh["w_ada"] = np.stack([_slabify(g("w_ada")[l]) for l in range(4)])
    sh["win_e"] = np.stack([_slabify(g("w_in_even")[e]) for e in range(2)])
    sh["wout_e"] = np.stack([_slabify(g("w_out_even")[e]) for e in range(2)])
    sh["win_o"] = np.stack([_slabify(g("w_in_odd")[o]) for o in range(2)])
    sh["wout_o"] = np.stack([_slabify(g("w_out_odd")[o]) for o in range(2)])
    sh["wm1"] = np.stack([_slabify(g("w_mlp_in")[l]) for l in range(4)])
    w2 = g("w_mlp_out").reshape(4, 4, 8, 128, 8, 128).transpose(0, 4, 1, 3, 2, 5)
    sh["wm2"] = np.ascontiguousarray(w2.reshape(4, 8, 4, 128, 1024))
    wg = np.stack([g("lru_w_r"), g("lru_w_i")])
    wg = wg.reshape(2, 2, 2, 4, 2, 128, 256).transpose(1, 2, 3, 5, 0, 4, 6)
    sh["wlru"] = np.ascontiguousarray(wg.reshape(2, 2, 4, 128, 1024))
    s5b = np.zeros((2, 128, 4, 4, 2, 128), np.float32)
    s5c = np.zeros((2, 128, 16, 2, 128), np.float32)
    for ri, (kb, kc) in enumerate((("s5_b_re", "s5_c_re"), ("s5_b_im", "s5_c_im"))):
        B = g(kb)
        C = g(kc)
        for gg in range(32):
            s_, gh = gg // 2, gg % 2
            ut, sl = s_ // 4, s_ % 4
            rows = slice(sl * 32 + gh * 16, sl * 32 + gh * 16 + 16)
            cols = slice(gh * 64, gh * 64 + 64)
            s5b[:, rows, ut, sl, ri, cols] = B[:, gg].transpose(0, 2, 1)
            s5c[:, cols, s_, ri, rows] = C[:, gg].transpose(0, 2, 1)
    sh["s5b"] = s5b
    sh["s5c"] = s5c
    return sh


def prepare_core(inp, c, sh):
    g = lambda k: np.asarray(inp[k], np.float32)
    m = dict(sh)
    m["xs"] = np.ascontiguousarray(g("x_sample")[c])
    m["xp"] = np.ascontiguousarray(g("x_prompt")[4 * c:4 * c + 4].reshape(1024, D))
    cv = np.stack([_fm(g("c_ctx"), 8), _fm(g("c")[c], 8)], axis=-1)
    m["cvec"] = np.ascontiguousarray(cv)
    h0 = np.stack([g("state_s5_re")[c], g("state_s5_im")[c]], axis=2)
    m["s5h0"] = _s5_state_layout(h0)
    m["dl0"] = np.ascontiguousarray(g("state_delta")[c].transpose(3, 0, 1, 2, 4))
    m["lru0"] = np.ascontiguousarray(g("state_lru")[c].reshape(2, 2, 8, 128).transpose(3, 0, 1, 2))
    return m


_NC_CACHE = {}


def kernel(**inputs):
    if "nc" not in _NC_CACHE:
        _NC_CACHE["nc"] = Builder().nc
    nc = _NC_CACHE["nc"]
    sh = prepare_shared(inputs)
    in_maps = [prepare_core(inputs, c, sh) for c in range(8)]
    res = run_bass_kernel_spmd(nc, in_maps, core_ids=list(range(8))).results
    y_prompt = np.concatenate([r["yp"].reshape(4, 256, D) for r in res], axis=0)
    y_sample = np.stack([r["ys"] for r in res], axis=0)
    s5 = np.concatenate([r["o_s5"] for r in res], axis=0)
    s5 = s5.reshape(32, 2, 2, 2, 2, 64, 16).transpose(0, 1, 2, 3, 6, 4, 5).reshape(32, 2, 2, 2, 32, 64)
    new_re = np.ascontiguousarray(s5[:, :, :, 0])
    new_im = np.ascontiguousarray(s5[:, :, :, 1])
    new_delta = np.concatenate([r["o_dl"] for r in res], axis=0)
    lru = np.concatenate([r["o_lru"] for r in res], axis=0)
    new_lru = np.ascontiguousarray(lru.transpose(0, 1, 2, 4, 3).reshape(32, 2, 2, 1024))
    return (y_prompt.astype(np.float32), y_sample.astype(np.float32), new_re.astype(np.float32),
            new_im.astype(np.float32), new_delta.astype(np.float32), new_lru.astype(np.float32))
```

```python
import numpy as np
from contextlib import ExitStack
import concourse.bass as bass
import concourse.mybir as mybir
from concourse.bass import AP
from concourse.bass_utils import run_bass_kernel_spmd

F32 = mybir.dt.float32
BF16 = mybir.dt.bfloat16
ALU = mybir.AluOpType
AF = mybir.ActivationFunctionType

ENGS = ("pe", "act", "dve", "pool", "sp")
NDMASEM = 12


class Prog:
    def __init__(self, nc, same_engine_sync=("act", "dve", "pool")):
        self.nc = nc
        self.es = ExitStack()
        self.q = {e: [] for e in ENGS}
        self.count = {e: 0 for e in ENGS}
        self.seen = {e: {} for e in ENGS}
        self.clock_at = {}
        self.last_w = {}
        self.readers = {}
        self.ses = set(same_engine_sync)
        self.sem = {}
        for e in ENGS:
            if e != "sp":
                self.sem[e] = self.es.enter_context(nc.semaphore("s_" + e))
        self.dsem = {}
        self.dcnt = {}
        self.drr = {}
        for qn in ("sp", "act"):
            self.dsem[qn] = [self.es.enter_context(nc.semaphore("d_%s%d" % (qn, i))) for i in range(NDMASEM)]
            self.dcnt[qn] = [0] * NDMASEM
            self.drr[qn] = 0
        self.nwaits = 0
        self.nops = 0

    def sb(self, name, shape, dt=F32):
        return self.es.enter_context(self.nc.sbuf_tensor("sb_" + name, list(shape), dt))

    def ps(self, name, shape, dt=F32):
        return self.es.enter_context(self.nc.psum_tensor("ps_" + name, list(shape), dt))

    def _deps(self, reads, writes):
        deps = set()
        for k in reads:
            t = self.last_w.get(k)
            if t is not None:
                deps.add(t)
        for k in writes:
            t = self.last_w.get(k)
            if t is not None:
                deps.add(t)
            for r in self.readers.get(k, ()):
                deps.add(r)
        return deps

    def _mkwaits(self, eng, deps):
        need = {}
        for (e, s) in deps:
            if need.get(e, 0) < s:
                need[e] = s
        waits = []
        seen = self.seen[eng]
        for e, s in need.items():
            if e == eng and eng not in self.ses:
                continue
            if seen.get(e, 0) >= s:
                continue
            waits.append((e, s))
        for e, s in waits:
            if seen.get(e, 0) < s:
                seen[e] = s
            ck = self.clock_at.get((e, s))
            if ck:
                for f, v in ck.items():
                    if seen.get(f, 0) < v:
                        seen[f] = v
        self.nwaits += len(waits)
        return waits

    def _record(self, tok, reads, writes):
        for k in reads:
            self.readers.setdefault(k, []).append(tok)
        for k in writes:
            self.last_w[k] = tok
            self.readers[k] = []

    def op(self, eng, fn, reads=(), writes=()):
        deps = self._deps(reads, writes)
        waits = self._mkwaits(eng, deps)
        self.count[eng] += 1
        tok = (eng, self.count[eng])
        self.q[eng].append((fn, waits, ("c", eng)))
        self.clock_at[tok] = dict(self.seen[eng])
        self._record(tok, reads, writes)
        self.nops += 1
        return tok

    def dma(self, qn, fn, reads=(), writes=()):
        deps = self._deps(reads, writes)
        i = self.drr[qn]
        self.drr[qn] = (i + 1) % NDMASEM
        semname = ("d", qn, i)
        prev = self.dcnt[qn][i]
        if prev:
            deps.add((semname, prev))
        waits = self._mkwaits(qn, deps)
        self.dcnt[qn][i] = prev + 16
        tok = (semname, prev + 16)
        self.q[qn].append((fn, waits, ("d", qn, i)))
        self.clock_at[tok] = dict(self.seen[qn])
        self._record(tok, reads, writes)
        self.nops += 1
        return tok

    def barrier(self):
        toks = set()
        for e in ENGS:
            if e != "sp" and self.count[e]:
                toks.add((e, self.count[e]))
        for qn in self.dsem:
            for i in range(NDMASEM):
                if self.dcnt[qn][i]:
                    toks.add((("d", qn, i), self.dcnt[qn][i]))
        for e in ENGS:
            w = self._mkwaits(e, toks)
            if w:
                self.q[e].append((None, w, None))
        self.last_w = {}
        self.readers = {}

    def _semof(self, e):
        if isinstance(e, tuple):
            return self.dsem[e[1]][e[2]]
        return self.sem[e]

    def emit(self, final_tokens):
        nc = self.nc
        fw = self._mkwaits("sp", set(final_tokens))
        with nc.Block() as block:
            def run(engname):
                def body(eng):
                    for fn, waits, kind in self.q[engname]:
                        if fn is None:
                            for (e, s) in waits:
                                eng.wait_ge(self._semof(e), s)
                            continue
                        for (e, s) in waits[1:]:
                            eng.wait_ge(self._semof(e), s)
                        ins = fn(eng)
                        if waits:
                            ins._wait_ge(self._semof(waits[0][0]), waits[0][1])
                        if kind[0] == "c":
                            ins.then_inc(self.sem[kind[1]], 1)
                        else:
                            ins.then_inc(self.dsem[kind[1]][kind[2]], 16)
                    if engname == "sp":
                        for (e, s) in fw:
                            eng.wait_ge(self._semof(e), s)
                return body
            block.sync(run("sp"))
            block.tensor(run("pe"))
            block.scalar(run("act"))
            block.vector(run("dve"))
            block.gpsimd(run("pool"))
        self.es.close()


D = 1024
KT = 8
DFF = 4096
GA, PA = 32, 64
HB = 4
CH = 64
EPS = 1e-6
LRU_C = 8.0
TC = 128
PASSES = {"S": dict(nseq=1, L=2048), "P": dict(nseq=4, L=256)}

VEC = {}
_nv = [0]


def _vreg(name, n):
    VEC[name] = (_nv[0], n)
    _nv[0] += n


for _l in range(4):
    for _n in ("n_mix_pre", "n_mix_post", "n_mlp_pre", "n_mlp_post"):
        _vreg("%s%d" % (_n, _l), 8)
    _vreg("b_ada%d" % _l, 48)
for _e in range(2):
    _vreg("s5_lam_re%d" % _e, 32)
    _vreg("s5_lam_im%d" % _e, 32)
    _vreg("s5_dt%d" % _e, 32)
    _vreg("s5_d%d" % _e, 4)
    _vreg("gdn_conv_w%d" % _e, 48)
    _vreg("gdn_conv_b%d" % _e, 12)
    _vreg("gdn_alog%d" % _e, 8)
    _vreg("gdn_dtb%d" % _e, 8)
    _vreg("gdn_onorm%d" % _e, 1)
for _o in range(2):
    _vreg("lru_conv_w%d" % _o, 32)
    _vreg("lru_conv_b%d" % _o, 8)
    _vreg("lru_b_r%d" % _o, 16)
    _vreg("lru_b_i%d" % _o, 16)
    _vreg("lru_lam%d" % _o, 16)
NV = _nv[0]


def vcol(name, i=0, n=1):
    o, _ = VEC[name]
    return slice(o + i, o + i + n)


class Builder:
    def __init__(self, layers=(0, 1, 2, 3), passes=("S", "P"), debug=False):
        self.layers = layers
        self.passes = passes
        self.debug = debug
        nc = self.nc = bass.Bass("TRN2", target_bir_lowering=False)
        self.P = Prog(nc)
        self.outs = []
        self._decl_dram()
        self._alloc()
        self._prologue()
        for pn in passes:
            self.run_pass(pn)
        self.P.emit(self.outs)

    def _decl_dram(self):
        nc = self.nc

        def din(name, shape, dt=F32):
            return nc.dram_tensor(name, list(shape), dt, kind="ExternalInput").ap()

        def dout(name, shape):
            return nc.dram_tensor(name, list(shape), F32, kind="ExternalOutput").ap()
        self.d = d = {}
        d["xs"] = din("xs", [2048, D])
        d["xp"] = din("xp", [1024, D])
        d["pos"] = din("pos", [2048, D])
        d["cst"] = din("cst", [128, 128 * 9])
        d["bmk"] = din("bmk", [128, 14 * 128])
        d["cvec"] = din("cvec", [128, 8, 2])
        d["vecs"] = din("vecs", [128, NV])
        d["w_ada"] = din("w_ada", [4, 48, 128, 1024])
        d["win_e"] = din("win_e", [2, 25, 128, 1024])
        d["wout_e"] = din("wout_e", [2, 8, 128, 1024])
        d["win_o"] = din("win_o", [2, 16, 128, 1024])
        d["wout_o"] = din("wout_o", [2, 8, 128, 1024])
        d["wm1"] = din("wm1", [4, 32, 128, 1024])
        d["wm2"] = din("wm2", [4, 8, 4, 128, 1024])
        d["wlru"] = din("wlru", [2, 2, 4, 128, 1024])
        d["s5b"] = din("s5b", [2, 128, 4, 4, 2, 128])
        d["s5c"] = din("s5c", [2, 128, 16, 2, 128])
        d["s5h0"] = din("s5h0", [128, 2, 2, 2, 16])
        d["dl0"] = din("dl0", [128, 2, 2, 4, 128])
        d["lru0"] = din("lru0", [128, 2, 2, 8])
        d["ys"] = dout("ys", [2048, D])
        d["yp"] = dout("yp", [1024, D])
        d["o_s5"] = dout("o_s5", [4, 2, 2, 2, 128, 16])
        d["o_dl"] = dout("o_dl", [4, 2, 2, 4, 128, 128])
        d["o_lru"] = dout("o_lru", [4, 2, 2, 128, 8])
        if self.debug:
            d["dbg"] = dout("dbg", [128, 8, 2048])

    def _alloc(self):
        P = self.P
        self.x = P.sb("x", [128, KT, 2048], F32)
        self.h = P.sb("h", [128, KT, 2048], BF16)
        self.yb = P.sb("yb", [128, KT, 2048], BF16)
        self.wst = P.sb("wst", [128, 2, 1024], F32)
        self.wbf = P.sb("wbf", [128, 2, 1024], BF16)
        self.cst = P.sb("cst", [128, 128 * 9], F32)
        self.cbf = P.sb("cbf", [128, 256], BF16)
        self.vecs = P.sb("vecs", [128, NV], F32)
        self.mod = P.sb("mod", [128, 4, 48, 2], F32)
        self.mvec = P.sb("mvec", [128, 4, 2, 4, 8], F32)
        self.cv = P.sb("cv", [128, 8, 2], F32)
        self.ARENA = 13312
        self.arena = P.sb("arena", [128, self.ARENA], F32)
        self.pb = [P.ps("pb%d" % i, [128, 512], F32) for i in range(8)]
        self.slab_i = 0
        self.pb_rr = 0
        self.finl = P.sb("finl", [128, 4, 2, 8], F32)
        self.epsc = P.sb("epsc", [128, 1], F32)
        self.onec = P.sb("onec", [128, 1], F32)
        self.lcst = P.sb("lcst", [128, 16], F32)
        self.lh0 = P.sb("lh0", [128, 2, 8], F32)
        self.fin5 = P.sb("fin5", [128, 4, 2, 2, 16], F32)
        self.s5p = P.sb("s5p", [128, 16, 32], F32)
        self.s5h = P.sb("s5h", [128, 2, 2, 16], F32)
        self.ident = self.cst[:, 0:128]
        self.ones = self.cst[:, 128:256]
        self.m_gt = self.cst[:, 256:384]
        self.m_ge = self.cst[:, 384:512]
        self.m_lt = self.cst[:, 512:640]
        self.m_le = self.cst[:, 640:768]
        self.m_ngt = self.cst[:, 768:896]
        self.jidx = self.cst[:, 896:1024]
        self.m_nlt = self.cst[:, 1024:1152]
        self.ones_bf = self.cbf[:, 0:128]
        self.ident_bf = self.cbf[:, 128:256]

    def carve(self, off, n, dt=F32, parts=128):
        if dt == F32:
            return self.arena[0:parts, off:off + n]
        v = self.arena[0:parts, off:off + (n + 1) // 2].bitcast(BF16)
        return v[:, 0:n]

    def _bk(self, w, *aps):
        extra = []
        for a in aps:
            nm = getattr(a, "name", None)
            if isinstance(nm, str) and nm.startswith("ps_pb"):
                k = ("BANK", nm)
                if k not in extra:
                    extra.append(k)
        return list(w) + extra if extra else w

    def act(self, out, in_, func, r, w, bias=None, scale=None):
        w = self._bk(w, out, in_)
        kw = {}
        if bias is not None:
            kw["bias"] = bias
        if scale is not None:
            kw["scale"] = scale
        return self.P.op("act", lambda e: e.activation(out=out, in_=in_, func=func, **kw), reads=r, writes=w)

    def tt(self, eng, out, a, b, op, r, w):
        w = self._bk(w, out, a, b)
        return self.P.op(eng, lambda e: e.tensor_tensor(out=out, in0=a, in1=b, op=op), reads=r, writes=w)

    def ts(self, eng, out, a, s1, op0, r, w, s2=None, op1=None):
        w = self._bk(w, out, a)
        if op1 is None:
            return self.P.op(eng, lambda e: e.tensor_scalar(out=out, in0=a, scalar1=s1, scalar2=None, op0=op0), reads=r, writes=w)
        return self.P.op(eng, lambda e: e.tensor_scalar(out=out, in0=a, scalar1=s1, scalar2=s2, op0=op0, op1=op1), reads=r, writes=w)

    def stt(self, out, a, s, b, op0, op1, r, w):
        w = self._bk(w, out, a, b)
        return self.P.op("dve", lambda e: e.scalar_tensor_tensor(out=out, in0=a, scalar=s, in1=b, op0=op0, op1=op1), reads=r, writes=w)

    def cp(self, eng, out, in_, r, w):
        w = self._bk(w, out, in_)
        if eng == "act":
            return self.P.op("act", lambda e: e.copy(out=out, in_=in_), reads=r, writes=w)
        return self.P.op(eng, lambda e: e.tensor_copy(out=out, in_=in_), reads=r, writes=w)

    def mm(self, out, lhsT, rhs, start, stop, r, w):
        w = self._bk(w, out)
        return self.P.op("pe", lambda e: e.matmul(out, lhsT=lhsT, rhs=rhs, start=start, stop=stop), reads=r, writes=w)

    def tr(self, out, in_, ident, r, w):
        w = self._bk(w, out)
        return self.P.op("pe", lambda e: e.transpose(out, in_, ident), reads=r, writes=w)

    def memset(self, eng, ap, val, w):
        return self.P.op(eng, lambda e: e.memset(ap, val), writes=w)

    def load(self, out, in_, w, r=()):
        return self.P.dma("sp", lambda e: e.dma_start(out=out, in_=in_), reads=r, writes=w)

    def store(self, out, in_, r):
        t = self.P.dma("sp", lambda e: e.dma_start(out=out, in_=in_), reads=r)
        self.outs.append(t)
        return t

    def slab(self, dram_ap, cast=True):
        deep = getattr(self, "deep_ring", False)
        st_bufs = [self.wst[:, 0, :], self.wst[:, 1, :]]
        bf_bufs = [self.wbf[:, 0, :], self.wbf[:, 1, :]]
        if deep:
            st_bufs += [self.carve(4608, 1024), self.carve(5632, 1024), self.carve(11264, 1024), self.carve(12288, 1024)]
            bf_bufs += [self.carve(6656, 1024, BF16)]
        self.slab_n = getattr(self, "slab_n", 0) + 1
        i = self.slab_n % len(st_bufs)
        j = self.slab_n % len(bf_bufs)
        self.load(st_bufs[i], dram_ap, w=[("wst", i)])
        if not cast:
            return st_bufs[i], ("wst", i)
        self.cp(getattr(self, "cast_eng", "act"), bf_bufs[j], st_bufs[i], r=[("wst", i)], w=[("wbf", j)])
        return bf_bufs[j], ("wbf", j)

    def _prologue(self):
        P = self.P
        d = self.d
        self.load(self.cst[:], d["cst"], w=["cst"])
        self.load(self.vecs[:], d["vecs"], w=["vecs"])
        self.load(self.cv[:], d["cvec"], w=["cv"])
        self.memset("pool", self.epsc[:], EPS, w=["epsc"])
        self.memset("pool", self.onec[:], 1.0, w=["onec"])
        self.cp("dve", self.cbf[:, 0:128], self.cst[:, 128:256], r=["cst"], w=["cbf"])
        self.cp("dve", self.cbf[:, 128:256], self.cst[:, 0:128], r=["cst", "cbf"], w=["cbf"])
        self.act(self.cv[:], self.cv[:], AF.Silu, r=["cv"], w=["cv"])
        for l in self.layers:
            pm = self.pb[l % 2]
            for ft in range(48):
                wv, wk = self.slab(d["w_ada"][l, ft], cast=False)
                w3 = wv.rearrange("p (k c) -> p k c", k=8)
                for kt in range(8):
                    self.mm(pm[:, 2 * ft:2 * ft + 2], w3[:, kt, :], self.cv[:, kt, :], kt == 0, kt == 7,
                            r=[wk, "cv"], w=[("pm", l % 2)])
            for j in range(2):
                self.tt("dve", self.mod[:, l, :, j], pm[:, 0:96].rearrange("p (f j) -> p f j", j=2)[:, :, j],
                        self.vecs[:, vcol("b_ada%d" % l, 0, 48)], ALU.add, r=[("pm", l % 2), "vecs"], w=[("mod", l, j)])
                for q, (npre, npost, o_sc, o_gt) in enumerate((("n_mix_pre", "n_mix_post", 8, 16), ("n_mlp_pre", "n_mlp_post", 32, 40))):
                    self.stt(self.mvec[:, l, j, 2 * q, :], self.mod[:, l, o_sc:o_sc + 8, j], 1.0,
                             self.vecs[:, vcol("%s%d" % (npre, l), 0, 8)], ALU.add, ALU.mult,
                             r=[("mod", l, j), "vecs"], w=[("mvec", l, j, 2 * q)])
                    self.tt("dve", self.mvec[:, l, j, 2 * q + 1, :], self.mod[:, l, o_gt:o_gt + 8, j],
                            self.vecs[:, vcol("%s%d" % (npost, l), 0, 8)], ALU.mult,
                            r=[("mod", l, j), "vecs"], w=[("mvec", l, j, 2 * q + 1)])
        P.barrier()

    def run_pass(self, pn):
        cfg = PASSES[pn]
        self.pn = pn
        self.nseq, self.L = cfg["nseq"], cfg["L"]
        self.T = self.nseq * self.L
        self.NB = self.T // 512
        self.j = 1 if pn == "S" else 0
        self.load_x()
        for l in self.layers:
            self.modnorm_all(l)
            if l % 2 == 0:
                self.even_mixer(l)
            else:
                self.odd_mixer(l)
            self.P.barrier()
            self.cast_eng = "dve"
            self.deep_ring = True
            self.out_proj(l)
            self.P.barrier()
            self.mlp(l)
            self.cast_eng = "act"
            self.deep_ring = False
            self.P.barrier()
        if self.debug and pn == self.passes[-1]:
            self.store(self.d["dbg"][:, :, 0:self.T], self.x[:, :, 0:self.T], r=self.xkeys())
        self.store_x()
        self.P.barrier()

    def xkeys(self):
        return [("x", b) for b in range(self.NB)]

    def load_x(self):
        T = self.T
        src = self.d["xs"] if self.pn == "S" else self.d["xp"]
        xt = [self.carve(i * 1024, 1024) for i in range(2)]
        pt = [self.carve(2048 + i * 1024, 1024) for i in range(2)]
        for tt_ in range(T // 128):
            i = tt_ % 2
            self.load(xt[i], src[tt_ * 128:(tt_ + 1) * 128, :], w=[("xt", i)])
            if self.pn == "S":
                self.load(pt[i], self.d["pos"][tt_ * 128:(tt_ + 1) * 128, :], w=[("pt", i)])
                self.tt("dve", xt[i], xt[i], pt[i], ALU.add, r=[("xt", i), ("pt", i)], w=[("xt", i)])
            for hf in range(2):
                pbk = self.pb[(2 * tt_ + hf) % 4]
                key = ("pb", (2 * tt_ + hf) % 4)
                for q in range(4):
                    kt = hf * 4 + q
                    self.tr(pbk[:, q * 128:(q + 1) * 128], xt[i][:, kt * 128:(kt + 1) * 128], self.ident,
                            r=[("xt", i), "cst"], w=[key])
                self.cp("act" if hf == 0 else "dve", self.x[:, hf * 4:hf * 4 + 4, tt_ * 128:(tt_ + 1) * 128],
                        pbk[:, :].rearrange("p (q t) -> p q t", q=4), r=[key], w=[("x", tt_ // 4)])
        self.P.barrier()

    def store_x(self):
        T = self.T
        dst = self.d["ys"] if self.pn == "S" else self.d["yp"]
        ot = [self.carve(i * 1024, 1024) for i in range(2)]
        for tt_ in range(T // 128):
            i = tt_ % 2
            for hf in range(2):
                pbk = self.pb[(2 * tt_ + hf) % 4]
                key = ("pb", (2 * tt_ + hf) % 4)
                for q in range(4):
                    kt = hf * 4 + q
                    self.tr(pbk[:, q * 128:(q + 1) * 128], self.x[:, kt, tt_ * 128:(tt_ + 1) * 128], self.ident,
                            r=[("x", tt_ // 4), "cst"], w=[key])
                self.cp("act" if hf == 0 else "dve", ot[i][:, hf * 512:(hf + 1) * 512], pbk[:, :], r=[key], w=[("ot", i)])
            self.store(dst[tt_ * 128:(tt_ + 1) * 128, :], ot[i], r=[("ot", i)])

    def rstd_block(self, src3, srckeys, dstkey, scale, off):
        sq = self.carve(off, 4096, BF16).rearrange("p (k t) -> p k t", k=8)
        rs = self.carve(off + 2048, 512)
        self.act(sq, src3, AF.Square, r=srckeys, w=[("sq", off)])
        pbk, key = self.pb[7], ("pb", 7)
        for kt in range(8):
            self.mm(pbk[:, :], self.ones_bf, sq[:, kt, :], kt == 0, kt == 7, r=[("sq", off), "cbf"], w=[key])
        self.act(rs, pbk[:, :], AF.Sqrt, r=[key], w=[dstkey], bias=self.epsc[:, 0:1], scale=scale)
        self.P.op("dve", lambda e: e.reciprocal(out=rs, in_=rs), reads=[dstkey], writes=[dstkey])
        return rs

    def modnorm_block(self, l, which, b, hdst, hkey, off):
        xs = self.x[:, :, b * 512:(b + 1) * 512]
        rs = self.rstd_block(xs, [("x", b)], ("rs", off), 1.0 / D, off)
        tmp = self.carve(off + 2560, 512 * 2).rearrange("p (i t) -> p i t", i=2)
        sh_off = 0 if which == 0 else 24
        for kt in range(8):
            tk = ("mtmp", off, kt % 2)
            self.stt(tmp[:, kt % 2, :], self.x[:, kt, b * 512:(b + 1) * 512], self.mvec[:, l, self.j, 2 * which, kt:kt + 1], rs,
                     ALU.mult, ALU.mult, r=[("x", b), ("rs", off), ("mvec", l, self.j, 2 * which)], w=[tk])
            self.act(hdst(kt), tmp[:, kt % 2, :], AF.Identity, r=[tk, ("mod", l, self.j)], w=[hkey],
                     bias=self.mod[:, l, sh_off + kt, self.j:self.j + 1])

    def modnorm_all(self, l):
        for b in range(self.NB):
            self.modnorm_block(l, 0, b, lambda kt, b=b: self.h[:, kt, b * 512:(b + 1) * 512], ("h", b), off=(b % 2) * 3584)
        self.P.barrier()

    def proj(self, slab_dram, b, pbi, hkeys=None, hsrc=None):
        raise NotImplementedError

    def proj_full(self, slab_dram, consume, ncols=128):
        wv, wk = self.slab(slab_dram)
        w3 = wv.rearrange("p (k c) -> p k c", k=8)
        for b in range(self.NB):
            pbi = self.pb_rr
            self.pb_rr = (self.pb_rr + 1) % 4
            pbk, key = self.pb[pbi], ("pb", pbi)
            for kt in range(8):
                self.mm(pbk[0:ncols, :], w3[:, kt, 0:ncols], self.h[:, kt, b * 512:(b + 1) * 512], kt == 0, kt == 7,
                        r=[wk, ("h", b)], w=[key])
            consume(b, pbk[0:ncols, :], key)

    def out_proj(self, l):
        e = l // 2
        wd = self.d["wout_e"][e] if l % 2 == 0 else self.d["wout_o"][e]
        self.resid_update(l, 0, lambda ft: wd[ft], self.yb, 8)

    def resid_update(self, l, which, slab_of, src, nk, per_block=None):
        ob = self.carve(7168, 4096).rearrange("p (k t) -> p k t", k=8)
        for b in range(self.NB):
            for ft in range(8):
                wv, wk = self.slab(slab_of(ft))
                w3 = wv.rearrange("p (k c) -> p k c", k=8)
                pbi = ft % 4
                pbk, key = self.pb[pbi], ("pb", pbi)
                for kt in range(8):
                    self.mm(pbk[:, :], w3[:, kt, :], src[:, kt, b * 512:(b + 1) * 512], kt == 0, kt == 7,
                            r=[wk, ("yb", b)], w=[key])
                self.cp("act", ob[:, ft, :], pbk[:, :], r=[key], w=[("ob", ft)])
            self.post_block(l, which, b, ob, [("ob", ft) for ft in range(8)])

    def post_block(self, l, which, b, ob, obkeys):
        rs = self.rstd_block(ob, obkeys, ("rs", 0), 1.0 / D, 0)
        for kt in range(8):
            tk = ("mtmp", 0, kt % 2)
            tmp = self.carve(2560 + (kt % 2) * 512, 512)
            self.stt(tmp, ob[:, kt, :], self.mvec[:, l, self.j, 2 * which + 1, kt:kt + 1], rs, ALU.mult, ALU.mult,
                     r=[("ob", kt), ("rs", 0), ("mvec", l, self.j, 2 * which + 1)], w=[tk])
            self.tt("pool", self.x[:, kt, b * 512:(b + 1) * 512], self.x[:, kt, b * 512:(b + 1) * 512], tmp, ALU.add,
                    r=[tk, ("x", b)], w=[("x", b)])

    def mlp(self, l):
        f1 = self.yb
        f1v = self.yb[:, :, :].rearrange("p k (a t) -> p (k a) t", t=512)
        hb = self.h[:, :, 0:512]
        ob = self.carve(7168, 4096).rearrange("p (k t) -> p k t", k=8)
        rl = [self.carve(3584 + i * 512, 512) for i in range(2)]
        for b in range(self.NB):
            self.modnorm_block(l, 1, b, lambda kt: self.h[:, kt, 0:512], ("h", 0), off=0)
            for jt in range(32):
                wv, wk = self.slab(self.d["wm1"][l, jt])
                w3 = wv.rearrange("p (k c) -> p k c", k=8)
                pbi = jt % 4
                pbk, key = self.pb[pbi], ("pb", pbi)
                for kt in range(8):
                    self.mm(pbk[:, :], w3[:, kt, :], self.h[:, kt, 0:512], kt == 0, kt == 7, r=[wk, ("h", 0)], w=[key])
                rk = ("rl", jt % 2)
                self.act(rl[jt % 2], pbk[:, :], AF.Relu, r=[key], w=[rk])
                self.tt("pool" if jt % 2 else "dve", f1v[:, jt, :], rl[jt % 2], rl[jt % 2], ALU.mult, r=[rk], w=[("f1", jt)])
            for ft in range(8):
                pbi = 4 + ft % 2
                pbk, key = self.pb[pbi], ("pb", pbi)
                for jg in range(4):
                    wv, wk = self.slab(self.d["wm2"][l, ft, jg])
                    w3 = wv.rearrange("p (k c) -> p k c", k=8)
                    for jj in range(8):
                        jt = jg * 8 + jj
                        self.mm(pbk[:, :], w3[:, jj, :], f1v[:, jt, :], jt == 0, jt == 31, r=[wk, ("f1", jt)], w=[key])
                self.cp("act", ob[:, ft, :], pbk[:, :], r=[key], w=[("ob", ft)])
            self.post_block(l, 1, b, ob, [("ob", ft) for ft in range(8)])

    def even_mixer(self, l):
        self.s5_mixer(l)
        self.P.barrier()
        self.gdn_mixer(l)

    def sincos(self, arg, sin_out, cos_out, k, r, key):
        TWO_PI = 6.283185307179586
        C1 = 6.28125
        C2 = TWO_PI - C1
        MAGIC = 12582912.0
        PI = 3.141592653589793
        kk, rk = (key, "k"), (key, "r")
        self.ts("dve", k, arg, 1.0 / TWO_PI, ALU.mult, r=[(key, "arg")], w=[kk], s2=MAGIC, op1=ALU.add)
        self.ts("dve", k, k, MAGIC, ALU.subtract, r=[kk], w=[kk])
        self.stt(r, k, -C1, arg, ALU.mult, ALU.add, r=[kk, (key, "arg")], w=[rk])
        self.stt(r, k, -C2, r, ALU.mult, ALU.add, r=[kk, rk], w=[rk])

        def wrap(y):
            self.ts("dve", k, y, PI, ALU.is_gt, r=[rk], w=[kk], s2=-TWO_PI, op1=ALU.mult)
            self.tt("dve", y, y, k, ALU.add, r=[rk, kk], w=[rk])
            self.ts("dve", k, y, -PI, ALU.is_lt, r=[rk], w=[kk], s2=TWO_PI, op1=ALU.mult)
            self.tt("dve", y, y, k, ALU.add, r=[rk, kk], w=[rk])
        wrap(r)
        self.act(sin_out, r, AF.Sin, r=[rk], w=[(key, "sin")])
        self.ts("dve", r, r, PI / 2, ALU.add, r=[rk], w=[rk])
        wrap(r)
        self.act(cos_out, r, AF.Sin, r=[rk], w=[(key, "cos")])

    def s5_params(self, e):
        V, R = self.vecs, self.s5p
        lr = V[:, vcol("s5_lam_re%d" % e, 0, 32)]
        li = V[:, vcol("s5_lam_im%d" % e, 0, 32)]
        K = "s5p"
        rw = dict(r=[K, "vecs"], w=[K])
        self.act(R[:, 7, :], V[:, vcol("s5_dt%d" % e, 0, 32)], AF.Exp, **rw)
        self.tt("dve", R[:, 6, :], lr, R[:, 7, :], ALU.mult, **rw)
        self.tt("dve", R[:, 0, :], li, R[:, 7, :], ALU.mult, **rw)
        self.act(R[:, 1, :], R[:, 6, :], AF.Exp, **rw)
        self.P.op("dve", lambda en: en.tensor_copy(out=R[:, 15, :], in_=R[:, 0, :]), reads=[K], writes=[("sc1", "arg")])
        self.sincos(R[:, 15, :], R[:, 10, :], R[:, 9, :], R[:, 7, :], R[:, 8, :], "sc1")
        self.ts("dve", R[:, 15, :], R[:, 0, :], float(TC), ALU.mult, r=[K, ("sc1", "sin"), ("sc1", "cos")], w=[("sc2", "arg")])
        self.sincos(R[:, 15, :], R[:, 14, :], R[:, 13, :], R[:, 7, :], R[:, 8, :], "sc2")
        rw = dict(r=[K, "vecs", ("sc1", "sin"), ("sc1", "cos"), ("sc2", "sin"), ("sc2", "cos")], w=[K])
        self.tt("dve", R[:, 2, :], R[:, 1, :], R[:, 9, :], ALU.mult, **rw)
        self.tt("dve", R[:, 3, :], R[:, 1, :], R[:, 10, :], ALU.mult, **rw)
        self.tt("dve", R[:, 6, :], lr, lr, ALU.mult, **rw)
        self.tt("dve", R[:, 7, :], li, li, ALU.mult, **rw)
        self.tt("dve", R[:, 6, :], R[:, 6, :], R[:, 7, :], ALU.add, **rw)
        self.P.op("dve", lambda en: en.reciprocal(out=R[:, 6, :], in_=R[:, 6, :]), reads=[K], writes=[K])
        self.ts("dve", R[:, 7, :], R[:, 2, :], -1.0, ALU.add, **rw)
        self.tt("dve", R[:, 4, :], R[:, 7, :], lr, ALU.mult, **rw)
        self.tt("dve", R[:, 8, :], R[:, 3, :], li, ALU.mult, **rw)
        self.tt("dve", R[:, 4, :], R[:, 4, :], R[:, 8, :], ALU.add, **rw)
        self.tt("dve", R[:, 4, :], R[:, 4, :], R[:, 6, :], ALU.mult, **rw)
        self.tt("dve", R[:, 5, :], R[:, 3, :], lr, ALU.mult, **rw)
        self.tt("dve", R[:, 8, :], R[:, 7, :], li, ALU.mult, **rw)
        self.tt("dve", R[:, 5, :], R[:, 5, :], R[:, 8, :], ALU.subtract, **rw)
        self.tt("dve", R[:, 5, :], R[:, 5, :], R[:, 6, :], ALU.mult, **rw)
        self.tt("dve", R[:, 11, :], R[:, 4, :], R[:, 9, :], ALU.mult, **rw)
        self.tt("dve", R[:, 8, :], R[:, 5, :], R[:, 10, :], ALU.mult, **rw)
        self.tt("dve", R[:, 11, :], R[:, 11, :], R[:, 8, :], ALU.add, **rw)
        self.tt("dve", R[:, 12, :], R[:, 5, :], R[:, 9, :], ALU.mult, **rw)
        self.tt("dve", R[:, 8, :], R[:, 4, :], R[:, 10, :], ALU.mult, **rw)
        self.tt("dve", R[:, 12, :], R[:, 12, :], R[:, 8, :], ALU.subtract, **rw)
        if self.pn == "S":
            H = self.s5h
            self.load(H[:], self.d["s5h0"][:, e], w=["s5h"])
            self.tt("dve", R[:, 6, :], R[:, 4, :], R[:, 4, :], ALU.mult, **rw)
            self.tt("dve", R[:, 7, :], R[:, 5, :], R[:, 5, :], ALU.mult, **rw)
            self.tt("dve", R[:, 6, :], R[:, 6, :], R[:, 7, :], ALU.add, **rw)
            self.P.op("dve", lambda en: en.reciprocal(out=R[:, 6, :], in_=R[:, 6, :]), reads=[K], writes=[K])
            self.tt("dve", R[:, 7, :], R[:, 9, :], R[:, 4, :], ALU.mult, **rw)
            self.tt("dve", R[:, 15, :], R[:, 10, :], R[:, 5, :], ALU.mult, **rw)
            self.tt("dve", R[:, 7, :], R[:, 7, :], R[:, 15, :], ALU.add, **rw)
            self.tt("dve", R[:, 7, :], R[:, 7, :], R[:, 6, :], ALU.mult, **rw)
            self.tt("dve", R[:, 8, :], R[:, 10, :], R[:, 4, :], ALU.mult, **rw)
            self.tt("dve", R[:, 15, :], R[:, 9, :], R[:, 5, :], ALU.mult, **rw)
            self.tt("dve", R[:, 8, :], R[:, 8, :], R[:, 15, :], ALU.subtract, **rw)
            self.tt("dve", R[:, 8, :], R[:, 8, :], R[:, 6, :], ALU.mult, **rw)
            Gr = R[:, 7, :].rearrange("p (d s) -> p d s", d=2)
            Gi = R[:, 8, :].rearrange("p (d s) -> p d s", d=2)
            t6 = R[:, 6, :].rearrange("p (d s) -> p d s", d=2)
            t15 = R[:, 15, :].rearrange("p (d s) -> p d s", d=2)
            rw2 = dict(r=[K, "s5h"], w=[K])
            self.tt("dve", t6, Gr, H[:, :, 0, :], ALU.mult, **rw2)
            self.tt("dve", t15, Gi, H[:, :, 1, :], ALU.mult, **rw2)
            self.tt("dve", t6, t6, t15, ALU.subtract, **rw2)
            self.tt("dve", t15, Gr, H[:, :, 1, :], ALU.mult, **rw2)
            self.tt("dve", Gr, Gi, H[:, :, 0, :], ALU.mult, **rw2)
            self.tt("dve", H[:, :, 1, :], t15, Gr, ALU.add, r=[K, "s5h"], w=["s5h"])
            self.cp("dve", H[:, :, 0, :], t6, r=[K, "s5h"], w=["s5h"])

    def s5_mixer(self, l):
        e = l // 2
        T, L, nseq, NB = self.T, self.L, self.nseq, self.NB
        V, R = self.vecs, self.s5p
        nch = T // TC
        cps = L // TC
        self.s5_params(e)
        self.P.barrier()
        A = self.carve
        costab = A(0, 1024).rearrange("p (i j) -> p i j", i=8)
        sintab = A(1024, 1024).rearrange("p (i j) -> p i j", i=8)
        rhotab = A(2048, 1024).rearrange("p (i j) -> p i j", i=8)
        argt = A(3072, 1024)
        kt_ = A(4096, 1024)
        rt_ = A(5120, 1024)
        slot = []
        for i in range(4):
            base = 3072 + i * 1280
            slot.append(dict(t=[A(base + q * 128, 128) for q in range(6)],
                             g2=A(base + 768, 256).rearrange("p (r t) -> p r t", r=2),
                             pr=[A(base + 1024 + q * 64, 128, BF16) for q in range(4)]))
        u_bf = A(8192, 2048, BF16)
        y_acc = A(9216, 2048)
        ctp = A(11264, 3072, BF16).rearrange("p (a d r c) -> p a d r c", a=4, d=2, r=3)
        bT = A(12800, 1024, BF16).rearrange("p (a r c) -> p a r c", a=4, r=2)
        cstage = A(4352, 1024).rearrange("p (a r c) -> p a r c", a=4, r=2)
        bstage = A(3072, 1024).rearrange("p (a r c) -> p a r c", a=4, r=2)
        cin = self.lcst[:, 0:8].rearrange("p (a r) -> p a r", a=4)
        ctmp = self.lcst[:, 8:16].rearrange("p (a r) -> p a r", a=4)
        eis = self.lh0[:, :, :].rearrange("p d (a r) -> p d a r", a=4)
        ep = [A(3072 + i * 512, 512) for i in range(2)]
        for ut in range(4):
            self.load(cstage, self.d["s5c"][e][:, 4 * ut:4 * ut + 4], w=["cstage"])
            self.load(bstage, self.d["s5b"][e][:, ut], w=["bstage"])
            self.cp("pool", bT, bstage, r=["bstage"], w=["bT"])
            for dd in range(2):
                for sl in range(4):
                    col = dd * 16 + 4 * ut + sl
                    fr, fi = R[:, 4, col:col + 1], R[:, 5, col:col + 1]
                    cre, cim = cstage[:, sl, 0, :], cstage[:, sl, 1, :]
                    tA, tB = slot[3]["t"][0], slot[3]["t"][1]
                    self.ts("pool", tA, cim, fi, ALU.mult, r=["cstage", "s5p"], w=["tA"])
                    self.stt(ctp[:, sl, dd, 0, :], cre, fr, tA, ALU.mult, ALU.subtract, r=["cstage", "s5p", "tA"], w=["ctp"])
                    self.ts("pool", tB, cim, fr, ALU.mult, r=["cstage", "s5p"], w=["tB"])
                    self.stt(tB, cre, fi, tB, ALU.mult, ALU.add, r=["cstage", "s5p", "tB"], w=["tB"])
                    self.ts("dve", ctp[:, sl, dd, 2, :], tB, -1.0, ALU.mult, r=["tB"], w=["ctp"])
                    self.ts("dve", ctp[:, sl, dd, 1, :], ctp[:, sl, dd, 0, :], -1.0, ALU.mult, r=["ctp"], w=["ctp"])
                    Ei_c = R[:, 14, col:col + 1]
                    self.ts("dve", eis[:, dd, sl, 0:1], Ei_c, -1.0, ALU.mult, r=["s5p"], w=["eis"])
                    self.cp("dve", eis[:, dd, sl, 1:2], Ei_c, r=["s5p"], w=["eis"])
            self.P.barrier()
            for dd in range(2):
                for sl in range(4):
                    col = dd * 16 + 4 * ut + sl
                    i = dd * 4 + sl
                    self.ts("dve", argt[:, i * 128:(i + 1) * 128], self.jidx, R[:, 0, col:col + 1], ALU.mult,
                            r=["cst", "s5p"], w=[("tab", "arg")])
                    self.ts("pool", rhotab[:, i, :], self.ones, R[:, 1, col:col + 1], ALU.mult, r=["cst", "s5p"], w=["rhotab"])
            self.sincos(argt, sintab[:, :, :].rearrange("p i j -> p (i j)"), costab[:, :, :].rearrange("p i j -> p (i j)"), kt_, rt_, "tab")
            def cons(b, ps, key):
                self.cp("act", u_bf[:, b * 512:(b + 1) * 512], ps, r=[key], w=["u_bf"])
            self.proj_full(self.d["win_e"][e, ut], cons)
            self.P.barrier()
            LA = 2
            for dd in range(2):
                order = list(range(nch)) if dd == 0 else list(range(nch - 1, -1, -1))
                tabk = [("tab", "sin"), ("tab", "cos")]

                def stageA(ci, c, sl, dd=dd):
                    tk = slice(c * TC, (c + 1) * TC)
                    if sl == 0:
                        for s2_ in range(4):
                            bu = self.pb[s2_][:, 0:256].rearrange("p (r t) -> p r t", r=2)
                            for ri in range(2):
                                self.mm(bu[:, ri, :], bT[:, s2_, ri, :], u_bf[:, tk], True, True,
                                        r=["bT", "u_bf"], w=[("bu", s2_, ri)])
                    reg = sl
                    i = dd * 4 + sl
                    S = slot[sl]
                    sk = lambda n, sl=sl: ("slot", sl, n)
                    bu = self.pb[reg][:, 0:256].rearrange("p (r t) -> p r t", r=2)
                    bre_p, bim_p = bu[:, 0, :], bu[:, 1, :]
                    if dd == 1:
                        bre_p, bim_p = self.rev(bre_p), self.rev(bim_p)
                    t1, t2, t3, t4, bre, bim = S["t"]
                    cs, sn = costab[:, i, :], sintab[:, i, :]
                    self.tt("dve", t1, bre_p, cs, ALU.mult, r=[("bu", reg, 0)] + tabk, w=[sk("t1")])
                    self.tt("dve", t2, bim_p, sn, ALU.mult, r=[("bu", reg, 1)] + tabk, w=[sk("t2")])
                    self.tt("pool", bre, t1, t2, ALU.add, r=[sk("t1"), sk("t2")], w=[sk("bre")])
                    self.tt("dve", t3, bim_p, cs, ALU.mult, r=[("bu", reg, 1)] + tabk, w=[sk("t3")])
                    self.tt("dve", t4, bre_p, sn, ALU.mult, r=[("bu", reg, 0)] + tabk, w=[sk("t4")])
                    self.tt("pool", bim, t3, t4, ALU.subtract, r=[sk("t3"), sk("t4")], w=[sk("bim")])

                def stageB(ci, c, sl, dd=dd):
                    tk = slice(c * TC, (c + 1) * TC)
                    seq = c // cps
                    first = (c % cps == 0) if dd == 0 else (c % cps == cps - 1)
                    last = (c % cps == cps - 1) if dd == 0 else (c % cps == 0)
                    ypk = ("yps", ci % 2)
                    yps = self.pb[4 + ci % 2][:, 0:TC]
                    i = dd * 4 + sl
                    col = dd * 16 + 4 * ut + sl
                    S = slot[sl]
                    sk = lambda n, sl=sl: ("slot", sl, n)
                    t1, t2, t3, t4, bre, bim = S["t"]
                    g2 = S["g2"]
                    gre, gim = g2[:, 0, :], g2[:, 1, :]
                    cs, sn, rh = costab[:, i, :], sintab[:, i, :], rhotab[:, i, :]
                    if first:
                        if self.pn == "S":
                            self.cp("pool", cin[:, sl, :], self.s5h[:, dd, :, 4 * ut + sl], r=["s5h"], w=[("cin", sl)])
                        else:
                            self.memset("pool", cin[:, sl, :], 0.0, w=[("cin", sl)])
                    for (g_, b_, ri) in ((gre, bre, 0), (gim, bim, 1)):
                        self.P.op("dve", lambda en, g_=g_, b_=b_, rh=rh, ini=cin[:, sl, ri:ri + 1]: en.tensor_tensor_scan(
                            out=g_, data0=rh, data1=b_, initial=ini, op0=ALU.mult, op1=ALU.add),
                            reads=[sk("bre" if ri == 0 else "bim"), "rhotab", ("cin", sl)], writes=[sk("g2")])
                    Er = R[:, 13, col:col + 1]
                    gl = g2[:, :, TC - 1]
                    (gps, gpn), (gfs, gfn) = gl.ap
                    gl_sw = AP(gl.tensor, gl.offset + gfs, [[gps, gpn], [-gfs, 2]])
                    gk = [sk("g2"), "s5p", "eis"]
                    self.tt("dve", ctmp[:, sl, :], gl_sw, eis[:, dd, sl, :], ALU.mult, r=gk, w=[("ctmp", sl)])
                    self.stt(cin[:, sl, :], gl, Er, ctmp[:, sl, :], ALU.mult, ALU.add, r=gk + [("ctmp", sl)], w=[("cin", sl)])
                    if last and self.pn == "P":
                        F2r, F2i = R[:, 11, col:col + 1], R[:, 12, col:col + 1]
                        s_ = 4 * ut + sl
                        o_re = self.fin5[:, seq, dd, 0, s_:s_ + 1]
                        o_im = self.fin5[:, seq, dd, 1, s_:s_ + 1]
                        ck = [("cin", sl), "s5p"]
                        self.ts("pool", ctmp[:, sl, 0:1], cin[:, sl, 1:2], F2i, ALU.mult, r=ck, w=[("ctmp", sl)])
                        self.ts("pool", ctmp[:, sl, 1:2], cin[:, sl, 0:1], F2i, ALU.mult, r=ck, w=[("ctmp", sl)])
                        self.stt(o_re, cin[:, sl, 0:1], F2r, ctmp[:, sl, 0:1], ALU.mult, ALU.subtract, r=ck + [("ctmp", sl)], w=["fin5"])
                        self.stt(o_im, cin[:, sl, 1:2], F2r, ctmp[:, sl, 1:2], ALU.mult, ALU.add, r=ck + [("ctmp", sl)], w=["fin5"])
                    prs = S["pr"]
                    prs_o = prs if dd == 0 else [self.rev(p_) for p_ in prs]
                    for q, (gg, tab, var) in enumerate(((gre, cs, 0), (gim, sn, 1), (gre, sn, 2), (gim, cs, 2))):
                        self.tt("pool", prs_o[q], gg, tab, ALU.mult, r=[sk("g2")] + tabk, w=[sk("pr%d" % q)])
                        self.mm(yps, ctp[:, sl, dd, var, :], prs[q], sl == 0 and q == 0, sl == 3 and q == 3,
                                r=["ctp", sk("pr%d" % q)], w=[ypk])
                    if sl == 3:
                        if dd == 0:
                            self.cp("act", y_acc[:, tk], yps, r=[ypk], w=[("y_acc", c)])
                        else:
                            self.tt("dve", y_acc[:, tk], y_acc[:, tk], yps, ALU.add, r=[ypk, ("y_acc", c)], w=[("y_acc", c)])

                units = [(ci, c, sl) for ci, c in enumerate(order) for sl in range(4)]
                for step in range(len(units) + LA):
                    if step < len(units):
                        stageA(*units[step])
                    if step - LA >= 0:
                        stageB(*units[step - LA])
            self.P.barrier()
            yk = [("y_acc", c) for c in range(nch)]

            def cons_z(b, ps, key, ut=ut):
                bs = slice(b * 512, (b + 1) * 512)
                self.act(ep[0], ps, AF.Sigmoid, r=[key], w=["ep0"])
                self.stt(ep[1], u_bf[:, bs], V[:, vcol("s5_d%d" % e, ut, 1)], y_acc[:, bs], ALU.mult, ALU.add,
                         r=["u_bf", "vecs"] + yk, w=["ep1"])
                self.act(ep[1], ep[1], AF.Gelu_apprx_tanh, r=["ep1"], w=["ep1"])
                self.tt("dve", self.yb[:, ut, bs], ep[1], ep[0], ALU.mult, r=["ep0", "ep1"], w=[("yb", b)])
            self.proj_full(self.d["win_e"][e, 4 + ut], cons_z)
            self.P.barrier()
        if self.pn == "P":
            for seq in range(4):
                for dd in range(2):
                    for ri in range(2):
                        self.store(self.d["o_s5"][seq, e, dd, ri], self.fin5[:, seq, dd, ri, :], r=["fin5"])

    def bmid(self, ap2, n):
        (ps, pn_), (fs, fn) = ap2.ap
        return AP(ap2.tensor, ap2.offset, [[ps, pn_], [0, n], [fs, fn]])

    def gdn_mixer(self, l):
        e = l // 2
        T, L, nseq, NB = self.T, self.L, self.nseq, self.NB
        V = self.vecs
        A = self.carve
        GC = 128
        NLV = 7
        nchk = T // GC
        cps = L // GC
        tb = 10240
        abtok = A(tb, 256).rearrange("p (n c) -> p n c", c=16)[:, 0:nchk, :]
        def tok8(off):
            return A(tb + off, 128).rearrange("p (n c) -> p n c", c=8)[:, 0:nchk, :]
        gtok, btok, gam, beg, kd, eglast = tok8(256), tok8(384), tok8(512), tok8(640), tok8(768), tok8(896)
        nexp = A(tb + 1024, 8)
        wv, wk = self.slab(self.d["win_e"][e, 24])
        w3 = wv.rearrange("p (k c) -> p k c", k=8)
        pq = self.pb[6]
        for n in range(nchk):
            for kt in range(8):
                self.mm(pq[:, n * 16:(n + 1) * 16], self.h[:, kt, n * GC:(n + 1) * GC], w3[:, kt, 0:16], kt == 0, kt == 7,
                        r=[wk, ("h", n // 4)], w=["pq"])
        self.cp("act", abtok, pq[:, 0:nchk * 16].rearrange("p (n c) -> p n c", c=16), r=["pq"], w=["abtok"])
        K = "gtokk"
        rw = dict(r=["abtok", "vecs", K], w=[K])
        self.tt("dve", gtok, abtok[:, :, 0:8], self.bmid(V[:, vcol("gdn_dtb%d" % e, 0, 8)], nchk), ALU.add, **rw)
        self.act(gtok, gtok, AF.Exp, **rw)
        self.act(gtok, gtok, AF.Ln, bias=self.onec[:, 0:1], **rw)
        self.act(nexp, V[:, vcol("gdn_alog%d" % e, 0, 8)], AF.Exp, **rw)
        self.ts("dve", nexp, nexp, -1.0, ALU.mult, **rw)
        self.tt("dve", gtok, gtok, self.bmid(nexp, nchk), ALU.mult, **rw)
        self.act(btok, abtok[:, :, 8:16], AF.Sigmoid, **rw)
        self.P.barrier()
        gdir = A(tb + 1032, 128)
        for dd, tri in ((0, self.m_le), (1, self.m_ge)):
            gd = gdir[:, dd * 64:dd * 64 + nchk * 4]
            self.cp("dve", gd.rearrange("p (n c) -> p n c", c=4), gtok[:, :, dd * 4:(dd + 1) * 4], r=[K], w=[("gdir", dd)])
            self.mm(pq[:, dd * 64:dd * 64 + nchk * 4], tri, gd, True, True, r=[("gdir", dd), "cst"], w=["pq"])
        g2d = A(tb + 256, nchk * 8)
        pl2 = pq[:, 256:256 + nchk * 8]
        self.mm(pl2, self.ones, g2d, True, True, r=[K, "cst"], w=["pq2"])
        pl = pl2.rearrange("p (n c) -> p n c", c=8)
        for dd in range(2):
            self.cp("act", gam[:, :, dd * 4:(dd + 1) * 4], pq[:, dd * 64:dd * 64 + nchk * 4].rearrange("p (n c) -> p n c", c=4),
                    r=["pq"], w=[K])
        self.act(eglast, pl, AF.Exp, r=["pq2"], w=[K])
        self.tt("dve", kd, pl, gam, ALU.subtract, r=["pq2", K], w=[K])
        self.act(kd, kd, AF.Exp, **rw)
        self.act(beg, gam, AF.Exp, **rw)
        self.tt("dve", beg, beg, btok, ALU.mult, **rw)
        self.P.barrier()
        bmk = A(11400, 2 * NLV * 128)
        self.load(bmk, self.d["bmk"], w=["bmk"])
        bml = bmk[:, 0:NLV * 128].rearrange("p (m c) -> p m c", m=NLV)
        bmu = bmk[:, NLV * 128:2 * NLV * 128].rearrange("p (m c) -> p m c", m=NLV)
        q_bf, k_bf, v_bf = A(0, 2048, BF16), A(1024, 2048, BF16), A(2048, 2048, BF16)
        craw, cout = A(3072, 2048), A(5120, 2048)
        o_acc = A(3072, 2048)
        I128 = self.ident
        for hd in range(4):
            for which, (dst, tile0) in enumerate(((q_bf, 8), (k_bf, 12), (v_bf, 16))):
                ci = which * 4 + hd

                def cons(b, ps, key):
                    self.cp("act", craw[:, b * 512:(b + 1) * 512], ps, r=[key], w=["craw"])
                self.proj_full(self.d["win_e"][e, tile0 + hd], cons)
                cw = lambda j, ci=ci: V[:, vcol("gdn_conv_w%d" % e, j * 12 + ci, 1)]
                co = cout[:, 0:T]
                self.act(co, craw[:, 0:T], AF.Identity, r=["craw", "vecs"], w=["cout"],
                         bias=V[:, vcol("gdn_conv_b%d" % e, ci, 1)], scale=cw(1))
                r3 = craw[:, 0:T].rearrange("p (s l) -> p s l", s=nseq)
                x3 = co.rearrange("p (s l) -> p s l", s=nseq)
                kk = dict(r=["craw", "cout", "vecs"], w=["cout"])
                self.stt(x3[:, :, 1:L], r3[:, :, 0:L - 1], cw(0), x3[:, :, 1:L], ALU.mult, ALU.add, **kk)
                self.stt(x3[:, :, 0:L - 1], r3[:, :, 1:L], cw(2), x3[:, :, 0:L - 1], ALU.mult, ALU.add, **kk)
                self.stt(x3[:, :, 0:L - 2], r3[:, :, 2:L], cw(3), x3[:, :, 0:L - 2], ALU.mult, ALU.add, **kk)
                self.act(co, co, AF.Silu, r=["cout"], w=["cout"])
                if which == 2:
                    self.cp("pool", dst[:, 0:T], co, r=["cout"], w=["qkv"])
                else:
                    for b in range(NB):
                        bs = slice(b * 512, (b + 1) * 512)
                        sq = A(7168, 512, BF16)
                        rs = A(7424, 512)
                        self.act(sq, cout[:, bs], AF.Square, r=["cout"], w=["gsq"])
                        self.mm(self.pb[7][:, :], self.ones_bf, sq, True, True, r=["gsq", "cbf"], w=[("pb", 7)])
                        self.act(rs, self.pb[7][:, :], AF.Sqrt, r=[("pb", 7)], w=["grs"], bias=self.epsc[:, 0:1])
                        self.P.op("dve", lambda en, rs=rs: en.reciprocal(out=rs, in_=rs), reads=["grs"], writes=["grs"])
                        self.stt(dst[:, bs], cout[:, bs], (128.0 ** -0.5) if which == 0 else 1.0, rs, ALU.mult, ALU.mult,
                                 r=["cout", "grs"], w=["qkv"])
            self.P.barrier()
            self.memset("pool", o_acc[:, 0:T], 0.0, w=[("o_acc", n) for n in range(nchk)])
            ST = []
            for st in range(2):
                base = 5120 + st * 2560
                o = [base]

                def nx(n, dt=F32, o=o):
                    ap = A(o[0], n, dt)
                    o[0] += n if dt == F32 else (n + 1) // 2
                    return ap
                ST.append(dict(gtri=nx(128), dce=nx(128), dec=nx(128), A0=nx(128), Ncm=nx(128), B0=nx(128), Nem=nx(128),
                               Nem2=nx(128),
                               T=nx(128), TT=nx(128), M1=nx(128), qk=nx(128, BF16), X0=nx(256), X1=nx(256),
                               wmt=nx(128, BF16), vn=nx(128, BF16), kdb=nx(128, BF16), eg=nx(128), qg=nx(128, BF16),
                               S=nx(128), Sb=nx(128, BF16)))
                assert o[0] - base <= 2560

            def unit(st, n, hd=hd):
                dd = st
                Sx = ST[st]
                col = dd * 4 + hd
                tk = slice(n * GC, (n + 1) * GC)
                bA, bB, bC, bD = self.pb[st * 4], self.pb[st * 4 + 1], self.pb[st * 4 + 2], self.pb[st * 4 + 3]
                k_ = lambda name: ("g", st, name)
                tri = self.m_le if dd == 0 else self.m_ge
                negs = self.m_ngt if dd == 0 else self.m_nlt
                incl = self.m_le if dd == 0 else self.m_ge
                g_c, b_c, ga_c = gtok[:, n, col:col + 1], btok[:, n, col:col + 1], gam[:, n, col:col + 1]
                beg_c, kd_c, egl_c = beg[:, n, col:col + 1], kd[:, n, col:col + 1], eglast[:, n, col:col + 1]
                raw_ce, rawq_ec, Gam, ktok = bA[:, 0:128], bA[:, 128:256], bA[:, 256:384], bA[:, 384:512]
                vtok, xp, wmt_p = bB[:, 0:128], bB[:, 128:384], bB[:, 384:512]
                vn_p, ot_p, s_p, b0_p = bC[:, 0:128], bC[:, 128:256], bC[:, 256:384], bC[:, 384:512]
                m1_p, m2_p, m2t_p = bD[:, 0:128], bD[:, 128:256], bD[:, 256:384]
                A0, B0, Ncm, Nem, Tm, TTm, M1s = Sx["A0"], Sx["B0"], Sx["Ncm"], Sx["Nem"], Sx["T"], Sx["TT"], Sx["M1"]
                X0, X1 = Sx["X0"], Sx["X1"]
                mce = lambda m: (bml if dd == 0 else bmu)[:, m, :]
                mec = lambda m: (bmu if dd == 0 else bml)[:, m, :]
                stages = []

                def s1():
                    self.ts("pool", Sx["gtri"], tri, g_c, ALU.mult, r=["cst", K], w=[k_("gtri")])
                    self.mm(Gam, self.ones, Sx["gtri"], True, True, r=["cst", k_("gtri")], w=[k_("Gam")])
                    self.mm(raw_ce, k_bf[:, tk], k_bf[:, tk], True, True, r=["qkv"], w=[k_("raw_ce")])
                    self.mm(rawq_ec, k_bf[:, tk], q_bf[:, tk], True, True, r=["qkv"], w=[k_("rawq")])
                    self.mm(ktok, k_bf[:, tk], self.ident_bf, True, True, r=["qkv", "cbf"], w=[k_("ktok")])
                    self.mm(vtok, v_bf[:, tk], self.ident_bf, True, True, r=["qkv", "cbf"], w=[k_("vtok")])
                stages.append(s1)

                def s2():
                    self.ts("dve", Sx["dce"], Gam, ga_c, ALU.subtract, r=[k_("Gam"), K], w=[k_("dce")], s2=0.0, op1=ALU.max)
                    self.act(Sx["dce"], Sx["dce"], AF.Exp, r=[k_("dce")], w=[k_("dce")], scale=-1.0)
                    self.ts("dve", Sx["dec"], Gam, ga_c, ALU.subtract, r=[k_("Gam"), K], w=[k_("dec")], s2=0.0, op1=ALU.min)
                    self.act(Sx["dec"], Sx["dec"], AF.Exp, r=[k_("dec")], w=[k_("dec")])
                    self.tt("pool", Sx["dce"], Sx["dce"], negs, ALU.mult, r=[k_("dce"), "cst"], w=[k_("dce")])
                    self.tt("pool", Sx["dec"], Sx["dec"], incl, ALU.mult, r=[k_("dec"), "cst"], w=[k_("dec")])
                    self.act(Sx["eg"], Gam, AF.Exp, r=[k_("Gam")], w=[k_("eg")])
                    self.tt("dve", Sx["qg"], q_bf[:, tk], Sx["eg"], ALU.mult, r=["qkv", k_("eg")], w=[k_("qg")])
                stages.append(s2)

                def s3():
                    self.stt(A0, raw_ce, b_c, Sx["dce"], ALU.mult, ALU.mult, r=[k_("raw_ce"), K, k_("dce")], w=[k_("A0")])
                    self.tt("dve", Sx["qk"], rawq_ec, Sx["dec"], ALU.mult, r=[k_("rawq"), k_("dec")], w=[k_("qk")])
                    self.tr(b0_p, A0, I128, r=[k_("A0"), "cst"], w=[k_("b0p")])
                    self.cp("act", B0, b0_p, r=[k_("b0p")], w=[k_("B0")])
                    self.act(X0[:, 0:128], vtok, AF.Identity, r=[k_("vtok"), K], w=[k_("X0")], scale=b_c)
                    self.act(X0[:, 128:256], ktok, AF.Identity, r=[k_("ktok"), K], w=[k_("X0")], scale=beg_c)
                    self.act(Sx["kdb"], ktok, AF.Identity, r=[k_("ktok"), K], w=[k_("kdb")], scale=kd_c)
                stages.append(s3)

                NemB = [Nem, Sx["Nem2"]]

                def sl0():
                    self.tt("pool", Ncm, A0, mce(0), ALU.mult, r=[k_("A0"), "bmk"], w=[k_("Ncm")])
                    self.tt("pool", NemB[0], B0, mec(0), ALU.mult, r=[k_("B0"), "bmk"], w=[k_("Nem0")])
                    self.tt("dve", Tm, Ncm, I128, ALU.add, r=[k_("Ncm"), "cst"], w=[k_("T")])
                    self.tt("dve", TTm, NemB[0], I128, ALU.add, r=[k_("Nem0"), "cst"], w=[k_("TT")])
                    self.tt("pool", NemB[1], B0, mec(1), ALU.mult, r=[k_("B0"), "bmk"], w=[k_("Nem1")])
                stages.append(sl0)
                for m in range(1, NLV):
                    def sla(m=m):
                        cur = NemB[m % 2]
                        self.mm(m1_p, cur, Tm, True, True, r=[k_("Nem%d" % (m % 2)), k_("T")], w=[k_("m1p")])
                        self.cp("act", M1s, m1_p, r=[k_("m1p")], w=[k_("M1")])
                        if m + 1 < NLV:
                            self.tt("pool", NemB[(m + 1) % 2], B0, mec(m + 1), ALU.mult, r=[k_("B0"), "bmk"],
                                    w=[k_("Nem%d" % ((m + 1) % 2))])

                    def slb(m=m):
                        self.mm(m2_p, TTm, M1s, True, True, r=[k_("TT"), k_("M1")], w=[k_("m2p")])
                        self.mm(m2t_p, M1s, TTm, True, True, r=[k_("TT"), k_("M1")], w=[k_("m2tp")])
                        self.tt("dve", Tm, Tm, m2_p, ALU.add, r=[k_("T"), k_("m2p")], w=[k_("T")])
                        self.tt("dve", TTm, TTm, m2t_p, ALU.add, r=[k_("TT"), k_("m2tp")], w=[k_("TT")])
                    stages.append(sla)
                    stages.append(slb)

                def sx():
                    self.mm(xp, TTm, X0, True, True, r=[k_("TT"), k_("X0")], w=[k_("xp")])
                    self.cp("act", X1, xp, r=[k_("xp")], w=[k_("X1")])
                stages.append(sx)

                def s4():
                    self.tr(wmt_p, X1[:, 128:256], I128, r=[k_("X1"), "cst"], w=[k_("wmtp")])
                    self.cp("act", Sx["wmt"], wmt_p, r=[k_("wmtp")], w=[k_("wmt")])
                    self.mm(vn_p, Sx["wmt"], Sx["Sb"], True, True, r=[k_("wmt"), k_("Sb")], w=[k_("vnp")])
                    self.tt("dve", Sx["vn"], X1[:, 0:128], vn_p, ALU.subtract, r=[k_("X1"), k_("vnp")], w=[k_("vn")])
                stages.append(s4)

                def s5():
                    self.mm(ot_p, Sx["Sb"], Sx["qg"], True, False, r=[k_("Sb"), k_("qg")], w=[k_("otp")])
                    self.mm(ot_p, Sx["vn"], Sx["qk"], False, True, r=[k_("vn"), k_("qk")], w=[k_("otp")])
                    self.mm(s_p, Sx["kdb"], Sx["vn"], True, True, r=[k_("kdb"), k_("vn")], w=[k_("sp")])
                    self.tt("dve", o_acc[:, tk], o_acc[:, tk], ot_p, ALU.add, r=[k_("otp"), ("o_acc", n)], w=[("o_acc", n)])
                    self.stt(Sx["S"], Sx["S"], egl_c, s_p, ALU.mult, ALU.add, r=[k_("S"), K, k_("sp")], w=[k_("S")])
                    self.cp("pool", Sx["Sb"], Sx["S"], r=[k_("S")], w=[k_("Sb")])
                stages.append(s5)
                return stages

            for seq in range(nseq):
                for st in range(2):
                    Sx = ST[st]
                    if self.pn == "S":
                        self.load(Sx["S"], self.d["dl0"][:, e, st, hd, :], w=[("g", st, "S")])
                    else:
                        self.memset("pool", Sx["S"], 0.0, w=[("g", st, "S")])
                    self.cp("pool", Sx["Sb"], Sx["S"], r=[("g", st, "S")], w=[("g", st, "Sb")])
                for i in range(cps):
                    units = [unit(0, seq * cps + i), unit(1, seq * cps + cps - 1 - i)]
                    for sg in range(len(units[0])):
                        for st in range(2):
                            units[st][sg]()
                if self.pn == "P":
                    for st in range(2):
                        self.store(self.d["o_dl"][seq, e, st, hd], ST[st]["S"], r=[("g", st, "S")])
            self.P.barrier()
            okeys = [("o_acc", n) for n in range(nchk)]
            et = [A(5120 + i * 512, 512) for i in range(3)]

            def cons_z(b, ps, key, hd=hd):
                bs = slice(b * 512, (b + 1) * 512)
                sq = A(7168, 512, BF16)
                self.act(et[0], ps, AF.Silu, r=[key], w=["et0"])
                self.act(sq, o_acc[:, bs], AF.Square, r=okeys, w=["gsq"])
                self.mm(self.pb[7][:, :], self.ones_bf, sq, True, True, r=["gsq", "cbf"], w=[("pb", 7)])
                self.act(et[1], self.pb[7][:, :], AF.Sqrt, r=[("pb", 7)], w=["et1"], bias=self.epsc[:, 0:1], scale=1.0 / 128)
                self.P.op("dve", lambda en: en.reciprocal(out=et[1], in_=et[1]), reads=["et1"], writes=["et1"])
                self.stt(et[2], o_acc[:, bs], V[:, vcol("gdn_onorm%d" % e, 0, 1)], et[1], ALU.mult, ALU.mult,
                         r=okeys + ["et1", "vecs"], w=["et2"])
                self.tt("dve", self.yb[:, 4 + hd, bs], et[2], et[0], ALU.mult, r=["et0", "et2"], w=[("yb", b)])
            self.proj_full(self.d["win_e"][e, 20 + hd], cons_z)
            self.P.barrier()

    def odd_mixer(self, l):
        o = l // 2
        T, L, nseq, NB = self.T, self.L, self.nseq, self.NB
        V = self.vecs
        xc = self.carve(0, 4096).rearrange("p (i t) -> p i t", i=2)
        xcb = self.carve(4096, 4096, BF16).rearrange("p (i t) -> p i t", i=2)
        hs = self.carve(6144, 2048)
        tmp = [self.carve(8192 + i * 512, 512) for i in range(6)]
        t_ra, t_i, t_a2, t_b = tmp[0], tmp[1], tmp[2], tmp[3]
        t_h = [tmp[4], tmp[5]]
        cst = self.lcst
        self.act(cst[:, :], V[:, vcol("lru_lam%d" % o, 0, 16)], AF.Exp, r=["vecs"], w=["lcst"], scale=-1.0)
        self.act(cst[:, :], cst[:, :], AF.Ln, r=["lcst"], w=["lcst"], bias=self.onec[:, 0:1])
        self.ts("dve", cst[:, :], cst[:, :], -LRU_C, ALU.mult, r=["lcst"], w=["lcst"])
        if self.pn == "S":
            self.load(self.lh0[:], self.d["lru0"][:, o], w=["lh0"])
        for n in range(4):
            for ti in range(2):
                ft = 2 * n + ti
                raw = hs

                def cons(b, ps, key, raw=raw):
                    self.cp("act", raw[:, b * 512:(b + 1) * 512], ps, r=[key], w=["hs"])
                self.proj_full(self.d["win_o"][o, ft], cons)
                cw = lambda j, ft=ft: V[:, vcol("lru_conv_w%d" % o, j * 8 + ft, 1)]
                xci = xc[:, ti, 0:T]
                self.act(xci, raw[:, 0:T], AF.Identity, r=["hs", "vecs"], w=[("xc", ti)],
                         bias=V[:, vcol("lru_conv_b%d" % o, ft, 1)], scale=cw(1))
                r3 = raw[:, 0:T].rearrange("p (s l) -> p s l", s=nseq)
                x3 = xci.rearrange("p (s l) -> p s l", s=nseq)
                self.stt(x3[:, :, 1:L], r3[:, :, 0:L - 1], cw(0), x3[:, :, 1:L], ALU.mult, ALU.add, r=["hs", ("xc", ti), "vecs"], w=[("xc", ti)])
                self.stt(x3[:, :, 0:L - 1], r3[:, :, 1:L], cw(2), x3[:, :, 0:L - 1], ALU.mult, ALU.add, r=["hs", ("xc", ti), "vecs"], w=[("xc", ti)])
                self.stt(x3[:, :, 0:L - 2], r3[:, :, 2:L], cw(3), x3[:, :, 0:L - 2], ALU.mult, ALU.add, r=["hs", ("xc", ti), "vecs"], w=[("xc", ti)])
                self.cp("pool", xcb[:, ti, 0:T], xci, r=[("xc", ti)], w=[("xcb", ti)])
            for ti in range(2):
                jt = 2 * n + ti
                for dd in range(2):
                    wv, wk = self.slab(self.d["wlru"][o, dd, n])
                    w4 = wv.rearrange("p (g k c) -> p g k c", g=2, k=2)
                    blocks = list(range(NB)) if dd == 0 else list(range(NB - 1, -1, -1))
                    prev = None
                    for bi, b in enumerate(blocks):
                        bs = slice(b * 512, (b + 1) * 512)
                        pr, pi_ = self.pb[4], self.pb[5]
                        for g, pp in ((0, pr), (1, pi_)):
                            for kt in range(2):
                                self.mm(pp[:, :], w4[:, g, kt, ti * 128:(ti + 1) * 128], xcb[:, kt, bs], kt == 0, kt == 1,
                                        r=[wk, ("xcb", kt)], w=[("pb", 4 + g)])
                        self.act(t_ra, pr[:, :], AF.Sigmoid, r=[("pb", 4), "vecs"], w=["t_ra"], bias=V[:, vcol("lru_b_r%d" % o, dd * 8 + jt, 1)])
                        self.act(t_ra, t_ra, AF.Exp, r=["t_ra", "lcst"], w=["t_ra"], scale=cst[:, dd * 8 + jt:dd * 8 + jt + 1])
                        self.act(t_i, pi_[:, :], AF.Sigmoid, r=[("pb", 5), "vecs"], w=["t_i"], bias=V[:, vcol("lru_b_i%d" % o, dd * 8 + jt, 1)])
                        self.tt("pool", t_a2, t_ra, t_ra, ALU.mult, r=["t_ra"], w=["t_a2"])
                        self.act(t_a2, t_a2, AF.Sqrt, r=["t_a2"], w=["t_a2"], bias=self.onec[:, 0:1], scale=-1.0)
                        self.tt("dve", t_b, t_i, xc[:, ti, bs], ALU.mult, r=["t_i", ("xc", ti)], w=["t_b"])
                        self.tt("dve", t_b, t_b, t_a2, ALU.mult, r=["t_b", "t_a2"], w=["t_b"])
                        th = t_h[bi % 2]
                        thk = ("t_h", bi % 2)
                        nsub = max(1, 512 // L)
                        seglen = min(512, L)
                        for sg in (range(nsub) if dd == 0 else range(nsub - 1, -1, -1)):
                            lo = sg * seglen
                            tok0 = b * 512 + lo
                            seq = tok0 // L
                            first = (tok0 % L == 0) if dd == 0 else ((tok0 + seglen) % L == 0)
                            if first:
                                init = self.lh0[:, dd, jt:jt + 1] if self.pn == "S" else 0.0
                                ir = ["lh0"] if self.pn == "S" else []
                            else:
                                pth = t_h[(bi - 1) % 2]
                                init = pth[:, 511:512] if dd == 0 else pth[:, 0:1]
                                ir = [("t_h", (bi - 1) % 2)]
                            o_ap, a_ap, b_ap = th[:, lo:lo + seglen], t_ra[:, lo:lo + seglen], t_b[:, lo:lo + seglen]
                            if dd == 1:
                                o_ap, a_ap, b_ap = self.rev(o_ap), self.rev(a_ap), self.rev(b_ap)
                            self.P.op("dve", lambda e, o_ap=o_ap, a_ap=a_ap, b_ap=b_ap, init=init: e.tensor_tensor_scan(
                                out=o_ap, data0=a_ap, data1=b_ap, initial=init, op0=ALU.mult, op1=ALU.add),
                                reads=["t_ra", "t_b"] + ir, writes=[thk])
                            last = ((tok0 + seglen) % L == 0) if dd == 0 else (tok0 % L == 0)
                            if last and self.pn == "P":
                                col = lo + seglen - 1 if dd == 0 else lo
                                self.cp("pool", self.finl[:, seq, dd, jt:jt + 1], th[:, col:col + 1], r=[thk], w=["finl"])
                        if dd == 0:
                            self.cp("pool", hs[:, bs], th, r=[thk], w=["hs"])
                        else:
                            self.tt("pool", hs[:, bs], hs[:, bs], th, ALU.add, r=[thk, "hs"], w=["hs"])

                def cons2(b, ps, key, jt=jt):
                    self.act(t_i, ps, AF.Gelu_apprx_tanh, r=[key], w=["t_i"])
                    self.tt("dve", self.yb[:, jt, b * 512:(b + 1) * 512], t_i, hs[:, b * 512:(b + 1) * 512], ALU.mult,
                            r=["t_i", "hs"], w=[("yb", b)])
                self.proj_full(self.d["win_o"][o, 8 + jt], cons2)
        if self.pn == "P":
            for seq in range(4):
                for dd in range(2):
                    self.store(self.d["o_lru"][seq, o, dd], self.finl[:, seq, dd, :], r=["finl"])

    def rev(self, ap):
        (ps, pn_), (fs, fn) = ap.ap
        return AP(ap.tensor, ap.offset + fs * (fn - 1), [[ps, pn_], [-fs, fn]])


def _fm(v, nt):
    return np.ascontiguousarray(np.asarray(v, np.float32).reshape(nt, 128).T)


def _slabify(W):
    W = np.asarray(W, np.float32)
    K, N = W.shape
    Np = (N + 127) // 128 * 128
    if Np != N:
        W = np.concatenate([W, np.zeros((K, Np - N), np.float32)], axis=1)
    return np.ascontiguousarray(W.reshape(8, 128, Np // 128, 128).transpose(2, 1, 0, 3).reshape(Np // 128, 128, 1024))


def _s5_state_layout(a):
    a = np.asarray(a, np.float32)
    lead = a.shape[:-2]
    a = a.reshape(lead + (16, 2, 64))
    nl = len(lead)
    a = np.moveaxis(a, (nl + 1, nl + 2), (0, 1))
    return np.ascontiguousarray(a.reshape((128,) + lead + (16,)))


def _grid_sincos(n_tokens):
    rows = n_tokens // 64
    row = np.repeat(np.arange(rows, dtype=np.float32), 64)
    col = np.tile(np.arange(64, dtype=np.float32), rows)
    n_freq = D // 4
    omega = (np.float32(10000.0) ** (-np.arange(n_freq, dtype=np.float32) / np.float32(n_freq))).astype(np.float32)
    ar = row[:, None] * omega
    ac = col[:, None] * omega
    return np.concatenate([np.sin(ar), np.cos(ar), np.sin(ac), np.cos(ac)], axis=-1).astype(np.float32)


def _consts():
    r = np.arange(128)[:, None]
    q = np.arange(128)[None, :]
    mats = [np.eye(128), np.ones((128, 128)), r > q, r >= q, r < q, r <= q, -(r > q).astype(np.float32), q + 0 * r, -(r < q).astype(np.float32)]
    return np.ascontiguousarray(np.concatenate([np.asarray(m, np.float32) for m in mats], axis=1))


def prepare_shared(inp):
    g = lambda k: np.asarray(inp[k], np.float32)
    sh = {}
    vecs = np.zeros((128, NV), np.float32)

    def put(name, arr):
        o, n = VEC[name]
        assert arr.shape == (128, n), (name, arr.shape, n)
        vecs[:, o:o + n] = arr
    for l in range(4):
        put("n_mix_pre%d" % l, _fm(g("norm_mix_pre")[l], 8))
        put("n_mix_post%d" % l, _fm(g("norm_mix_post")[l], 8))
        put("n_mlp_pre%d" % l, _fm(g("norm_mlp_pre")[l], 8))
        put("n_mlp_post%d" % l, _fm(g("norm_mlp_post")[l], 8))
        put("b_ada%d" % l, _fm(g("b_ada")[l], 48))
    for e in range(2):
        put("s5_lam_re%d" % e, _s5_state_layout(g("s5_lam_re")[e]).reshape(128, 32))
        put("s5_lam_im%d" % e, _s5_state_layout(g("s5_lam_im")[e]).reshape(128, 32))
        ldt = np.broadcast_to(g("s5_log_dt")[e][:, :, None], (2, 32, 64))
        put("s5_dt%d" % e, _s5_state_layout(ldt).reshape(128, 32))
        put("s5_d%d" % e, _fm(g("s5_d")[e], 4))
        cw = g("gdn_conv_w")[e]
        put("gdn_conv_w%d" % e, np.concatenate([_fm(cw[j], 12) for j in range(4)], axis=1))
        put("gdn_conv_b%d" % e, _fm(g("gdn_conv_b")[e], 12))
        put("gdn_alog%d" % e, np.broadcast_to(g("gdn_a_log")[e].reshape(1, 8), (128, 8)))
        put("gdn_dtb%d" % e, np.broadcast_to(g("gdn_dt_bias")[e].reshape(1, 8), (128, 8)))
        put("gdn_onorm%d" % e, g("gdn_o_norm")[e].reshape(128, 1))
    for o in range(2):
        cw = g("lru_conv_w")[o]
        put("lru_conv_w%d" % o, np.concatenate([_fm(cw[j], 8) for j in range(4)], axis=1))
        put("lru_conv_b%d" % o, _fm(g("lru_conv_b")[o], 8))
        put("lru_b_r%d" % o, np.concatenate([_fm(g("lru_b_r")[o, dd], 8) for dd in range(2)], axis=1))
        put("lru_b_i%d" % o, np.concatenate([_fm(g("lru_b_i")[o, dd], 8) for dd in range(2)], axis=1))
        put("lru_lam%d" % o, np.concatenate([_fm(g("lru_lam")[o, dd], 8) for dd in range(2)], axis=1))
    sh["vecs"] = vecs
    sh["cst"] = _consts()
    sh["pos"] = _grid_sincos(2048)
    c_ = np.arange(128)[:, None]
    e_ = np.arange(128)[None, :]
    lows = []
    for m in range(7):
        sz = 1 << m
        lows.append(((c_ // (2 * sz) == e_ // (2 * sz)) & (c_ % (2 * sz) >= sz) & (e_ % (2 * sz) < sz)).astype(np.float32))
    ups = [mk.T for mk in lows]
    sh["bmk"] = np.ascontiguousarray(np.concatenate(lows + ups, axis=1))
    sh["w_ada"] = np.stack([_slabify(g("w_ada")[l]) for l in range(4)])
    sh["win_e"] = np.stack([_slabify(g("w_in_even")[e]) for e in range(2)])
    sh["wout_e"] = np.stack([_slabify(g("w_out_even")[e]) for e in range(2)])
    sh["win_o"] = np.stack([_slabify(g("w_in_odd")[o]) for o in range(2)])
    sh["wout_o"] = np.stack([_slabify(g("w_out_odd")[o]) for o in range(2)])
    sh["wm1"] = np.stack([_slabify(g("w_mlp_in")[l]) for l in range(4)])
    w2 = g("w_mlp_out").reshape(4, 4, 8, 128, 8, 128).transpose(0, 4, 1, 3, 2, 5)
    sh["wm2"] = np.ascontiguousarray(w2.reshape(4, 8, 4, 128, 1024))
    wg = np.stack([g("lru_w_r"), g("lru_w_i")])
    wg = wg.reshape(2, 2, 2, 4, 2, 128, 256).transpose(1, 2, 3, 5, 0, 4, 6)
    sh["wlru"] = np.ascontiguousarray(wg.reshape(2, 2, 4, 128, 1024))
    s5b = np.zeros((2, 128, 4, 4, 2, 128), np.float32)
    s5c = np.zeros((2, 128, 16, 2, 128), np.float32)
    for ri, (kb, kc) in enumerate((("s5_b_re", "s5_c_re"), ("s5_b_im", "s5_c_im"))):
        B = g(kb)
        C = g(kc)
        for gg in range(32):
            s_, gh = gg // 2, gg % 2
            ut, sl = s_ // 4, s_ % 4
            rows = slice(sl * 32 + gh * 16, sl * 32 + gh * 16 + 16)
            cols = slice(gh * 64, gh * 64 + 64)
            s5b[:, rows, ut, sl, ri, cols] = B[:, gg].transpose(0, 2, 1)
            s5c[:, cols, s_, ri, rows] = C[:, gg].transpose(0, 2, 1)
    sh["s5b"] = s5b
    sh["s5c"] = s5c
    return sh


def prepare_core(inp, c, sh):
    g = lambda k: np.asarray(inp[k], np.float32)
    m = dict(sh)
    m["xs"] = np.ascontiguousarray(g("x_sample")[c])
    m["xp"] = np.ascontiguousarray(g("x_prompt")[4 * c:4 * c + 4].reshape(1024, D))
    cv = np.stack([_fm(g("c_ctx"), 8), _fm(g("c")[c], 8)], axis=-1)
    m["cvec"] = np.ascontiguousarray(cv)
    h0 = np.stack([g("state_s5_re")[c], g("state_s5_im")[c]], axis=2)
    m["s5h0"] = _s5_state_layout(h0)
    m["dl0"] = np.ascontiguousarray(g("state_delta")[c].transpose(3, 0, 1, 2, 4))
    m["lru0"] = np.ascontiguousarray(g("state_lru")[c].reshape(2, 2, 8, 128).transpose(3, 0, 1, 2))
    return m


_NC_CACHE = {}


def kernel(**inputs):
    if "nc" not in _NC_CACHE:
        _NC_CACHE["nc"] = Builder().nc
    nc = _NC_CACHE["nc"]
    sh = prepare_shared(inputs)
    in_maps = [prepare_core(inputs, c, sh) for c in range(8)]
    res = run_bass_kernel_spmd(nc, in_maps, core_ids=list(range(8))).results
    y_prompt = np.concatenate([r["yp"].reshape(4, 256, D) for r in res], axis=0)
    y_sample = np.stack([r["ys"] for r in res], axis=0)
    s5 = np.concatenate([r["o_s5"] for r in res], axis=0)
    s5 = s5.reshape(32, 2, 2, 2, 2, 64, 16).transpose(0, 1, 2, 3, 6, 4, 5).reshape(32, 2, 2, 2, 32, 64)
    new_re = np.ascontiguousarray(s5[:, :, :, 0])
    new_im = np.ascontiguousarray(s5[:, :, :, 1])
    new_delta = np.concatenate([r["o_dl"] for r in res], axis=0)
    lru = np.concatenate([r["o_lru"] for r in res], axis=0)
    new_lru = np.ascontiguousarray(lru.transpose(0, 1, 2, 4, 3).reshape(32, 2, 2, 1024))
    return (y_prompt.astype(np.float32), y_sample.astype(np.float32), new_re.astype(np.float32),
            new_im.astype(np.float32), new_delta.astype(np.float32), new_lru.astype(np.float32))
```
